# Optimizing a Trainium2 kernel written in Bass

```python
import math
import jax
import jax.numpy as jnp
from jax import lax
import numpy as np

D_MODEL = 1024
BATCH = 32
SEQ = 256
DEPTH = 2
DEC_BATCH = 2
DEC_SEQ = 4096
PAST_LEN = 256

GRID_W = 64
ROPE_BASE = 10000.0
EPS = 1e-6
NEG_INF = -1e30
Q_BLOCK = 128
N_BRANCH = 4
BRANCH_W = 256

A_HEADS = 4
A_KV_HEADS = 2
A_GROUP = A_HEADS // A_KV_HEADS
A_HD = 64
A_WINDOW = 128
A_SCALE = A_HD ** -0.5
B_HEADS = 4
B_Q_RANK = 192
B_KV_RANK = 128
B_NOPE = 64
B_ROPE = 32
B_VD = 64
B_SCALE = (B_NOPE + B_ROPE) ** -0.5
C_GROUPS = 4
C_GW = 64
C_WINDOWS = (2, 4, 8, 16)
D_HEADS = 4
D_DK = 64
D_DV = 64
D_CONV = 5
D_CHUNK = 64
FF_RAW = -(-8 * D_MODEL // 3)
D_FF = -(-FF_RAW // 256) * 256

IN_SIZES = (A_HEADS * A_HD, A_KV_HEADS * A_HD, A_KV_HEADS * A_HD,
            B_Q_RANK, B_KV_RANK, B_ROPE,
            C_GROUPS * C_GW,
            D_HEADS * (2 * D_DK + D_DV), D_HEADS * D_DV, 2 * D_HEADS, 2 * D_HEADS,
            N_BRANCH * D_MODEL)
P_IN = sum(IN_SIZES)

kernel_name = 'hybrid_prefix_diffusion_step'


def _split_points(sizes):
    pts, acc = [], 0
    for s in sizes[:-1]:
        acc += s
        pts.append(acc)
    return pts


def rmsnorm(x, g):
    xf = x.astype(jnp.float32)
    y = xf * lax.rsqrt(jnp.mean(xf * xf, axis=-1, keepdims=True) + EPS)
    return (y * g.astype(jnp.float32)).astype(x.dtype)


def l2norm(x):
    return x * lax.rsqrt(jnp.sum(x * x, axis=-1, keepdims=True) + EPS)


def axial_rope_tables(n_tokens, dim):
    n_rows = n_tokens // GRID_W
    row = jnp.repeat(jnp.arange(n_rows), GRID_W).astype(jnp.float32)
    col = jnp.tile(jnp.arange(GRID_W), n_rows).astype(jnp.float32)
    nfreq = dim // 4
    inv = ROPE_BASE ** (-jnp.arange(nfreq, dtype=jnp.float32) / nfreq)
    ang_r = row[:, None] * inv
    ang_c = col[:, None] * inv
    ang = jnp.concatenate([ang_r, ang_r, ang_c, ang_c], axis=-1)
    return jnp.cos(ang), jnp.sin(ang)


def apply_rope(x, cos, sin):
    xf = x.astype(jnp.float32)
    x1, x2, x3, x4 = jnp.split(xf, 4, axis=-1)
    rot = jnp.concatenate([-x2, x1, -x4, x3], axis=-1)
    return (xf * cos[:, None, :] + rot * sin[:, None, :]).astype(x.dtype)


def joint_softmax(scores, sink=None):
    sizes = [s.shape[-1] for s in scores]
    s = jnp.concatenate(scores, axis=-1)
    if sink is not None:
        s = jnp.concatenate([s, jnp.broadcast_to(sink.astype(jnp.float32), s.shape[:-1] + (1,))], axis=-1)
    p = jax.nn.softmax(s, axis=-1)
    return jnp.split(p[..., :sum(sizes)], _split_points(sizes), axis=-1)


def dense_attention(q, k, v, scale, sink=None, q_ctx=None, k_ctx=None, v_ctx=None):
    B, T, KVH, G, d = q.shape
    nb = T // Q_BLOCK
    with_ctx = q_ctx is not None

    def blocks(a):
        return jnp.moveaxis(a.reshape((B, nb, Q_BLOCK) + a.shape[2:]), 1, 0)

    xs = (blocks(q),) + ((blocks(q_ctx),) if with_ctx else ())

    def one_block(qs):
        scores = [jnp.einsum('bqhgd,bshd->bhgqs', qs[0], k, preferred_element_type=jnp.float32) * scale]
        vals = [v]
        if with_ctx:
            scores.append(jnp.einsum('bqhgd,bshd->bhgqs', qs[1], k_ctx, preferred_element_type=jnp.float32) * scale)
            vals.append(v_ctx)
        probs = joint_softmax(scores, sink)
        out = jnp.einsum('bhgqs,bshe->bqhge', probs[0].astype(vals[0].dtype), vals[0])
        for p, vv in zip(probs[1:], vals[1:]):
            out = out + jnp.einsum('bhgqs,bshe->bqhge', p.astype(vv.dtype), vv)
        return out

    out = lax.map(one_block, xs)
    return jnp.moveaxis(out, 0, 1).reshape(B, T, -1)


def window_attention(q, k, v, q_ctx, k_ctx, v_ctx, sink):
    B, T, KVH, G, d = q.shape
    nb = T // Q_BLOCK
    pad = [(0, 0), (Q_BLOCK, Q_BLOCK), (0, 0), (0, 0)]

    def band(a):
        ap = jnp.pad(a, pad).reshape(B, nb + 2, Q_BLOCK, KVH, a.shape[-1])
        return jnp.concatenate([ap[:, :-2], ap[:, 1:-1], ap[:, 2:]], axis=2)

    kb, vb = band(k), band(v)
    qb = q.reshape(B, nb, Q_BLOCK, KVH, G, d)
    qcb = q_ctx.reshape(B, nb, Q_BLOCK, KVH, G, d)
    q_pos = jnp.arange(T).reshape(nb, Q_BLOCK)
    k_pos = (jnp.arange(nb)[:, None] - 1) * Q_BLOCK + jnp.arange(3 * Q_BLOCK)[None, :]
    valid = ((jnp.abs(q_pos[:, :, None] - k_pos[:, None, :]) <= A_WINDOW)
             & (k_pos >= 0)[:, None, :] & (k_pos < T)[:, None, :])
    s_loc = jnp.einsum('bnqhgd,bnshd->bnhgqs', qb, kb, preferred_element_type=jnp.float32) * A_SCALE
    s_loc = jnp.where(valid[None, :, None, None], s_loc, NEG_INF)
    s_ctx = jnp.einsum('bnqhgd,bchd->bnhgqc', qcb, k_ctx, preferred_element_type=jnp.float32) * A_SCALE
    p_loc, p_ctx = joint_softmax([s_loc, s_ctx], sink)
    out = (jnp.einsum('bnhgqs,bnshe->bnqhge', p_loc.astype(vb.dtype), vb)
           + jnp.einsum('bnhgqc,bche->bnqhge', p_ctx.astype(v_ctx.dtype), v_ctx))
    return out.reshape(B, T, -1)


def mla_keys_values(ckv, k_pe, w_ukv):
    B, S, _ = ckv.shape
    kv = jnp.matmul(ckv, w_ukv).reshape(B, S, B_HEADS, B_NOPE + B_VD)
    k_pe_h = jnp.broadcast_to(k_pe[:, :, None, :], (B, S, B_HEADS, B_ROPE)).astype(kv.dtype)
    return jnp.concatenate([kv[..., :B_NOPE], k_pe_h], axis=-1), kv[..., B_NOPE:]


def pool_mixer(x, w_pool, scale):
    B, T, _ = x.shape
    xf = x.astype(jnp.float32).reshape(B, T, C_GROUPS, C_GW)
    cs = jnp.concatenate([jnp.zeros((B, 1, C_GROUPS, C_GW), jnp.float32), jnp.cumsum(xf, axis=1)], axis=1)
    t = jnp.arange(T)
    outs = []
    for gi, w in enumerate(C_WINDOWS):
        lo = jnp.clip(t - w // 2, 0, T)
        hi = jnp.clip(t + w - w // 2, 0, T)
        win_sum = jnp.take(cs[:, :, gi], hi, axis=1) - jnp.take(cs[:, :, gi], lo, axis=1)
        outs.append(win_sum / (hi - lo).astype(jnp.float32)[None, :, None] - xf[:, :, gi])
    y = jnp.stack(outs, axis=2)
    y = jnp.einsum('btgc,gcd->btgd', y, w_pool.astype(jnp.float32))
    return (y.reshape(B, T, -1) * scale.astype(jnp.float32)).astype(x.dtype)


def short_conv(x, w):
    C = x.shape[-1]
    pad = D_CONV // 2
    return lax.conv_general_dilated(x, w[:, None, :].astype(x.dtype), window_strides=(1,),
                                    padding=[(pad, pad)], dimension_numbers=('NWC', 'WIO', 'NWC'),
                                    feature_group_count=C)


def gated_delta_chunked(q, k, v, g, beta, s0):
    B, H, T, DK = q.shape
    DV = v.shape[-1]
    C = D_CHUNK
    N = T // C

    def chunks(a):
        return a.reshape((B, H, N, C) + a.shape[3:])

    q, k, v, g, beta = chunks(q), chunks(k), chunks(v), chunks(g), chunks(beta)
    g = jnp.cumsum(g, axis=-1)
    incl = jnp.tril(jnp.ones((C, C), bool))
    strict = jnp.tril(jnp.ones((C, C), bool), -1)
    decay = jnp.exp(jnp.where(incl, g[..., :, None] - g[..., None, :], -jnp.inf))
    k_beta = k * beta[..., None]
    lmat = jnp.where(strict, jnp.einsum('bhncd,bhnsd->bhncs', k_beta, k) * decay, 0.0)
    eye = jnp.eye(C, dtype=jnp.float32)
    rhs = jnp.concatenate([v * beta[..., None], k_beta * jnp.exp(g)[..., None]], axis=-1)
    sol = lax.linalg.triangular_solve(lmat + eye, rhs, left_side=True, lower=True, unit_diagonal=True)
    u, w = sol[..., :DV], sol[..., DV:]
    a_intra = jnp.einsum('bhncd,bhnsd->bhncs', q, k) * decay

    def step(S, inp):
        q_i, k_i, u_i, w_i, g_i, a_i = inp
        v_new = u_i - jnp.einsum('bhcd,bhde->bhce', w_i, S)
        o_i = (jnp.einsum('bhcd,bhde->bhce', q_i * jnp.exp(g_i)[..., None], S)
               + jnp.einsum('bhcs,bhse->bhce', a_i, v_new))
        g_last = g_i[..., -1:]
        S = (S * jnp.exp(g_last)[..., None]
             + jnp.einsum('bhcd,bhce->bhde', k_i * jnp.exp(g_last - g_i)[..., None], v_new))
        return S, o_i

    xs = tuple(jnp.moveaxis(a, 2, 0) for a in (q, k, u, w, g, a_intra))
    s_final, o = lax.scan(step, s0.astype(jnp.float32), xs)
    return jnp.moveaxis(o, 0, 2).reshape(B, H, T, DV), s_final


def gdn_mixer(qkv, z, beta_raw, alpha_raw, conv_w, a_log, dt_bias, g_norm, s0_f, s0_b):
    B, T, _ = qkv.shape
    u = jax.nn.silu(short_conv(qkv, conv_w)).astype(jnp.float32)
    q, k, v = jnp.split(u, [D_HEADS * D_DK, 2 * D_HEADS * D_DK], axis=-1)

    def heads(a):
        return a.reshape(B, T, D_HEADS, -1).transpose(0, 2, 1, 3)

    q = l2norm(heads(q)) * (D_DK ** -0.5)
    k = l2norm(heads(k))
    v = heads(v)
    beta = jax.nn.sigmoid(beta_raw.astype(jnp.float32)).reshape(B, T, 2, D_HEADS).transpose(2, 0, 3, 1)
    alpha = alpha_raw.astype(jnp.float32).reshape(B, T, 2, D_HEADS).transpose(2, 0, 3, 1)
    g = -jnp.exp(a_log.astype(jnp.float32))[:, None, :, None] * jax.nn.softplus(
        alpha + dt_bias.astype(jnp.float32)[:, None, :, None])
    o_f, s_f = gated_delta_chunked(q, k, v, g[0], beta[0], s0_f)

    def rev(a):
        return jnp.flip(a, axis=2)

    o_b, s_b = gated_delta_chunked(rev(q), rev(k), rev(v), rev(g[1]), rev(beta[1]), s0_b)
    o = (o_f + rev(o_b)).transpose(0, 2, 1, 3)
    o = rmsnorm(o, g_norm) * jax.nn.silu(z.astype(jnp.float32).reshape(B, T, D_HEADS, D_DV))
    return o.reshape(B, T, -1).astype(qkv.dtype), s_f, s_b


def trunk_layer(x, cond, lp, ctx):
    B, T, _ = x.shape
    latent = ctx is not None
    mod = jnp.matmul(jax.nn.silu(cond), lp['w_mod']) + lp['b_mod']
    sh1, sc1, ga1, sh2, sc2, ga2 = jnp.split(mod[:, None, :], 6, axis=-1)

    h = rmsnorm(x, lp['g_pre1']) * (1 + sc1) + sh1
    proj = jnp.matmul(h, lp['w_in'])
    (a_q, a_k, a_v, b_cq, b_ckv, b_kr, c_in, d_qkv, d_z, d_beta, d_alpha,
     gate_raw) = jnp.split(proj, _split_points(IN_SIZES), axis=-1)

    qa = a_q.reshape(B, T, A_HEADS, A_HD)
    ka = a_k.reshape(B, T, A_KV_HEADS, A_HD)
    va = a_v.reshape(B, T, A_KV_HEADS, A_HD)

    def grp(t):
        return t.reshape(B, T, A_KV_HEADS, A_GROUP, A_HD)

    sink = lp['a_sink'].reshape(A_KV_HEADS, A_GROUP, 1, 1)

    cq = rmsnorm(b_cq, lp['b_g_cq'])
    qb = jnp.matmul(cq, lp['b_w_uq']).reshape(B, T, B_HEADS, B_NOPE + B_ROPE)
    ckv = rmsnorm(b_ckv, lp['b_g_ckv'])

    if latent:
        ctx_a_k, ctx_a_v, ctx_ckv, ctx_kr, s0_f, s0_b = ctx
        cos_a, sin_a = axial_rope_tables(T, A_HD)
        cos_b, sin_b = axial_rope_tables(T, B_ROPE)
        out_a = window_attention(grp(apply_rope(qa, cos_a, sin_a)), apply_rope(ka, cos_a, sin_a), va,
                                 grp(qa), ctx_a_k, ctx_a_v, sink)
        q_nope, q_pe = qb[..., :B_NOPE], qb[..., B_NOPE:]
        kr_rot = apply_rope(b_kr[:, :, None, :], cos_b, sin_b)[:, :, 0]
        k_lat, v_lat = mla_keys_values(ckv, kr_rot, lp['b_w_ukv'])
        k_ctx, v_ctx = mla_keys_values(ctx_ckv, ctx_kr, lp['b_w_ukv'])
        q_lat = jnp.concatenate([q_nope, apply_rope(q_pe, cos_b, sin_b)], axis=-1)[:, :, :, None]
        out_b = dense_attention(q_lat, k_lat, v_lat, B_SCALE,
                                q_ctx=qb[:, :, :, None], k_ctx=k_ctx, v_ctx=v_ctx)
    else:
        s0_f = jnp.zeros((B, D_HEADS, D_DK, D_DV), jnp.float32)
        s0_b = s0_f
        out_a = dense_attention(grp(qa), ka, va, A_SCALE, sink=sink)
        k_c, v_c = mla_keys_values(ckv, b_kr, lp['b_w_ukv'])
        out_b = dense_attention(qb[:, :, :, None], k_c, v_c, B_SCALE)

    out_c = pool_mixer(c_in, lp['c_w_pool'], lp['c_scale'])
    out_d, s_f, s_b = gdn_mixer(d_qkv, d_z, d_beta, d_alpha, lp['d_conv'], lp['d_a_log'],
                                lp['d_dt_bias'], lp['d_g_norm'], s0_f, s0_b)

    branches = jnp.stack([out_a.astype(x.dtype), out_b.astype(x.dtype), out_c, out_d], axis=2)
    br = jnp.einsum('btmc,mcd->btmd', branches, lp['w_br'])
    gates = jax.nn.sigmoid(gate_raw.reshape(B, T, N_BRANCH, D_MODEL))
    mix = jnp.matmul(jnp.sum(gates * br, axis=2), lp['w_o'])
    x = x + ga1 * rmsnorm(mix, lp['g_post1'])

    h2 = rmsnorm(x, lp['g_pre2']) * (1 + sc2) + sh2
    gt, up = jnp.split(jnp.matmul(h2, lp['w_up']), 2, axis=-1)
    f = jnp.matmul(jax.nn.silu(gt) * up, lp['w_down'])
    x = x + ga2 * rmsnorm(f, lp['g_post2'])

    new_ctx = None if latent else (ka, va, ckv, b_kr, s_f, s_b)
    return x, new_ctx


def setup_inputs(seed: int = 0) -> dict:
    key = jax.random.key(seed)
    ks = jax.random.split(key, 32)
    D = D_MODEL

    def nrm(i, shape, scale):
        return jax.random.normal(ks[i], shape, jnp.float32) * scale

    a_log = jnp.log(jax.random.uniform(ks[25], (DEPTH, 2, D_HEADS), jnp.float32, 1.0, 16.0))
    dt = jnp.exp(jax.random.uniform(ks[26], (DEPTH, 2, D_HEADS), jnp.float32,
                                    math.log(1e-3), math.log(1e-1)))
    dt_bias = dt + jnp.log(-jnp.expm1(-dt))
    return {
        'x_prompt': nrm(0, (BATCH, SEQ, D), 1.0),
        'x_sample': nrm(1, (DEC_BATCH, DEC_SEQ, D), 1.0),
        'cache_a_k': nrm(2, (DEC_BATCH, DEPTH, PAST_LEN, A_KV_HEADS, A_HD), 1.0),
        'cache_a_v': nrm(3, (DEC_BATCH, DEPTH, PAST_LEN, A_KV_HEADS, A_HD), 1.0),
        'cache_b_ckv': nrm(4, (DEC_BATCH, DEPTH, PAST_LEN, B_KV_RANK), 1.0),
        'cache_b_krope': nrm(5, (DEC_BATCH, DEPTH, PAST_LEN, B_ROPE), 1.0),
        'state_d_fwd': nrm(6, (DEC_BATCH, DEPTH, D_HEADS, D_DK, D_DV), 0.1),
        'state_d_bwd': nrm(7, (DEC_BATCH, DEPTH, D_HEADS, D_DK, D_DV), 0.1),
        'c': nrm(8, (DEC_BATCH, D), 1.0),
        'c_ctx': nrm(9, (D,), 1.0),
        'w_mod': nrm(10, (DEPTH, D, 6 * D), 0.5 * D ** -0.5),
        'b_mod': nrm(11, (DEPTH, 6 * D), 0.01),
        'g_pre1': 1.0 + nrm(12, (DEPTH, D), 0.1),
        'g_post1': 1.0 + nrm(13, (DEPTH, D), 0.1),
        'g_pre2': 1.0 + nrm(14, (DEPTH, D), 0.1),
        'g_post2': 1.0 + nrm(15, (DEPTH, D), 0.1),
        'w_in': nrm(16, (DEPTH, D, P_IN), D ** -0.5),
        'a_sink': nrm(17, (DEPTH, A_HEADS), 0.5),
        'b_g_cq': 1.0 + nrm(18, (DEPTH, B_Q_RANK), 0.1),
        'b_g_ckv': 1.0 + nrm(19, (DEPTH, B_KV_RANK), 0.1),
        'b_w_uq': nrm(20, (DEPTH, B_Q_RANK, B_HEADS * (B_NOPE + B_ROPE)), B_Q_RANK ** -0.5),
        'b_w_ukv': nrm(21, (DEPTH, B_KV_RANK, B_HEADS * (B_NOPE + B_VD)), B_KV_RANK ** -0.5),
        'c_w_pool': nrm(22, (DEPTH, C_GROUPS, C_GW, C_GW), C_GW ** -0.5),
        'c_scale': 1.0 + nrm(23, (DEPTH, C_GROUPS * C_GW), 0.1),
        'd_conv': nrm(24, (DEPTH, D_CONV, D_HEADS * (2 * D_DK + D_DV)), D_CONV ** -0.5),
        'd_a_log': a_log,
        'd_dt_bias': dt_bias,
        'd_g_norm': 1.0 + nrm(27, (DEPTH, D_DV), 0.1),
        'w_br': nrm(28, (DEPTH, N_BRANCH, BRANCH_W, D), BRANCH_W ** -0.5),
        'w_o': nrm(29, (DEPTH, D, D), D ** -0.5),
        'w_up': nrm(30, (DEPTH, D, 2 * D_FF), D ** -0.5),
        'w_down': nrm(31, (DEPTH, D_FF, D), D_FF ** -0.5),
    }


def reference(x_prompt, x_sample, cache_a_k, cache_a_v, cache_b_ckv, cache_b_krope,
              state_d_fwd, state_d_bwd, c, c_ctx, w_mod, b_mod, g_pre1, g_post1, g_pre2, g_post2,
              w_in, a_sink, b_g_cq, b_g_ckv, b_w_uq, b_w_ukv, c_w_pool, c_scale, d_conv,
              d_a_log, d_dt_bias, d_g_norm, w_br, w_o, w_up, w_down):
    y_prompt = x_prompt
    y_sample = x_sample
    cond_ctx = c_ctx[None, :]
    ak_l, av_l, ckv_l, kr_l, sf_l, sb_l = [], [], [], [], [], []
    for l in range(DEPTH):
        lp = {
            'w_mod': w_mod[l], 'b_mod': b_mod[l], 'g_pre1': g_pre1[l], 'g_post1': g_post1[l],
            'g_pre2': g_pre2[l], 'g_post2': g_post2[l], 'w_in': w_in[l], 'a_sink': a_sink[l],
            'b_g_cq': b_g_cq[l], 'b_g_ckv': b_g_ckv[l], 'b_w_uq': b_w_uq[l], 'b_w_ukv': b_w_ukv[l],
            'c_w_pool': c_w_pool[l], 'c_scale': c_scale[l], 'd_conv': d_conv[l],
            'd_a_log': d_a_log[l], 'd_dt_bias': d_dt_bias[l], 'd_g_norm': d_g_norm[l],
            'w_br': w_br[l], 'w_o': w_o[l], 'w_up': w_up[l], 'w_down': w_down[l],
        }
        y_prompt, (ak, av, ckv, kr, sf, sb) = trunk_layer(y_prompt, cond_ctx, lp, None)
        ak_l.append(ak)
        av_l.append(av)
        ckv_l.append(ckv)
        kr_l.append(kr)
        sf_l.append(sf)
        sb_l.append(sb)
        ctx = (cache_a_k[:, l], cache_a_v[:, l], cache_b_ckv[:, l], cache_b_krope[:, l],
               state_d_fwd[:, l], state_d_bwd[:, l])
        y_sample, _ = trunk_layer(y_sample, c, lp, ctx)
    new_cache_a_k = jnp.stack(ak_l, axis=1)
    new_cache_a_v = jnp.stack(av_l, axis=1)
    new_cache_b_ckv = jnp.stack(ckv_l, axis=1)
    new_cache_b_krope = jnp.stack(kr_l, axis=1)
    new_state_d_fwd = jnp.stack(sf_l, axis=1)
    new_state_d_bwd = jnp.stack(sb_l, axis=1)
    return (y_prompt, y_sample, new_cache_a_k, new_cache_a_v, new_cache_b_ckv, new_cache_b_krope, new_state_d_fwd, new_state_d_bwd)
```

```python
import os
import numpy as np
import concourse.bass as bass
import concourse.mybir as mybir
from concourse.bass_utils import run_bass_kernel_spmd
from contextlib import ExitStack


F32 = mybir.dt.float32
BF16 = mybir.dt.bfloat16
I32 = mybir.dt.int32
AF = mybir.ActivationFunctionType
ALU = mybir.AluOpType
AX = mybir.AxisListType

ENGS = ("pe", "act", "dve", "pool", "sp")
SEM_ROT = 4000


class Prog:
    def __init__(self, nc, n_dma_sems=12):
        self.nc = nc
        self.es = ExitStack()
        self.ops = {e: [] for e in ENGS}
        self.cnt = {e: 0 for e in ENGS}
        self.cur_sem = {}
        self.seen = {e: {} for e in ENGS}
        self.res_w = {}
        self.res_r = {}
        self.sem_count = 0
        for e in ENGS:
            self.cur_sem[e] = self._new_sem()
        self.dma_pool = {}
        for e in ("sp", "pool", "act"):
            self.dma_pool[e] = [[self._new_sem(), 0] for _ in range(n_dma_sems)]
        self.dma_rr = {e: 0 for e in ("sp", "pool", "act")}
        self.out_events = []
        self._uid = 0

    def _new_sem(self):
        self.sem_count += 1
        return self.es.enter_context(self.nc.semaphore(f"s{self.sem_count}"))

    def sbuf(self, name, shape, dt):
        return self.es.enter_context(self.nc.sbuf_tensor(name, list(shape), dt))

    def psum(self, name, shape, dt=F32):
        return self.es.enter_context(self.nc.psum_tensor(name, list(shape), dt))

    def _need(self, eng, ev):
        if ev is None:
            return
        sem, val, _ = ev
        k = id(sem)
        if self.seen[eng].get(k, 0) >= val:
            return
        self.seen[eng][k] = val
        self.ops[eng].append(("wait", sem, val))

    def _deps(self, eng, reads, writes, pe_accum=False):
        for r in reads:
            self._need(eng, self.res_w.get(r))
        for w in writes:
            lw = self.res_w.get(w)
            if not (pe_accum and lw is not None and lw[2] == "pe" and eng == "pe"):
                self._need(eng, lw)
            for ev in self.res_r.get(w, ()):
                self._need(eng, ev)

    def _commit(self, ev, reads, writes):
        for r in reads:
            self.res_r.setdefault(r, []).append(ev)
            if len(self.res_r[r]) > 24:
                best = {}
                for s, v, e in self.res_r[r]:
                    if id(s) not in best or best[id(s)][1] < v:
                        best[id(s)] = (s, v, e)
                self.res_r[r] = list(best.values())
        for w in writes:
            self.res_w[w] = ev
            self.res_r[w] = []

    def op(self, eng, fn, reads=(), writes=(), pe_accum=False):
        self._deps(eng, reads, writes, pe_accum)
        if self.cnt[eng] >= SEM_ROT:
            self.cur_sem[eng] = self._new_sem()
            self.cnt[eng] = 0
        self.cnt[eng] += 1
        ev = (self.cur_sem[eng], self.cnt[eng], eng)
        self.ops[eng].append(("op", fn, ev[0], 1))
        self._commit(ev, reads, writes)
        return ev

    def dma(self, q, out, in_, reads=(), writes=(), is_output=False, **kw):
        pool = self.dma_pool[q]
        i = self.dma_rr[q]
        self.dma_rr[q] = (i + 1) % len(pool)
        slot = pool[i]
        sem = slot[0]
        if slot[1] > 0:
            self._need(q, (sem, slot[1], "dma"))
        self._deps(q, reads, writes)
        slot[1] += 16
        ev = (sem, slot[1], "dma")

        def fn(e, out=out, in_=in_, kw=kw):
            return e.dma_start(out=out, in_=in_, **kw)
        self.ops[q].append(("op", fn, sem, 16))
        self._commit(ev, reads, writes)
        if is_output:
            self.out_events.append(ev)
        return ev

    def barrier(self):
        evs = []
        for e in ENGS:
            if self.cnt[e] > 0:
                evs.append((self.cur_sem[e], self.cnt[e], e))
        for q in self.dma_pool:
            for sem, val in self.dma_pool[q]:
                if val > 0:
                    evs.append((sem, val, "dma"))
        for e in ENGS:
            for ev in evs:
                if ev[2] == e and ev[0] is self.cur_sem[e]:
                    continue
                self._need(e, ev)
        self.res_w.clear()
        self.res_r.clear()

    def finish(self):
        for ev in self.out_events:
            self._need("sp", ev)
        self.barrier()
        nc = self.nc
        emap = {"pe": "tensor", "act": "scalar", "dve": "vector", "pool": "gpsimd", "sp": "sync"}
        with nc.Block() as block:
            for e in ENGS:
                lst = self.ops[e]

                def body(h, lst=lst):
                    for item in lst:
                        if item[0] == "wait":
                            h.wait_ge(item[1], item[2])
                        else:
                            ins = item[1](h)
                            ins.then_inc(item[2], item[3])
                getattr(block, emap[e])(body)
        self.es.close()


D = 1024
NK = 8
TB = 512
EPS = 1e-6
D_FF = 2816
NKF = 22
CTX = 256

O_AQ, O_AK, O_AV, O_CQ, O_CKV, O_KR, O_C, O_DQKV, O_DZ, O_DB, O_DA, O_G = 0, 256, 384, 512, 704, 832, 864, 1120, 1888, 2144, 2152, 2160


def rope_perm(dim):
    q = dim // 4
    idx = np.arange(dim)
    perm = np.where((idx // q) % 2 == 0, idx + q, idx - q)
    sign = np.where((idx // q) % 2 == 0, -1.0, 1.0).astype(np.float32)
    return perm, sign


def rope_tables(T, dim):
    GRID_W = 64
    n_rows = T // GRID_W
    row = np.repeat(np.arange(n_rows), GRID_W).astype(np.float32)
    col = np.tile(np.arange(GRID_W), n_rows).astype(np.float32)
    nfreq = dim // 4
    inv = (np.float32(10000.0) ** (-np.arange(nfreq, dtype=np.float32) / np.float32(nfreq))).astype(np.float32)
    ang_r = row[:, None] * inv
    ang_c = col[:, None] * inv
    ang = np.concatenate([ang_r, ang_r, ang_c, ang_c], axis=-1).astype(np.float32)
    return np.cos(ang).astype(np.float32), np.sin(ang).astype(np.float32)


class Cfg:
    def __init__(self, T_S=4096, NSEQ=4):
        self.T_S, self.NSEQ = T_S, NSEQ
        self.T_P = NSEQ * 256
        self.NT = T_S + self.T_P
        self.NBS = T_S // TB
        self.NBP = self.T_P // TB
        self.NB = self.NBS + self.NBP


V_BMOD, V_GPRE1, V_GPOST1, V_GPRE2, V_GPOST2, V_GCQ, V_CSC, V_CONV = 0, 48, 56, 64, 72, 80, 82, 84
NV = 84 + 30
R_GCKV, R_GNORM, R_SINK, R_ALOG, R_DTB = 0, 128, 192, 196, 204
NR = 212
C_ID, C_LI, C_LS, C_UI, C_US, C_ONE, C_BLK = 0, 128, 256, 384, 512, 640, 768
NCONST = 896


def fm(v, nk):
    return np.ascontiguousarray(v.reshape(nk, 128).T)


def host_prep(cfg, inp, b_idx, seq_ids):
    f32 = np.float32
    m = {}
    m["xs"] = np.ascontiguousarray(inp["x_sample"][b_idx, :cfg.T_S])
    m["xp"] = np.ascontiguousarray(inp["x_prompt"][seq_ids].reshape(cfg.T_P, D))
    m["cak"] = np.ascontiguousarray(inp["cache_a_k"][b_idx].reshape(2, CTX, 128))
    m["cav"] = np.ascontiguousarray(inp["cache_a_v"][b_idx].reshape(2, CTX, 128))
    m["cckv"] = np.ascontiguousarray(inp["cache_b_ckv"][b_idx])
    m["ckr"] = np.ascontiguousarray(inp["cache_b_krope"][b_idx])
    m["sdf"] = np.ascontiguousarray(inp["state_d_fwd"][b_idx])
    m["sdb"] = np.ascontiguousarray(inp["state_d_bwd"][b_idx])
    c2 = np.stack([inp["c"][b_idx], inp["c_ctx"]], axis=1)
    m["c2T"] = np.ascontiguousarray(c2.reshape(NK, 128, 2).transpose(1, 0, 2))
    return m


def host_prep_shared(cfg, inp):
    f32 = np.float32
    m = {}
    w_in = inp["w_in"]
    pA, sA = rope_perm(64)
    pB, sB = rope_perm(32)
    aq = np.arange(O_AQ, O_AK)
    ak = np.arange(O_AK, O_AV)
    cq = np.arange(O_CQ, O_CKV)
    kr = np.arange(O_KR, O_C)
    cc = np.arange(O_C, O_DQKV)
    dq = np.arange(O_DQKV, O_DZ)
    aqp = (O_AQ + (np.arange(256) // 64) * 64 + pA[np.arange(256) % 64])
    akp = (O_AK + (np.arange(128) // 64) * 64 + pA[np.arange(128) % 64])
    krp = O_KR + pB
    cols = np.concatenate([aq, ak, cq[:128],
                           cq[128:], kr, krp, cc, dq[:128],
                           dq[128:640],
                           dq[640:], aqp, akp])
    assert cols.size == 2048
    m["wfm"] = np.ascontiguousarray(w_in[:, :, cols])
    tok = np.zeros((2, D, 1024), f32)
    tok[:, :, 0:128] = w_in[:, :, O_AK:O_AV]
    tok[:, :, 128:256] = w_in[:, :, O_AV:O_CQ]
    tok[:, :, 256:384] = w_in[:, :, O_CKV:O_KR]
    tok[:, :, 384:416] = w_in[:, :, O_KR:O_C]
    tok[:, :, 512:784] = w_in[:, :, O_DZ:O_G]
    m["wtok"] = tok
    m["wgate"] = np.ascontiguousarray(w_in[:, :, O_G:])
    m["wmod"] = inp["w_mod"]
    m["wbr"] = inp["w_br"]
    m["wo"] = inp["w_o"]
    ucols = []
    for n in range(11):
        for j in (2 * n, 2 * n + 1):
            ucols.append(np.arange(j * 128, (j + 1) * 128))
        for j in (2 * n, 2 * n + 1):
            ucols.append(D_FF + np.arange(j * 128, (j + 1) * 128))
    ucols = np.concatenate(ucols)
    m["wup"] = np.ascontiguousarray(inp["w_up"][:, :, ucols])
    m["wdown"] = inp["w_down"]
    wuq = inp["b_w_uq"]
    pcols = np.arange(384)
    h_, r_ = pcols // 96, pcols % 96
    pcols_p = np.where(r_ >= 64, h_ * 96 + 64 + pB[np.clip(r_ - 64, 0, 31)], pcols)
    wuq2 = np.zeros((2, 256, 768), f32)
    wuq2[:, :192, :384] = wuq
    wuq2[:, :192, 384:] = wuq[:, :, pcols_p]
    m["wuq2"] = wuq2
    wukv = inp["b_w_ukv"]
    ncols = np.concatenate([h * 128 + np.arange(64) for h in range(4)])
    vcols = np.concatenate([h * 128 + 64 + np.arange(64) for h in range(4)])
    m["wukv2"] = np.ascontiguousarray(np.concatenate([wukv[:, :, ncols], wukv[:, :, vcols]], axis=2))
    vecs = np.zeros((2, 128, NV), f32)
    rowsb = np.zeros((2, 128, NR), f32)
    for l in range(2):
        vecs[l, :, V_BMOD:V_BMOD + 48] = fm(inp["b_mod"][l], 48)
        vecs[l, :, V_GPRE1:V_GPRE1 + 8] = fm(inp["g_pre1"][l], 8)
        vecs[l, :, V_GPOST1:V_GPOST1 + 8] = fm(inp["g_post1"][l], 8)
        vecs[l, :, V_GPRE2:V_GPRE2 + 8] = fm(inp["g_pre2"][l], 8)
        vecs[l, :, V_GPOST2:V_GPOST2 + 8] = fm(inp["g_post2"][l], 8)
        gcq = np.zeros(256, f32)
        gcq[:192] = inp["b_g_cq"][l]
        vecs[l, :, V_GCQ:V_GCQ + 2] = fm(gcq, 2)
        vecs[l, :, V_CSC:V_CSC + 2] = fm(inp["c_scale"][l], 2)
        for tap in range(5):
            vecs[l, :, V_CONV + tap * 6:V_CONV + tap * 6 + 6] = fm(inp["d_conv"][l, tap], 6)
        rowsb[l, :, R_GCKV:R_GCKV + 128] = inp["b_g_ckv"][l][None, :]
        rowsb[l, :, R_GNORM:R_GNORM + 64] = inp["d_g_norm"][l][None, :]
        rowsb[l, :, R_SINK:R_SINK + 4] = inp["a_sink"][l][None, :]
        rowsb[l, :, R_ALOG:R_ALOG + 8] = inp["d_a_log"][l].reshape(8)[None, :]
        rowsb[l, :, R_DTB:R_DTB + 8] = inp["d_dt_bias"][l].reshape(8)[None, :]
    m["vecs"] = vecs
    m["rowsb"] = rowsb
    m["wpool"] = inp["c_w_pool"]
    cA, sA_ = rope_tables(cfg.T_S, 64)
    cB, sB_ = rope_tables(cfg.T_S, 32)
    ropeA = np.zeros((128, 2, cfg.T_S), f32)
    for p in range(128):
        ropeA[p, 0] = cA[:, p % 64]
        ropeA[p, 1] = sA_[:, p % 64] * sA[p % 64]
    ropeB = np.zeros((96, 2, cfg.T_S), f32)
    for p in range(32):
        for base in (0, 64):
            ropeB[base + p, 0] = cB[:, p]
            ropeB[base + p, 1] = sB_[:, p] * sB[p]
    m["ropeA"] = ropeA
    m["ropeB"] = ropeB

    def invcnt(T):
        t = np.arange(T)
        out = np.zeros((128, 2, T), f32)
        for g, w in enumerate((2, 4, 8, 16)):
            lo = np.clip(t - w // 2, 0, T)
            hi = np.clip(t + w - w // 2, 0, T)
            ic = (1.0 / (hi - lo).astype(f32)).astype(f32)
            out[(g % 2) * 64:(g % 2) * 64 + 64, g // 2, :] = ic[None, :]
        return out
    m["icS"] = invcnt(cfg.T_S)
    m["icP"] = invcnt(256)
    cst = np.zeros((128, NCONST), f32)
    i = np.arange(128)
    cst[:, C_ID:C_ID + 128] = (i[:, None] == i[None, :])
    cst[:, C_LI:C_LI + 128] = (i[:, None] <= i[None, :])
    cst[:, C_LS:C_LS + 128] = (i[:, None] < i[None, :])
    cst[:, C_UI:C_UI + 128] = (i[:, None] >= i[None, :])
    cst[:, C_US:C_US + 128] = (i[:, None] > i[None, :])
    cst[:, C_ONE:C_ONE + 128] = 1.0
    cst[:, C_BLK:C_BLK + 128] = ((i[:, None] // 64) == (i[None, :] // 64))
    m["consts"] = cst
    return m


class Ring:
    def __init__(self, P, name, n, shape, dt, psum=False):
        self.name, self.n, self.i = name, n, 0
        self.tiles = [(P.psum if psum else P.sbuf)(f"{name}{i}", shape, dt) for i in range(n)]

    def next(self):
        t, k = self.tiles[self.i], (self.name, self.i)
        self.i = (self.i + 1) % self.n
        return t, k


class ARing:
    def __init__(self, name, aps):
        self.name, self.tiles, self.n, self.i = name, aps, len(aps), 0

    def next(self):
        t, k = self.tiles[self.i], (self.name, self.i)
        self.i = (self.i + 1) % self.n
        return t, k


class Builder:
    def __init__(self, cfg, debug=(), debug_in=()):
        self.cfg = cfg
        self.debug = set(debug)
        self.debug_in = set(debug_in)
        nc = bass.Bass("TRN2", target_bir_lowering=False)
        self.nc = nc
        self.P = Prog(nc, n_dma_sems=10)
        self.dram = {}
        self.declare_io()
        self.alloc()

    def din(self, name, shape, dt=F32):
        self.dram[name] = self.nc.dram_tensor(name, list(shape), dt, kind="ExternalInput").ap()
        return self.dram[name]

    def dout(self, name, shape, dt=F32):
        self.dram[name] = self.nc.dram_tensor(name, list(shape), dt, kind="ExternalOutput").ap()
        return self.dram[name]

    def dscr(self, name, shape, dt):
        kind = "ExternalOutput" if name in self.debug else ("ExternalInput" if name in self.debug_in else "Internal")
        self.dram[name] = self.nc.dram_tensor(name, list(shape), dt, kind=kind).ap()
        return self.dram[name]

    def declare_io(self):
        c = self.cfg
        self.din("xs", [c.T_S, D]); self.din("xp", [c.T_P, D])
        self.din("cak", [2, CTX, 128]); self.din("cav", [2, CTX, 128])
        self.din("cckv", [2, CTX, 128]); self.din("ckr", [2, CTX, 32])
        self.din("sdf", [2, 4, 64, 64]); self.din("sdb", [2, 4, 64, 64])
        self.din("c2T", [128, NK, 2])
        self.din("wfm", [2, D, 2048]); self.din("wtok", [2, D, 1024]); self.din("wgate", [2, D, 4096])
        self.din("wmod", [2, D, 6144]); self.din("wbr", [2, 4, 256, D]); self.din("wo", [2, D, D])
        self.din("wup", [2, D, 2 * D_FF]); self.din("wdown", [2, D_FF, D])
        self.din("wuq2", [2, 256, 768]); self.din("wukv2", [2, 128, 512])
        self.din("vecs", [2, 128, NV]); self.din("rowsb", [2, 128, NR]); self.din("wpool", [2, 4, 64, 64])
        self.din("ropeA", [128, 2, c.T_S]); self.din("ropeB", [96, 2, c.T_S])
        self.din("icS", [128, 2, c.T_S]); self.din("icP", [128, 2, 256]); self.din("consts", [128, NCONST])
        self.dout("ys", [c.T_S, D]); self.dout("yp", [c.T_P, D])
        self.dout("o_ak", [c.NSEQ, 2, 256, 128]); self.dout("o_av", [c.NSEQ, 2, 256, 128])
        self.dout("o_ckv", [c.NSEQ, 2, 256, 128]); self.dout("o_kr", [c.NSEQ, 2, 256, 32])
        self.dout("o_sf", [c.NSEQ, 2, 4, 64, 64]); self.dout("o_sb", [c.NSEQ, 2, 4, 64, 64])
        for l in range(2):
            self.dscr(f"Wfm{l}", [4, 128, NK, 512], BF16)
            self.dscr(f"Wtok{l}", [2, 128, NK, 512], BF16)
            self.dscr(f"Wgate{l}", [8, 128, NK, 512], BF16)
            self.dscr(f"Wmod{l}", [12, 128, NK, 512], BF16)
            self.dscr(f"Wbr{l}", [4, 128, 2, 1024], BF16)
            self.dscr(f"Wo{l}", [2, 128, NK, 512], BF16)
            self.dscr(f"Wup{l}", [11, 128, NK, 512], BF16)
            self.dscr(f"Wdown{l}", [8, 128, NKF, 128], BF16)
        NT = c.NT
        self.dscr("XT", [D, NT], F32)
        self.dscr("HT", [D, NT], BF16)
        self.dscr("AQT", [256, NT], BF16)
        self.dscr("AQRT", [256, c.T_S], BF16)
        self.dscr("AKT", [128, NT], BF16)
        self.dscr("AKCT", [128, CTX], BF16)
        self.dscr("AV", [NT, 128], BF16)
        self.dscr("QCT", [4, 96, NT], BF16)
        self.dscr("QLT", [4, 96, c.T_S], BF16)
        self.dscr("BKT", [4, 96, NT + CTX], BF16)
        self.dscr("BV", [NT + CTX, 256], BF16)
        self.dscr("CT", [256, NT], F32)
        self.dscr("DTs", [768, NT], F32)
        self.dscr("DZ", [NT, 272], F32)
        self.dscr("BRT", [4, 256, NT], BF16)
        for nm in ("QNT", "KNT"):
            self.dscr(nm, [256, NT], F32)
        for nm in ("KN", "VV", "OF", "OB"):
            self.dscr(nm, [NT, 256], F32)

    def alloc(self):
        P = self.P
        self.cst = P.sbuf("cst", [128, NCONST], F32)
        self.onesb = P.sbuf("onesb", [128, 128], BF16)
        self.identb = P.sbuf("identb", [128, 128], BF16)
        self.vecs = [P.sbuf(f"vecs{l}", [128, NV], F32) for l in range(2)]
        self.rowsb = [P.sbuf(f"rowsb{l}", [128, NR], F32) for l in range(2)]
        self.modv = P.sbuf("modv", [128, 48, 2], F32)
        self.gms = [P.sbuf(f"gm{l}", [128, 6, 8, 2], F32) for l in range(2)]
        self.c2 = P.sbuf("c2", [128, NK, 2], F32)
        self.c2b = P.sbuf("c2b", [128, NK, 2], BF16)
        self.wuq = P.sbuf("wuq", [128, 2, 768], BF16)
        self.wukv = P.sbuf("wukv", [128, 512], BF16)
        self.wring = Ring(P, "wb", 6, [128, 4096], BF16)
        self.ps = Ring(P, "ps", 6, [128, 512], F32, psum=True)
        self.psx = Ring(P, "psx", 2, [128, 512], F32, psum=True)
        self.xT = P.sbuf("xT", [128, NK, TB], F32)
        self.sq = P.sbuf("sq", [128, NK, TB], BF16)
        self.hT = P.sbuf("hT", [128, NK, TB], BF16)
        self.rstd = P.sbuf("rstd", [128, TB], F32)
        self.f32r = Ring(P, "f32r", 5, [128, TB], F32)
        self.bf16r = Ring(P, "bf16r", 6, [128, TB], BF16)
        self.small = Ring(P, "small", 8, [128, 8], F32)
        self.arenaF = P.sbuf("arenaF", [128, 12288], F32)
        self.arenaB = P.sbuf("arenaB", [128, 24576], BF16)
        aF, aB = self.arenaF, self.arenaB
        self.cqs = aF[:, 0:1024].rearrange("p (k t) -> p k t", k=2)
        self.tab = aF[:, 1024:2048].rearrange("p (k t) -> p k t", k=2)
        self.tabB = aF[0:96, 2048:3072].rearrange("p (k t) -> p k t", k=2)
        self.tokf = ARing("tokf", [aF[:, 3072 + i * 512:3072 + (i + 1) * 512] for i in range(3)])
        self.xin = ARing("xin", [aF[:, 4608 + i * 1024:4608 + (i + 1) * 1024] for i in range(2)])
        self.cqn = aB[:, 0:1024].rearrange("p (k t) -> p k t", k=2)
        self.ckvnT = aB[:, 1024:1536]
        self.accT = aF[:, 0:4096].rearrange("p (k t) -> p k t", k=NK)
        self.mixT = aF[:, 4096:8192].rearrange("p (k t) -> p k t", k=NK)
        self.yt = ARing("yt", [aF[:, 8192 + i * 1024:8192 + (i + 1) * 1024] for i in range(2)])
        self.gT = aB[:, 0:4096].rearrange("p (k t) -> p k t", k=NK)
        self.brT = aB[:, 4096:8192].rearrange("p (m c t) -> p m c t", m=4, c=2)
        self.aT = aB[:, 8192:8192 + NKF * TB].rearrange("p (k t) -> p k t", k=NKF)

    def mm(self, out, lhsT, rhs, start, stop, reads, writes):
        return self.P.op("pe", lambda e: e.matmul(out, lhsT=lhsT, rhs=rhs, start=start, stop=stop), reads, writes, pe_accum=True)

    def tr(self, out, in_, ident, reads, writes):
        return self.P.op("pe", lambda e: e.transpose(out, in_, ident), reads, writes, pe_accum=True)

    def act(self, out, in_, func, reads, writes, **kw):
        return self.P.op("act", lambda e: e.activation(out=out, in_=in_, func=func, **kw), reads, writes)

    def dve(self, fn, reads, writes):
        return self.P.op("dve", fn, reads, writes)

    def pool(self, fn, reads, writes):
        return self.P.op("pool", fn, reads, writes)

    def load(self, out, in_, reads=(), writes=()):
        return self.P.dma("sp", out, in_, reads=reads, writes=writes)

    def store(self, out, in_, reads=(), writes=(), is_output=False):
        return self.P.dma("pool", out, in_, reads=reads, writes=writes, is_output=is_output)

    def wload(self, dram_panel, kc, pc, rkey):
        t, k = self.wring.next()
        v = t[:, 0:kc * pc].rearrange("p (k c) -> p k c", k=kc)
        self.load(v, dram_panel, reads=[rkey], writes=[k])
        return v, k

    def phase_init(self):
        d = self.dram
        self.load(self.cst[:], d["consts"], writes=["cst"])
        for l in range(2):
            self.load(self.vecs[l][:], d["vecs"][l], writes=[f"vecs{l}"])
            self.load(self.rowsb[l][:], d["rowsb"][l], writes=[f"rowsb{l}"])
        self.load(self.c2[:], d["c2T"], writes=["c2"])
        self.dve(lambda e: e.tensor_copy(out=self.onesb[:], in_=self.cst[:, C_ONE:C_ONE + 128]), ["cst"], ["onesb"])
        self.dve(lambda e: e.tensor_copy(out=self.identb[:], in_=self.cst[:, C_ID:C_ID + 128]), ["cst"], ["identb"])
        self.act(self.c2b[:], self.c2[:], AF.Silu, ["c2"], ["c2b"])

    def phase_w(self, l):
        d = self.dram

        def cast(dst, src2d, n, pc, key):
            srcv = src2d.rearrange("(k p) c -> p k c", p=128)
            for i in range(n):
                self.store(dst[i], srcv[:, :, i * pc:(i + 1) * pc], writes=[(key, i)])
        cast(d[f"Wmod{l}"], d["wmod"][l], 12, 512, f"Wmod{l}")
        cast(d[f"Wfm{l}"], d["wfm"][l], 4, 512, f"Wfm{l}")
        cast(d[f"Wtok{l}"], d["wtok"][l], 2, 512, f"Wtok{l}")
        cast(d[f"Wgate{l}"], d["wgate"][l], 8, 512, f"Wgate{l}")
        for m_ in range(4):
            self.store(d[f"Wbr{l}"][m_], d["wbr"][l, m_].rearrange("(k p) c -> p k c", p=128), writes=[(f"Wbr{l}", m_)])
        cast(d[f"Wo{l}"], d["wo"][l], 2, 512, f"Wo{l}")
        cast(d[f"Wup{l}"], d["wup"][l], 11, 512, f"Wup{l}")
        cast(d[f"Wdown{l}"], d["wdown"][l], 8, 128, f"Wdown{l}")

    def layer_setup(self, l):
        d, P = self.dram, self.P
        self.store(self.wuq[:], d["wuq2"][l].rearrange("(k p) c -> p k c", p=128), writes=["wuq"])
        self.store(self.wukv[:], d["wukv2"][l], writes=["wukv"])
        ps, pk = self.psx.next()
        for n in range(12):
            w, wk = self.wload(d[f"Wmod{l}"][n], NK, 512, (f"Wmod{l}", n))
            for j in range(4):
                fc = n * 4 + j
                for k in range(NK):
                    self.mm(ps[:, fc * 2:fc * 2 + 2], w[:, k, j * 128:(j + 1) * 128], self.c2b[:, k, :],
                            k == 0, k == NK - 1, [wk, "c2b"], [pk])
        vec = self.vecs[l]
        for c_ in range(2):
            self.dve(lambda e, c_=c_: e.tensor_tensor(out=self.modv[:, :, c_], in0=ps[:, 0:96].rearrange("p (f c) -> p f c", c=2)[:, :, c_],
                                                      in1=vec[:, V_BMOD:V_BMOD + 48], op=ALU.add), [pk, f"vecs{l}"], ["modv"])
        gm = self.gms[l]
        for c_ in range(2):
            mv = lambda i, c_=c_: self.modv[:, i * 8:(i + 1) * 8, c_]
            self.dve(lambda e, c_=c_, mv=mv: e.scalar_tensor_tensor(out=gm[:, 0, :, c_], in0=mv(1), scalar=1.0, in1=vec[:, V_GPRE1:V_GPRE1 + 8], op0=ALU.add, op1=ALU.mult), ["modv", f"vecs{l}"], [f"gm{l}"])
            self.dve(lambda e, c_=c_, mv=mv: e.tensor_copy(out=gm[:, 1, :, c_], in_=mv(0)), ["modv"], [f"gm{l}"])
            self.dve(lambda e, c_=c_, mv=mv: e.tensor_tensor(out=gm[:, 2, :, c_], in0=mv(2), in1=vec[:, V_GPOST1:V_GPOST1 + 8], op=ALU.mult), ["modv", f"vecs{l}"], [f"gm{l}"])
            self.dve(lambda e, c_=c_, mv=mv: e.scalar_tensor_tensor(out=gm[:, 3, :, c_], in0=mv(4), scalar=1.0, in1=vec[:, V_GPRE2:V_GPRE2 + 8], op0=ALU.add, op1=ALU.mult), ["modv", f"vecs{l}"], [f"gm{l}"])
            self.dve(lambda e, c_=c_, mv=mv: e.tensor_copy(out=gm[:, 4, :, c_], in_=mv(3)), ["modv"], [f"gm{l}"])
            self.dve(lambda e, c_=c_, mv=mv: e.tensor_tensor(out=gm[:, 5, :, c_], in0=mv(5), in1=vec[:, V_GPOST2:V_GPOST2 + 8], op=ALU.mult), ["modv", f"vecs{l}"], [f"gm{l}"])

    def fm_rstd(self, src, skey, nk, nfeat, out_rstd, okey, krows=None):
        ps, pk = self.psx.next()
        for k in range(nk):
            rows = 128 if krows is None else krows[k]
            self.act(self.sq[0:rows, k, :], src[0:rows, k, :], AF.Square, [skey], [("sq", k)])
            self.mm(ps[:], self.onesb[0:rows, :], self.sq[0:rows, k, :], k == 0, k == nk - 1, [("sq", k), "onesb"], [pk])
        self.act(out_rstd, ps[:], AF.Sqrt, [pk], [okey], bias=self.eps_ap, scale=1.0 / nfeat)
        self.dve(lambda e: e.reciprocal(out=out_rstd, in_=out_rstd), [okey], [okey])

    def blk_info(self, bi):
        c = self.cfg
        t0 = bi * TB
        sample = bi < c.NBS
        return t0, sample, (0 if sample else 1)

    def phase_a_block(self, l, bi, load_x=True):
        c, d, P = self.cfg, self.dram, self.P
        t0, sample, cond = self.blk_info(bi)
        rope = sample
        ts = slice(t0, t0 + TB)
        xT = self.xT
        if l == 0:
            for tt in range(TB // 128):
                xi, xk = self.xin.next()
                src = d["xs"][t0 + tt * 128:t0 + (tt + 1) * 128, :] if sample else d["xp"][t0 - c.T_S + tt * 128:t0 - c.T_S + (tt + 1) * 128, :]
                self.load(xi[:], src, writes=[xk])
                for k in range(NK):
                    ps, pk = self.ps.next()
                    self.tr(ps[:, 0:128], xi[:, k * 128:(k + 1) * 128], self.cst[:, C_ID:C_ID + 128], [xk, "cst"], [pk])
                    if k % 2 == 0:
                        self.act(xT[:, k, tt * 128:(tt + 1) * 128], ps[:, 0:128], AF.Copy, [pk], [("xT", k)])
                    else:
                        self.dve(lambda e, k=k, ps=ps, tt=tt: e.tensor_copy(out=xT[:, k, tt * 128:(tt + 1) * 128], in_=ps[:, 0:128]), [pk], [("xT", k)])
            self.store(d["XT"].rearrange("(k p) t -> p k t", p=128)[:, :, ts], xT[:], reads=[("xT", k) for k in range(NK)], writes=[("XT", bi)])
        elif load_x:
            self.load(xT[:], d["XT"].rearrange("(k p) t -> p k t", p=128)[:, :, ts], reads=[("XT", bi)], writes=[("xT", k) for k in range(NK)])
        xkeys = [("xT", k) for k in range(NK)]
        ps, pk = self.psx.next()
        for k in range(NK):
            self.act(self.sq[:, k, :], xT[:, k, :], AF.Square, [("xT", k)], [("sq", k)])
            self.mm(ps[:], self.onesb[:], self.sq[:, k, :], k == 0, k == NK - 1, [("sq", k), "onesb"], [pk])
        self.act(self.rstd[:], ps[:], AF.Sqrt, [pk], ["rstd"], bias=self.eps_ap, scale=1.0 / D)
        self.dve(lambda e: e.reciprocal(out=self.rstd[:], in_=self.rstd[:]), ["rstd"], ["rstd"])
        for k in range(NK):
            t, tk = self.f32r.next()
            self.dve(lambda e, k=k, t=t: e.tensor_tensor(out=t[:], in0=xT[:, k, :], in1=self.rstd[:], op=ALU.mult), [("xT", k), "rstd"], [tk])
            if True:
                self.dve(lambda e, k=k, t=t: e.tensor_scalar(out=self.hT[:, k, :], in0=t[:], scalar1=self.gms[l][:, 0, k, cond:cond + 1], scalar2=self.gms[l][:, 1, k, cond:cond + 1], op0=ALU.mult, op1=ALU.add), [tk, f"gm{l}"], [("hT", k)])
            else:
                self.act(self.hT[:, k, :], t[:], AF.Identity, [tk], [("hT", k)], scale=self.gms[l][:, 0, k, cond:cond + 1], bias=self.gms[l][:, 1, k, cond:cond + 1])
        hkeys = [("hT", k) for k in range(NK)]
        self.store(d["HT"].rearrange("(k p) t -> p k t", p=128)[:, :, ts], self.hT[:], reads=hkeys, writes=[("HT", bi)])
        if rope:
            self.load(self.tab[:], d["ropeA"][:, :, ts], writes=["tab"])
            self.load(self.tabB[:], d["ropeB"][:, :, ts], writes=["tabB"])
        hT = self.hT

        def proj_fm(w, wk, c0, m):
            ps, pk = self.ps.next()
            for k in range(NK):
                self.mm(ps[0:m, :], w[:, k, c0:c0 + m], hT[:, k, :], k == 0, k == NK - 1, [wk, ("hT", k)], [pk])
            return ps, pk

        def evac_store(ps, pk, m, dst, dt, eng="act", wkey=None):
            ring = self.bf16r if dt == BF16 else self.f32r
            t, tk = ring.next()
            if eng == "act":
                self.act(t[0:m, :], ps[0:m, :], AF.Copy, [pk], [tk])
            else:
                self.dve(lambda e: e.tensor_copy(out=t[0:m, :], in_=ps[0:m, :]), [pk], [tk])
            self.store(dst, t[0:m, :], reads=[tk], writes=[wkey] if wkey else [])

        def rope_store(ps, pk, psp, ppk, m, tabt, tabk, dsts, wkey, p0=0):
            a1, ak1 = self.f32r.next()
            a2, ak2 = self.f32r.next()
            o, ok = self.bf16r.next()
            sl = slice(p0, p0 + m)
            self.act(a1[sl, :], ps[sl, :], AF.Copy, [pk], [ak1])
            self.act(a2[sl, :], psp[sl, :], AF.Copy, [ppk], [ak2])
            self.dve(lambda e: e.tensor_tensor(out=a1[sl, :], in0=a1[sl, :], in1=tabt[sl, 0, :], op=ALU.mult), [ak1, tabk], [ak1])
            self.dve(lambda e: e.tensor_tensor(out=a2[sl, :], in0=a2[sl, :], in1=tabt[sl, 1, :], op=ALU.mult), [ak2, tabk], [ak2])
            self.dve(lambda e: e.tensor_tensor(out=o[sl, :], in0=a1[sl, :], in1=a2[sl, :], op=ALU.add), [ak1, ak2], [ok])
            for dst in dsts:
                self.store(dst, o[sl, :], reads=[ok], writes=[wkey])
            return o, ok

        Wfm = d[f"Wfm{l}"]
        w0, wk0 = self.wload(Wfm[0], NK, 512, (f"Wfm{l}", 0))
        w1, wk1 = self.wload(Wfm[1], NK, 512, (f"Wfm{l}", 1))
        w3 = wk3 = None
        if rope:
            w3, wk3 = self.wload(Wfm[3], NK, 512, (f"Wfm{l}", 3))
        for j in range(2):
            ps, pk = proj_fm(w0, wk0, j * 128, 128)
            evac_store(ps, pk, 128, d["AQT"][j * 128:(j + 1) * 128, ts], BF16, "act", ("AQT", bi))
            if rope:
                psp, ppk = proj_fm(w3, wk3, 128 + j * 128, 128)
                rope_store(ps, pk, psp, ppk, 128, self.tab, "tab", [d["AQRT"][j * 128:(j + 1) * 128, ts]], ("AQRT", bi))
        ps, pk = proj_fm(w0, wk0, 256, 128)
        if rope:
            psp, ppk = proj_fm(w3, wk3, 384, 128)
            rope_store(ps, pk, psp, ppk, 128, self.tab, "tab", [d["AKT"][:, ts]], ("AKT", bi))
        else:
            evac_store(ps, pk, 128, d["AKT"][:, ts], BF16, "act", ("AKT", bi))
        ps, pk = proj_fm(w0, wk0, 384, 128)
        self.act(self.cqs[:, 0, :], ps[:], AF.Copy, [pk], [("cqs", 0)])
        ps, pk = proj_fm(w1, wk1, 0, 64)
        self.act(self.cqs[0:64, 1, :], ps[0:64, :], AF.Copy, [pk], [("cqs", 1)])
        pss, psk = self.psx.next()
        for k, rows in ((0, 128), (1, 64)):
            self.act(self.sq[0:rows, k, :], self.cqs[0:rows, k, :], AF.Square, [("cqs", k)], [("sq", k)])
            self.mm(pss[:], self.onesb[0:rows, :], self.sq[0:rows, k, :], k == 0, k == 1, [("sq", k), "onesb"], [psk])
        rq, rqk = self.f32r.next()
        self.act(rq[:], pss[:], AF.Sqrt, [psk], [rqk], bias=self.eps_ap, scale=1.0 / 192)
        self.dve(lambda e: e.reciprocal(out=rq[:], in_=rq[:]), [rqk], [rqk])
        vec = self.vecs[l]
        for k, rows in ((0, 128), (1, 64)):
            self.dve(lambda e, k=k, rows=rows: e.scalar_tensor_tensor(out=self.cqn[0:rows, k, :], in0=self.cqs[0:rows, k, :], scalar=vec[0:rows, V_GCQ + k:V_GCQ + k + 1],
                                                                      in1=rq[0:rows, :], op0=ALU.mult, op1=ALU.mult), [("cqs", k), rqk, f"vecs{l}"], [("cqn", k)])
        for h in range(4):
            ps, pk = self.ps.next()
            self.mm(ps[0:96, :], self.wuq[:, 0, h * 96:(h + 1) * 96], self.cqn[:, 0, :], True, False, ["wuq", ("cqn", 0)], [pk])
            self.mm(ps[0:96, :], self.wuq[0:64, 1, h * 96:(h + 1) * 96], self.cqn[0:64, 1, :], False, True, ["wuq", ("cqn", 1)], [pk])
            t, tk = self.bf16r.next()
            self.act(t[0:96, :], ps[0:96, :], AF.Copy, [pk], [tk])
            self.store(d["QCT"][h, :, ts], t[0:96, :], reads=[tk], writes=[("QCT", bi)])
            if rope:
                psp, ppk = self.ps.next()
                self.mm(psp[0:96, :], self.wuq[:, 0, 384 + h * 96:384 + (h + 1) * 96], self.cqn[:, 0, :], True, False, ["wuq", ("cqn", 0)], [ppk])
                self.mm(psp[0:96, :], self.wuq[0:64, 1, 384 + h * 96:384 + (h + 1) * 96], self.cqn[0:64, 1, :], False, True, ["wuq", ("cqn", 1)], [ppk])
                self.store(d["QLT"][h, 0:64, ts], t[0:64, :], reads=[tk], writes=[("QLT", bi)])
                rope_store(ps, pk, psp, ppk, 32, self.tabB, "tabB", [d["QLT"][h, 64:96, ts]], ("QLT", bi), p0=64)
        ps, pk = proj_fm(w1, wk1, 64, 32)
        bkt_dsts = [d["BKT"][h, 64:96, ts] for h in range(4)]
        if rope:
            psp, ppk = proj_fm(w1, wk1, 96, 32)
            rope_store(ps, pk, psp, ppk, 32, self.tabB, "tabB", bkt_dsts, ("BKT", bi))
        else:
            t, tk = self.bf16r.next()
            self.act(t[0:32, :], ps[0:32, :], AF.Copy, [pk], [tk])
            for dst in bkt_dsts:
                self.store(dst, t[0:32, :], reads=[tk], writes=[("BKT", bi)])
        for j in range(2):
            ps, pk = proj_fm(w1, wk1, 128 + j * 128, 128)
            evac_store(ps, pk, 128, d["CT"][j * 128:(j + 1) * 128, ts], F32, "dve" if j else "act", ("CT", bi))
        ps, pk = proj_fm(w1, wk1, 384, 128)
        evac_store(ps, pk, 128, d["DTs"][0:128, ts], F32, "act", ("DTs", bi))
        w2, wk2 = self.wload(Wfm[2], NK, 512, (f"Wfm{l}", 2))
        for j in range(4):
            ps, pk = proj_fm(w2, wk2, j * 128, 128)
            evac_store(ps, pk, 128, d["DTs"][(j + 1) * 128:(j + 2) * 128, ts], F32, "dve" if j % 2 else "act", ("DTs", bi))
        if not rope:
            w3, wk3 = self.wload(Wfm[3], NK, 512, (f"Wfm{l}", 3))
        ps, pk = proj_fm(w3, wk3, 0, 128)
        evac_store(ps, pk, 128, d["DTs"][640:768, ts], F32, "dve", ("DTs", bi))
        wt0, wtk0 = self.wload(d[f"Wtok{l}"][0], NK, 512, (f"Wtok{l}", 0))
        wt1, wtk1 = self.wload(d[f"Wtok{l}"][1], NK, 512, (f"Wtok{l}", 1))
        rb = self.rowsb[l]
        for tt in range(TB // 128):
            tsl = slice(tt * 128, (tt + 1) * 128)
            g0 = t0 + tt * 128
            psA, pkA = self.ps.next()
            for k in range(NK):
                self.mm(psA[:, 0:416], hT[:, k, tsl], wt0[:, k, 0:416], k == 0, k == NK - 1, [wtk0, ("hT", k)], [pkA])
            psB, pkB = self.ps.next()
            for k in range(NK):
                self.mm(psB[:, 0:272], hT[:, k, tsl], wt1[:, k, 0:272], k == 0, k == NK - 1, [wtk1, ("hT", k)], [pkB])
            tz, tzk = self.tokf.next()
            self.act(tz[:, 0:272], psB[:, 0:272], AF.Copy, [pkB], [tzk])
            self.store(d["DZ"][g0:g0 + 128, :], tz[:, 0:272], reads=[tzk], writes=[("DZ", bi)])
            ta, tak = self.tokf.next()
            self.dve(lambda e, ta=ta, psA=psA: e.tensor_copy(out=ta[:, 0:416], in_=psA[:, 0:416]), [pkA], [tak])
            tb, tbk = self.bf16r.next()
            self.dve(lambda e, ta=ta, tb=tb: e.tensor_copy(out=tb[:, 0:128], in_=ta[:, 128:256]), [tak], [tbk])
            self.store(d["AV"][g0:g0 + 128, :], tb[:, 0:128], reads=[tbk], writes=[("AV", bi)])
            junk, jk = self.f32r.next()
            sm, smk = self.small.next()
            self.act(junk[:, 0:128], ta[:, 256:384], AF.Square, [tak], [jk, smk], accum_out=sm[:, 0:1])
            self.act(sm[:, 1:2], sm[:, 0:1], AF.Sqrt, [smk], [smk], bias=self.eps_ap, scale=1.0 / 128)
            self.dve(lambda e, sm=sm: e.reciprocal(out=sm[:, 2:3], in_=sm[:, 1:2]), [smk], [smk])
            cn, cnk = self.f32r.next()
            self.dve(lambda e, cn=cn, ta=ta, sm=sm: e.scalar_tensor_tensor(out=cn[:, 0:128], in0=ta[:, 256:384], scalar=sm[:, 2:3], in1=rb[:, R_GCKV:R_GCKV + 128],
                                                                         op0=ALU.mult, op1=ALU.mult), [tak, smk, f"rowsb{l}"], [cnk])
            if not sample:
                sq_, tq_ = divmod(g0 - c.T_S, 256)
                self.store(d["o_ak"][sq_, l, tq_:tq_ + 128, :], ta[:, 0:128], reads=[tak], is_output=True)
                self.store(d["o_av"][sq_, l, tq_:tq_ + 128, :], ta[:, 128:256], reads=[tak], is_output=True)
                self.store(d["o_kr"][sq_, l, tq_:tq_ + 128, :], ta[:, 384:416], reads=[tak], is_output=True)
                self.store(d["o_ckv"][sq_, l, tq_:tq_ + 128, :], cn[:, 0:128], reads=[cnk], is_output=True)
            psT, pkT = self.ps.next()
            self.tr(psT[:, 0:128], cn[:, 0:128], self.cst[:, C_ID:C_ID + 128], [cnk, "cst"], [pkT])
            self.act(self.ckvnT[:, tsl], psT[:, 0:128], AF.Copy, [pkT], [("ckvnT", tt)])
            self.mla_v(self.ckvnT[:, tsl], [("ckvnT", tt)], g0, ("BV", bi))
        self.mla_knope(self.ckvnT[:], [("ckvnT", i) for i in range(4)], TB, t0, ("BKT", bi))

    def mla_v(self, ckvnT_tile, key, g0, wkey):
        d = self.dram
        ps, pk = self.ps.next()
        self.mm(ps[:, 0:256], ckvnT_tile, self.wukv[:, 256:512], True, True, list(key) + ["wukv"], [pk])
        t, tk = self.bf16r.next()
        self.dve(lambda e: e.tensor_copy(out=t[:, 0:256], in_=ps[:, 0:256]), [pk], [tk])
        self.store(d["BV"][g0:g0 + 128, :], t[:, 0:256], reads=[tk], writes=[wkey])

    def mla_knope(self, ckvnT, key, n, t0, wkey):
        d = self.dram
        for h in range(4):
            ps, pk = self.ps.next()
            self.mm(ps[0:64, 0:n], self.wukv[:, h * 64:(h + 1) * 64], ckvnT, True, True, list(key) + ["wukv"], [pk])
            t, tk = self.bf16r.next()
            self.act(t[0:64, 0:n], ps[0:64, 0:n], AF.Copy, [pk], [tk])
            self.store(d["BKT"][h, 0:64, t0:t0 + n], t[0:64, 0:n], reads=[tk], writes=[wkey])

    def phase_ctx(self, l):
        c, d = self.cfg, self.dram
        NT = c.NT
        ident = self.cst[:, C_ID:C_ID + 128]
        for tt in range(2):
            tsl = slice(tt * 128, (tt + 1) * 128)
            xi, xk = self.xin.next()
            self.load(xi[:, 0:128], d["cckv"][l, tsl, :], writes=[xk])
            self.load(xi[:, 128:160], d["ckr"][l, tsl, :], writes=[xk])
            self.load(xi[:, 256:384], d["cak"][l, tsl, :], writes=[xk])
            ps, pk = self.ps.next()
            self.tr(ps[:, 0:128], xi[:, 0:128], ident, [xk, "cst"], [pk])
            self.act(self.ckvnT[:, tsl], ps[:, 0:128], AF.Copy, [pk], [("ckvnT", tt)])
            self.mla_v(self.ckvnT[:, tsl], [("ckvnT", tt)], NT + tt * 128, ("BV", "ctx"))
            ps, pk = self.ps.next()
            self.tr(ps[0:32, 0:128], xi[:, 128:160], ident, [xk, "cst"], [pk])
            t, tk = self.bf16r.next()
            self.act(t[0:32, 0:128], ps[0:32, 0:128], AF.Copy, [pk], [tk])
            for h in range(4):
                self.store(d["BKT"][h, 64:96, NT + tt * 128:NT + (tt + 1) * 128], t[0:32, 0:128], reads=[tk], writes=[("BKT", "ctx")])
            ps, pk = self.ps.next()
            self.tr(ps[:, 0:128], xi[:, 256:384], ident, [xk, "cst"], [pk])
            t, tk = self.bf16r.next()
            self.act(t[:, 0:128], ps[:, 0:128], AF.Copy, [pk], [tk])
            self.store(d["AKCT"][:, tsl], t[:, 0:128], reads=[tk], writes=[("AKCT", 0)])
        self.mla_knope(self.ckvnT[:, 0:256], [("ckvnT", 0), ("ckvnT", 1)], 256, NT, ("BKT", "ctx"))

    def setup_eps(self):
        self.epst = self.P.sbuf("epst", [128, 1], F32)
        self.pool(lambda e: e.memset(self.epst[:], EPS), [], ["epst"])
        self.eps_ap = self.epst[:, 0:1]


class BuilderC(Builder):
    def norm_resid(self, l, gidx, cond):
        ps, pk = self.psx.next()
        for k in range(NK):
            self.act(self.sq[:, k, :], self.mixT[:, k, :], AF.Square, [("mixT", k)], [("sq", k)])
            self.mm(ps[:], self.onesb[:], self.sq[:, k, :], k == 0, k == NK - 1, [("sq", k), "onesb"], [pk])
        self.act(self.rstd[:], ps[:], AF.Sqrt, [pk], ["rstd"], bias=self.eps_ap, scale=1.0 / D)
        self.dve(lambda e: e.reciprocal(out=self.rstd[:], in_=self.rstd[:]), ["rstd"], ["rstd"])
        gm = self.gms[l]
        for k in range(NK):
            t, tk = self.f32r.next()
            self.dve(lambda e, k=k, t=t: e.tensor_tensor(out=t[:], in0=self.mixT[:, k, :], in1=self.rstd[:], op=ALU.mult), [("mixT", k), "rstd"], [tk])
            self.dve(lambda e, k=k, t=t: e.scalar_tensor_tensor(out=self.xT[:, k, :], in0=t[:], scalar=gm[:, gidx, k, cond:cond + 1], in1=self.xT[:, k, :],
                                                              op0=ALU.mult, op1=ALU.add), [tk, f"gm{l}", ("xT", k)], [("xT", k)])

    def norm_h(self, l, gs, gb, cond):
        ps, pk = self.psx.next()
        for k in range(NK):
            self.act(self.sq[:, k, :], self.xT[:, k, :], AF.Square, [("xT", k)], [("sq", k)])
            self.mm(ps[:], self.onesb[:], self.sq[:, k, :], k == 0, k == NK - 1, [("sq", k), "onesb"], [pk])
        self.act(self.rstd[:], ps[:], AF.Sqrt, [pk], ["rstd"], bias=self.eps_ap, scale=1.0 / D)
        self.dve(lambda e: e.reciprocal(out=self.rstd[:], in_=self.rstd[:]), ["rstd"], ["rstd"])
        gm = self.gms[l]
        for k in range(NK):
            t, tk = self.f32r.next()
            self.dve(lambda e, k=k, t=t: e.tensor_tensor(out=t[:], in0=self.xT[:, k, :], in1=self.rstd[:], op=ALU.mult), [("xT", k), "rstd"], [tk])
            self.dve(lambda e, k=k, t=t: e.tensor_scalar(out=self.hT[:, k, :], in0=t[:], scalar1=gm[:, gs, k, cond:cond + 1], scalar2=gm[:, gb, k, cond:cond + 1],
                                                       op0=ALU.mult, op1=ALU.add), [tk, f"gm{l}"], [("hT", k)])

    def phase_c_block(self, l, bi, last):
        c, d = self.cfg, self.dram
        t0, sample, cond = self.blk_info(bi)
        ts = slice(t0, t0 + TB)
        xT, hT, gT, brT, accT, mixT, aT = self.xT, self.hT, self.gT, self.brT, self.accT, self.mixT, self.aT
        self.load(hT[:], d["HT"].rearrange("(k p) t -> p k t", p=128)[:, :, ts], reads=[("HT", bi)], writes=[("hT", k) for k in range(NK)])
        self.load(xT[:], d["XT"].rearrange("(k p) t -> p k t", p=128)[:, :, ts], reads=[("XT", bi)], writes=[("xT", k) for k in range(NK)])
        for m in range(4):
            self.load(brT[:, m], d["BRT"][m].rearrange("(c p) t -> p c t", p=128)[:, :, ts], reads=[("BRT", m)], writes=[("brT", m)])
        for m in range(4):
            wg = [self.wload(d[f"Wgate{l}"][2 * m + i], NK, 512, (f"Wgate{l}", 2 * m + i)) for i in range(2)]
            wb, wbk = self.wload(d[f"Wbr{l}"][m], 2, 1024, (f"Wbr{l}", m))
            for fc in range(NK):
                w, wk = wg[fc // 4]
                co = (fc % 4) * 128
                psg, pgk = self.ps.next()
                for k in range(NK):
                    self.mm(psg[:], w[:, k, co:co + 128], hT[:, k, :], k == 0, k == NK - 1, [wk, ("hT", k)], [pgk])
                psb, pbk = self.ps.next()
                for cc in range(2):
                    self.mm(psb[:], wb[:, cc, fc * 128:(fc + 1) * 128], brT[:, m, cc, :], cc == 0, cc == 1, [wbk, ("brT", m)], [pbk])
                sg, sgk = self.f32r.next()
                self.act(sg[:], psg[:], AF.Sigmoid, [pgk], [sgk])
                if m == 0:
                    self.dve(lambda e, sg=sg, psb=psb, fc=fc: e.tensor_tensor(out=accT[:, fc, :], in0=sg[:], in1=psb[:], op=ALU.mult), [sgk, pbk], [("accT", fc)])
                else:
                    self.dve(lambda e, sg=sg, psb=psb: e.tensor_tensor(out=sg[:], in0=sg[:], in1=psb[:], op=ALU.mult), [sgk, pbk], [sgk])
                    if m < 3:
                        self.dve(lambda e, sg=sg, fc=fc: e.tensor_tensor(out=accT[:, fc, :], in0=accT[:, fc, :], in1=sg[:], op=ALU.add), [sgk, ("accT", fc)], [("accT", fc)])
                    else:
                        self.dve(lambda e, sg=sg, fc=fc: e.tensor_tensor(out=gT[:, fc, :], in0=accT[:, fc, :], in1=sg[:], op=ALU.add), [sgk, ("accT", fc)], [("gT", fc)])
        wo = [self.wload(d[f"Wo{l}"][i], NK, 512, (f"Wo{l}", i)) for i in range(2)]
        for fc in range(NK):
            w, wk = wo[fc // 4]
            co = (fc % 4) * 128
            ps, pk = self.ps.next()
            for k in range(NK):
                self.mm(ps[:], w[:, k, co:co + 128], gT[:, k, :], k == 0, k == NK - 1, [wk, ("gT", k)], [pk])
            self.act(mixT[:, fc, :], ps[:], AF.Copy, [pk], [("mixT", fc)])
        self.norm_resid(l, 2, cond)
        self.norm_h(l, 3, 4, cond)
        for n in range(11):
            w, wk = self.wload(d[f"Wup{l}"][n], NK, 512, (f"Wup{l}", n))
            for j in range(2):
                psg, pgk = self.ps.next()
                for k in range(NK):
                    self.mm(psg[:], w[:, k, j * 128:(j + 1) * 128], hT[:, k, :], k == 0, k == NK - 1, [wk, ("hT", k)], [pgk])
                psu, puk = self.ps.next()
                for k in range(NK):
                    self.mm(psu[:], w[:, k, 256 + j * 128:256 + (j + 1) * 128], hT[:, k, :], k == 0, k == NK - 1, [wk, ("hT", k)], [puk])
                sg, sgk = self.f32r.next()
                self.act(sg[:], psg[:], AF.Silu, [pgk], [sgk])
                kf = 2 * n + j
                self.dve(lambda e, sg=sg, psu=psu, kf=kf: e.tensor_tensor(out=aT[:, kf, :], in0=sg[:], in1=psu[:], op=ALU.mult), [sgk, puk], [("aT", kf)])
        for fc in range(NK):
            w, wk = self.wload(d[f"Wdown{l}"][fc], NKF, 128, (f"Wdown{l}", fc))
            ps, pk = self.ps.next()
            for kf in range(NKF):
                self.mm(ps[:], w[:, kf, :], aT[:, kf, :], kf == 0, kf == NKF - 1, [wk, ("aT", kf)], [pk])
            self.act(mixT[:, fc, :], ps[:], AF.Copy, [pk], [("mixT", fc)])
        self.norm_resid(l, 5, cond)
        xkeys = [("xT", k) for k in range(NK)]
        if not last:
            self.store(d["XT"].rearrange("(k p) t -> p k t", p=128)[:, :, ts], xT[:], reads=xkeys, writes=[("XT", bi)])
        else:
            ident = self.cst[:, C_ID:C_ID + 128]
            for tt in range(TB // 128):
                yt, ytk = self.yt.next()
                for k in range(NK):
                    ps, pk = self.ps.next()
                    self.tr(ps[:, 0:128], xT[:, k, tt * 128:(tt + 1) * 128], ident, [("xT", k), "cst"], [pk])
                    if k % 2 == 0:
                        self.act(yt[:, k * 128:(k + 1) * 128], ps[:, 0:128], AF.Copy, [pk], [ytk])
                    else:
                        self.dve(lambda e, yt=yt, ps=ps, k=k: e.tensor_copy(out=yt[:, k * 128:(k + 1) * 128], in_=ps[:, 0:128]), [pk], [ytk])
                g0 = t0 + tt * 128
                dst = d["ys"][g0:g0 + 128, :] if sample else d["yp"][g0 - c.T_S:g0 - c.T_S + 128, :]
                self.store(dst, yt[:], reads=[ytk], is_output=True)


A_SCALE = 0.125
B_SCALE = 96.0 ** -0.5
C_WINDOWS = (2, 4, 8, 16)


class BuilderM(BuilderC):
    def mixer_rings(self):
        pst = self.ps.tiles + self.psx.tiles
        self.sc = ARing("sc", [pst[i][:] for i in range(4)])
        self.accp = ARing("accp", [pst[i][:] for i in range(4, 8)])

    def mixer_setup(self, l):
        d = self.dram
        aF, aB = self.arenaF, self.arenaB
        rb = self.rowsb[l]
        self.sinkexp = aF[0:64, 11776:12288]
        self.maskP = aB[:, 23552:24064]
        self.maskN = aB[:, 24064:24576]
        self.wpbd = aB[:, 23296:23552].rearrange("p (c d) -> p c d", c=2)
        wpf = aF[:, 11520:11776].rearrange("p (c d) -> p c d", c=2)
        se, sek = self.small.next()
        self.act(se[:, 0:4], rb[:, R_SINK:R_SINK + 4], AF.Exp, [f"rowsb{l}"], [sek])
        for h in range(4):
            self.dve(lambda e, h=h: e.tensor_scalar(out=self.sinkexp[:, h * 128:(h + 1) * 128], in0=self.cst[0:64, C_ONE:C_ONE + 128], scalar1=se[0:64, h:h + 1], scalar2=None, op0=ALU.mult),
                     [sek, "cst"], ["sinkexp"])
            self.dve(lambda e, h=h: e.tensor_copy(out=self.maskP[:, h * 128:(h + 1) * 128], in_=self.cst[:, C_UI:C_UI + 128]), ["cst"], ["maskP"])
            self.dve(lambda e, h=h: e.tensor_copy(out=self.maskN[:, h * 128:(h + 1) * 128], in_=self.cst[:, C_LI:C_LI + 128]), ["cst"], ["maskN"])
        self.pool(lambda e: e.memset(wpf[:], 0.0), [], ["wpf"])
        for g in range(4):
            p0 = (g % 2) * 64
            self.load(wpf[p0:p0 + 64, g // 2, p0:p0 + 64], d["wpool"][l, g], writes=["wpf"])
        self.dve(lambda e: e.tensor_copy(out=self.wpbd[:], in_=wpf[:]), ["wpf"], ["wpbd"])

    def mixer_c(self, l):
        c, d = self.cfg, self.dram
        aF, aB = self.arenaF, self.arenaB
        L = 528
        X = aF[:, 0:2 * L].rearrange("p (c t) -> p c t", c=2)
        Pa = aF[:, 2 * L:3 * L]
        Pb = aF[:, 3 * L:4 * L]
        Y = aF[:, 4 * L:4 * L + 512]
        IC = aF[:, 5 * L:5 * L + 1024].rearrange("p (c t) -> p c t", c=2)
        Yb = aB[:, 18432:18944]
        vec = self.vecs[l]
        segs = []
        for s0 in range(0, c.T_S, 512):
            segs.append((0, c.T_S, s0, 512, "icS", s0))
        for s in range(c.NSEQ):
            segs.append((c.T_S + s * 256, 256, 0, 256, "icP", 0))
        for (base, T, s0, n, ictab, ic0) in segs:
            lo, hi = max(s0 - 8, 0), min(s0 + n + 8, T)
            self.dve(lambda e: e.memset(X[:], 0.0), [], ["X"])
            self.load(X[:, :, lo - (s0 - 8):hi - (s0 - 8)], d["CT"].rearrange("(c p) t -> p c t", p=128)[:, :, base + lo:base + hi], writes=["X"])
            self.load(IC[:, :, 0:n], d[ictab][:, :, ic0:ic0 + n], writes=["IC"])
            for ch in range(2):
                x = X[:, ch, :]
                self.dve(lambda e, x=x: e.tensor_tensor(out=Pa[:, 0:L - 1], in0=x[:, 0:L - 1], in1=x[:, 1:L], op=ALU.add), ["X"], ["Pa"])
                self.dve(lambda e: e.tensor_tensor(out=Pb[:, 0:L - 3], in0=Pa[:, 0:L - 3], in1=Pa[:, 2:L - 1], op=ALU.add), ["Pa"], ["Pb"])
                if ch == 0:
                    srcs = [(Pa, 2, "Pa"), (Pb, 4, "Pb")]
                else:
                    self.dve(lambda e: e.tensor_tensor(out=Pa[:, 0:L - 7], in0=Pb[:, 0:L - 7], in1=Pb[:, 4:L - 3], op=ALU.add), ["Pb"], ["Pa"])
                    self.dve(lambda e: e.tensor_tensor(out=Pb[:, 0:L - 15], in0=Pa[:, 0:L - 15], in1=Pa[:, 8:L - 7], op=ALU.add), ["Pa"], ["Pb"])
                    srcs = [(Pa, 8, "Pa"), (Pb, 16, "Pb")]
                for half, (Pw, w, pk_) in enumerate(srcs):
                    sl = slice(half * 64, half * 64 + 64)
                    o0 = 8 - w // 2
                    self.dve(lambda e, Pw=Pw, sl=sl, o0=o0, ch=ch, n=n: e.tensor_tensor(out=Y[sl, 0:n], in0=Pw[sl, o0:o0 + n], in1=IC[sl, ch, 0:n], op=ALU.mult), [pk_, "IC"], ["Y"])
                    self.dve(lambda e, sl=sl, x=x, n=n: e.tensor_tensor(out=Yb[sl, 0:n], in0=Y[sl, 0:n], in1=x[sl, 8:8 + n], op=ALU.subtract), ["Y", "X"], ["Yb"])
                ps, pk = self.sc.next()
                self.mm(ps[:, 0:n], self.wpbd[:, ch, :], Yb[:, 0:n], True, True, ["wpbd", "Yb"], [pk])
                o, ok = self.bf16r.next()
                self.dve(lambda e, o=o, ps=ps, ch=ch, n=n: e.tensor_scalar(out=o[:, 0:n], in0=ps[:, 0:n], scalar1=vec[:, V_CSC + ch:V_CSC + ch + 1], scalar2=None, op0=ALU.mult), [pk, f"vecs{l}"], [ok])
                self.store(d["BRT"][2, ch * 128:(ch + 1) * 128, base + s0:base + s0 + n], o[:, 0:n], reads=[ok], writes=[("BRT", 2)])
                yield

    def attn_a_qblock(self, keytiles, qsl_dst):
        d = self.dram
        pts = []
        for (kfn, vt, qfn, mask, rkeys) in keytiles:
            ps, pk = self.sc.next()
            for h in range(4):
                self.mm(ps[:, h * 128:(h + 1) * 128], kfn(h // 2), qfn(h), True, True, rkeys, [pk])
            pt, ptk = self.ptr.next()
            self.act(pt[:], ps[:], AF.Exp, [pk], [ptk], scale=A_SCALE)
            if mask is not None:
                self.dve(lambda e, pt=pt, mask=mask: e.tensor_tensor(out=pt[:], in0=pt[:], in1=mask, op=ALU.mult), [ptk, "maskP", "maskN"], [ptk])
            pts.append((pt, ptk, vt, rkeys))
        pso, pok = self.accp.next()
        for h in range(4):
            for j, (pt, ptk, vt, rkeys) in enumerate(pts):
                self.mm(pso[0:64, h * 128:(h + 1) * 128], vt[:, (h // 2) * 64:(h // 2) * 64 + 64], pt[:, h * 128:(h + 1) * 128], j == 0, j == len(pts) - 1, [ptk] + list(rkeys), [pok])
        psd, pdk = self.accp.next()
        for j, (pt, ptk, vt, rkeys) in enumerate(pts):
            self.mm(psd[0:64, :], self.onesb[:, 0:64], pt[:], j == 0, j == len(pts) - 1, [ptk, "onesb"], [pdk])
        den, dk = self.f32r.next()
        self.dve(lambda e, den=den, psd=psd: e.tensor_tensor(out=den[0:64, :], in0=psd[0:64, :], in1=self.sinkexp, op=ALU.add), [pdk, "sinkexp"], [dk])
        self.dve(lambda e, den=den: e.reciprocal(out=den[0:64, :], in_=den[0:64, :]), [dk], [dk])
        o, ok = self.bf16r.next()
        self.dve(lambda e, o=o, den=den, pso=pso: e.tensor_tensor(out=o[0:64, :], in0=pso[0:64, :], in1=den[0:64, :], op=ALU.mult), [pok, dk], [ok])
        self.store(d["BRT"][0].rearrange("(h d) t -> d h t", d=64)[:, :, qsl_dst], o[0:64, :].rearrange("p (h t) -> p h t", h=4), reads=[ok], writes=[("BRT", 0)])

    def mixer_a(self, l):
        c, d = self.cfg, self.dram
        aB = self.arenaB
        SEG = 1024
        qT = aB[0:64, 0:4096].rearrange("p (h t) -> p h t", h=4)
        qrT = aB[0:64, 4096:8192].rearrange("p (h t) -> p h t", h=4)
        kT = aB[0:64, 8192:8192 + 2 * 1280].rearrange("p (h t) -> p h t", h=2)
        vT = aB[:, 10752:10752 + 10 * 128].rearrange("p (j e) -> p j e", j=10)
        kcT = aB[0:64, 12032:12544].rearrange("p (h t) -> p h t", h=2)
        vc = aB[:, 12544:12800].rearrange("p (j e) -> p j e", j=2)
        self.ptr = ARing("ptr", [aB[:, 12800 + i * 512:12800 + (i + 1) * 512] for i in range(6)])
        AQT3 = d["AQT"].rearrange("(h d) t -> d h t", d=64)
        AQR3 = d["AQRT"].rearrange("(h d) t -> d h t", d=64)
        AKT3 = d["AKT"].rearrange("(h d) t -> d h t", d=64)
        AKC3 = d["AKCT"].rearrange("(h d) t -> d h t", d=64)
        self.load(kcT[:], AKC3, reads=[("AKCT", 0)], writes=["kcT"])
        self.store(vc[:], d["cav"][l].rearrange("(j p) e -> p j e", p=128), writes=["vc"])
        for s0 in range(0, c.T_S, SEG):
            n = min(SEG, c.T_S - s0)
            klo, khi = max(s0 - 128, 0), min(s0 + n + 128, c.T_S)
            self.load(qT[:, :, 0:n], AQT3[:, :, s0:s0 + n], reads=[("AQT", "all")], writes=["qT"])
            self.load(qrT[:, :, 0:n], AQR3[:, :, s0:s0 + n], reads=[("AQRT", "all")], writes=["qrT"])
            self.load(kT[:, :, 0:khi - klo], AKT3[:, :, klo:khi], reads=[("AKT", "all")], writes=["kT"])
            self.load(vT[:, 0:(khi - klo) // 128, :], d["AV"][klo:khi, :].rearrange("(j p) e -> p j e", p=128), reads=[("AV", "all")], writes=["vT"])
            for qb in range(n // 128):
                q0 = s0 + qb * 128
                tiles = []
                for dlt, mask in ((-1, self.maskP), (0, None), (1, self.maskN)):
                    k0 = q0 + dlt * 128
                    if k0 < 0 or k0 >= c.T_S:
                        continue
                    ko = k0 - klo
                    tiles.append((lambda kvh, ko=ko: kT[:, kvh, ko:ko + 128], vT[:, ko // 128, :],
                                  lambda h, qb=qb: qrT[:, h, qb * 128:(qb + 1) * 128], mask, ["kT", "vT", "qrT"]))
                for j in range(2):
                    tiles.append((lambda kvh, j=j: kcT[:, kvh, j * 128:(j + 1) * 128], vc[:, j, :],
                                  lambda h, qb=qb: qT[:, h, qb * 128:(qb + 1) * 128], None, ["kcT", "vc", "qT"]))
                self.attn_a_qblock(tiles, slice(q0, q0 + 128))
                yield
        for s in range(c.NSEQ):
            b0 = c.T_S + s * 256
            self.load(qT[:, :, 0:256], AQT3[:, :, b0:b0 + 256], reads=[("AQT", "all")], writes=["qT"])
            self.load(kT[:, :, 0:256], AKT3[:, :, b0:b0 + 256], reads=[("AKT", "all")], writes=["kT"])
            self.load(vT[:, 0:2, :], d["AV"][b0:b0 + 256, :].rearrange("(j p) e -> p j e", p=128), reads=[("AV", "all")], writes=["vT"])
            for qb in range(2):
                tiles = []
                for j in range(2):
                    tiles.append((lambda kvh, j=j: kT[:, kvh, j * 128:(j + 1) * 128], vT[:, j, :],
                                  lambda h, qb=qb: qT[:, h, qb * 128:(qb + 1) * 128], None, ["kT", "vT", "qT"]))
                self.attn_a_qblock(tiles, slice(b0 + qb * 128, b0 + (qb + 1) * 128))
                yield

    def mla_head(self, h, qsegs, ktiles, kT, vT, rk):
        d = self.dram
        for (qfn, n, dsl) in qsegs:
            pso, pok = self.accp.next()
            psd, pdk = self.accp.next()
            nt = len(ktiles)
            pend = None
            for j, (ko, vj, src) in enumerate(ktiles):
                ps, pk = self.sc.next()
                self.mm(ps[:, 0:n], kT[:, ko:ko + 128], qfn(src), True, True, rk, [pk])
                pt, ptk = self.ptr.next()
                self.act(pt[:, 0:n], ps[:, 0:n], AF.Exp, [pk], [ptk], scale=B_SCALE)
                if pend is not None:
                    jj, ppt, pptk, pvj = pend
                    self.mm(pso[0:64, 0:n], vT[:, pvj, :], ppt[:, 0:n], jj == 0, jj == nt - 1, [pptk] + rk, [pok])
                    self.mm(psd[0:64, 0:n], self.onesb[:, 0:64], ppt[:, 0:n], jj == 0, jj == nt - 1, [pptk, "onesb"], [pdk])
                pend = (j, pt, ptk, vj)
                yield
            jj, ppt, pptk, pvj = pend
            self.mm(pso[0:64, 0:n], vT[:, pvj, :], ppt[:, 0:n], jj == 0, jj == nt - 1, [pptk] + rk, [pok])
            self.mm(psd[0:64, 0:n], self.onesb[:, 0:64], ppt[:, 0:n], jj == 0, jj == nt - 1, [pptk, "onesb"], [pdk])
            den, dk = self.f32r.next()
            self.dve(lambda e, den=den, psd=psd, n=n: e.reciprocal(out=den[0:64, 0:n], in_=psd[0:64, 0:n]), [pdk], [dk])
            o, ok = self.bf16r.next()
            self.dve(lambda e, o=o, den=den, pso=pso, n=n: e.tensor_tensor(out=o[0:64, 0:n], in0=pso[0:64, 0:n], in1=den[0:64, 0:n], op=ALU.mult), [pok, dk], [ok])
            self.store(d["BRT"][1, h * 64:(h + 1) * 64, dsl], o[0:64, 0:n], reads=[ok], writes=[("BRT", 1)])

    def mixer_b(self, l):
        c, d = self.cfg, self.dram
        aB = self.arenaB
        T_S, NT = c.T_S, c.NT
        NKT = T_S // 128 + 2
        qL = aB[0:96, 0:T_S]
        qC = aB[0:96, 4096:4096 + T_S]
        kT = aB[0:96, 8192:8192 + T_S + 256]
        vT = aB[:, 12544:12544 + NKT * 64].rearrange("p (j e) -> p j e", e=64)
        base = 12544 + 34 * 64
        self.ptr = ARing("ptr", [aB[:, base + i * 512:base + (i + 1) * 512] for i in range(6)])
        rk = ["qL", "qC", "kT", "vT"]
        for h in range(4):
            self.load(qL, d["QLT"][h], reads=[("QLT", "all")], writes=["qL"])
            self.load(qC, d["QCT"][h, :, 0:T_S], reads=[("QCT", "all")], writes=["qC"])
            self.load(kT[:, 0:T_S], d["BKT"][h, :, 0:T_S], reads=[("BKT", "all")], writes=["kT"])
            self.load(kT[:, T_S:T_S + 256], d["BKT"][h, :, NT:NT + 256], reads=[("BKT", "ctx")], writes=["kT"])
            self.load(vT[:, 0:T_S // 128, :], d["BV"][0:T_S, h * 64:(h + 1) * 64].rearrange("(j p) e -> p j e", p=128), reads=[("BV", "all")], writes=["vT"])
            self.load(vT[:, T_S // 128:NKT, :], d["BV"][NT:NT + 256, h * 64:(h + 1) * 64].rearrange("(j p) e -> p j e", p=128), reads=[("BV", "ctx")], writes=["vT"])
            ktiles = [(j * 128, j, "lat") for j in range(T_S // 128)] + [(T_S + j * 128, T_S // 128 + j, "ctx") for j in range(2)]
            qsegs = []
            for q0 in range(0, T_S, 512):
                qsegs.append((lambda src, q0=q0: (qL if src == "lat" else qC)[:, q0:q0 + 512], 512, slice(q0, q0 + 512)))
            yield from self.mla_head(h, qsegs, ktiles, kT, vT, rk)
            self.load(qC[:, 0:c.T_P], d["QCT"][h, :, T_S:NT], reads=[("QCT", "all")], writes=["qC"])
            self.load(kT[:, 0:c.T_P], d["BKT"][h, :, T_S:NT], reads=[("BKT", "all")], writes=["kT"])
            self.load(vT[:, 0:c.T_P // 128, :], d["BV"][T_S:NT, h * 64:(h + 1) * 64].rearrange("(j p) e -> p j e", p=128), reads=[("BV", "all")], writes=["vT"])
            for s in range(c.NSEQ):
                ktiles = [(s * 256 + j * 128, s * 2 + j, "ctx") for j in range(2)]
                qsegs = [(lambda src, s=s: qC[:, s * 256:(s + 1) * 256], 256, slice(T_S + s * 256, T_S + (s + 1) * 256))]
                yield from self.mla_head(h, qsegs, ktiles, kT, vT, rk)


class BuilderD(BuilderM):
    def mixer_d1(self, l):
        c, d = self.cfg, self.dram
        aF = self.arenaF
        LX = 516
        xin = aF[:, 0:6 * LX].rearrange("p (c t) -> p c t", c=6)
        u = aF[:, 3096:3096 + 3072].rearrange("p (c t) -> p c t", c=6)
        sqf = aF[:, 6168:6168 + 512]
        rs = aF[:, 6680:6680 + 512]
        tm = ARing("tm", [aF[:, 7192 + i * 512:7192 + (i + 1) * 512] for i in range(2)])
        vec = self.vecs[l]
        blk = self.cst[:, C_BLK:C_BLK + 128]
        ident = self.cst[:, C_ID:C_ID + 128]
        DT3 = d["DTs"].rearrange("(c p) t -> p c t", p=128)
        segs = [(0, c.T_S, s0, 512) for s0 in range(0, c.T_S, 512)] + [(c.T_S + s * 256, 256, 0, 256) for s in range(c.NSEQ)]
        for (base, T, s0, n) in segs:
            lo, hi = max(s0 - 2, 0), min(s0 + n + 2, T)
            self.dve(lambda e: e.memset(xin[:], 0.0), [], ["xin"])
            self.load(xin[:, :, lo - (s0 - 2):hi - (s0 - 2)], DT3[:, :, base + lo:base + hi], writes=["xin"])
            for ch in range(6):
                w = lambda tap, ch=ch: vec[:, V_CONV + tap * 6 + ch:V_CONV + tap * 6 + ch + 1]
                self.dve(lambda e, ch=ch, n=n, w=w: e.tensor_scalar(out=u[:, ch, 0:n], in0=xin[:, ch, 0:n], scalar1=w(0), scalar2=None, op0=ALU.mult), ["xin", f"vecs{l}"], [("u", ch)])
                for tap in range(1, 5):
                    self.dve(lambda e, ch=ch, n=n, w=w, tap=tap: e.scalar_tensor_tensor(out=u[:, ch, 0:n], in0=xin[:, ch, tap:tap + n], scalar=w(tap), in1=u[:, ch, 0:n], op0=ALU.mult, op1=ALU.add),
                             ["xin", f"vecs{l}", ("u", ch)], [("u", ch)])
                self.act(u[:, ch, 0:n], u[:, ch, 0:n], AF.Silu, [("u", ch)], [("u", ch)])
                if ch < 4:
                    self.dve(lambda e, ch=ch, n=n: e.tensor_tensor(out=sqf[:, 0:n], in0=u[:, ch, 0:n], in1=u[:, ch, 0:n], op=ALU.mult), [("u", ch)], ["sqf"])
                    ps, pk = self.sc.next()
                    self.mm(ps[:, 0:n], blk, sqf[:, 0:n], True, True, ["sqf", "cst"], [pk])
                    self.act(rs[:, 0:n], ps[:, 0:n], AF.Sqrt, [pk], ["rs"], bias=self.eps_ap, scale=1.0)
                    self.dve(lambda e, n=n: e.reciprocal(out=rs[:, 0:n], in_=rs[:, 0:n]), ["rs"], ["rs"])
                    self.dve(lambda e, ch=ch, n=n: e.scalar_tensor_tensor(out=u[:, ch, 0:n], in0=u[:, ch, 0:n], scalar=(0.125 if ch < 2 else 1.0), in1=rs[:, 0:n], op0=ALU.mult, op1=ALU.mult),
                             [("u", ch), "rs"], [("u", ch)])
                yield
            tsl = slice(base + s0, base + s0 + n)
            self.store(d["QNT"].rearrange("(c p) t -> p c t", p=128)[:, :, tsl], u[:, 0:2, 0:n], reads=[("u", 0), ("u", 1)], writes=[("QNT", 0)])
            self.store(d["KNT"].rearrange("(c p) t -> p c t", p=128)[:, :, tsl], u[:, 2:4, 0:n], reads=[("u", 2), ("u", 3)], writes=[("KNT", 0)])
            for tt in range(n // 128):
                t, tk = tm.next()
                for j, ch in enumerate((2, 3, 4, 5)):
                    ps, pk = self.sc.next()
                    self.tr(ps[:, 0:128], u[:, ch, tt * 128:(tt + 1) * 128], ident, [("u", ch), "cst"], [pk])
                    self.act(t[:, j * 128:(j + 1) * 128], ps[:, 0:128], AF.Copy, [pk], [tk])
                g0 = base + s0 + tt * 128
                self.store(d["KN"][g0:g0 + 128, :], t[:, 0:256], reads=[tk], writes=[("KN", 0)])
                self.store(d["VV"][g0:g0 + 128, :], t[:, 256:512], reads=[tk], writes=[("VV", 0)])
                yield

    def mixer_d2(self, l):
        c, d = self.cfg, self.dram
        aF = self.arenaF
        cst = self.cst
        LI, LS, UI, US, ONE, ident = (cst[:, o:o + 128] for o in (C_LI, C_LS, C_UI, C_US, C_ONE, C_ID))
        rb = self.rowsb[l]
        off = [0]

        def carve(ncols, parts=128):
            a = aF[0:parts, off[0]:off[0] + ncols]
            off[0] += ncols
            return a
        S = [[carve(64, 64) for h in range(4)] for dr in range(2)]
        xflat = self.xT[:].rearrange("p k t -> p (k t)")
        xo = [0]

        def carve_x(ncols, parts=128):
            a = xflat[0:parts, xo[0]:xo[0] + ncols]
            xo[0] += ncols
            return a
        qk = ARing("qk", [carve_x(1024, 64).rearrange("p (h s t) -> p h s t", h=4, s=2) for _ in range(2)])
        knv = ARing("knv", [carve_x(512) for _ in range(2)])
        dz = ARing("dz", [carve_x(16) for _ in range(2)])
        gs = ARing("gs", [carve_x(64) for _ in range(2)])
        ealog = carve_x(8)
        O = ARing("O", [carve_x(256) for _ in range(2)])
        assert xo[0] <= 4096, xo[0]
        RU = []
        for u_ in range(4):
            RU.append(dict(
                gL=ARing(f"gL{u_}", [carve(128)]), ET=ARing(f"ET{u_}", [carve(128)]), tA=ARing(f"tA{u_}", [carve(128) for _ in range(2)]),
                Pm=ARing(f"Pm{u_}", [carve(128) for _ in range(6)]), Yr=ARing(f"Yr{u_}", [carve(128) for _ in range(6)]),
                AqT=ARing(f"AqT{u_}", [carve(128)]), kd=ARing(f"kd{u_}", [carve(64)]), XU=ARing(f"XU{u_}", [carve(128)]),
                wT=ARing(f"wT{u_}", [carve(128, 64)]), vn=ARing(f"vn{u_}", [carve(64)]), qs=ARing(f"qs{u_}", [carve(64)])))
        assert off[0] <= 11520, off[0]
        self.act(ealog[:], rb[:, R_ALOG:R_ALOG + 8], AF.Exp, [f"rowsb{l}"], ["ealog"])

        seqs = [(0, c.T_S, True, 0)] + [(c.T_S + s * 256, 256, False, s) for s in range(c.NSEQ)]
        QN4 = d["QNT"].rearrange("(h dd) t -> dd h t", dd=64)
        KN4 = d["KNT"].rearrange("(h dd) t -> dd h t", dd=64)
        for (base, T, sample, sidx) in seqs:
            N = T // 128
            for dr in range(2):
                for h in range(4):
                    if sample:
                        self.load(S[dr][h], d["sdf" if dr == 0 else "sdb"][l, h], writes=[("S", dr, h)])
                    else:
                        self.dve(lambda e, dr=dr, h=h: e.memset(S[dr][h], 0.0), [], [("S", dr, h)])
            for i in range(N):
                for dr in range(2):
                    ci = i if dr == 0 else N - 1 - i
                    g0 = base + ci * 128
                    q, qkk = qk.next()
                    self.load(q[:, :, 0, :], QN4[:, :, g0:g0 + 128], reads=[("QNT", 0)], writes=[qkk])
                    self.load(q[:, :, 1, :], KN4[:, :, g0:g0 + 128], reads=[("KNT", 0)], writes=[qkk])
                    kv, kvk = knv.next()
                    self.load(kv[:, 0:256], d["KN"][g0:g0 + 128, :], reads=[("KN", 0)], writes=[kvk])
                    self.load(kv[:, 256:512], d["VV"][g0:g0 + 128, :], reads=[("VV", 0)], writes=[kvk])
                    z, zk = dz.next()
                    self.load(z[:], d["DZ"][g0:g0 + 128, 256:272], reads=[("DZ", 0)], writes=[zk])
                    g, gk = gs.next()
                    c0, c1 = dr * 4, dr * 4 + 4
                    self.act(g[:, 8:12], z[:, c0:c1], AF.Sigmoid, [zk], [gk])
                    self.dve(lambda e, g=g: e.tensor_scalar(out=g[:, 16:20], in0=g[:, 8:12], scalar1=-1.0, scalar2=None, op0=ALU.mult), [gk], [gk])
                    self.dve(lambda e, g=g, z=z, c0=c0, c1=c1: e.tensor_tensor(out=g[:, 48:52], in0=z[:, 8 + c0:8 + c1], in1=rb[:, R_DTB + c0:R_DTB + c1], op=ALU.add), [zk, f"rowsb{l}"], [gk])
                    self.act(g[:, 48:52], g[:, 48:52], AF.Exp, [gk], [gk])
                    self.act(g[:, 48:52], g[:, 48:52], AF.Ln, [gk], [gk], bias=1.0, scale=1.0)
                    self.dve(lambda e, g=g, c0=c0, c1=c1: e.scalar_tensor_tensor(out=g[:, 0:4], in0=g[:, 48:52], scalar=-1.0, in1=ealog[:, c0:c1], op0=ALU.mult, op1=ALU.mult), [gk, "ealog"], [gk])
                    ps, pk = self.sc.next()
                    self.mm(ps[:, 0:4], LI if dr == 0 else UI, g[:, 0:4], True, True, [gk, "cst"], [pk])
                    self.mm(ps[:, 4:8], ONE, g[:, 0:4], True, True, [gk, "cst"], [pk])
                    self.act(g[:, 24:28], ps[:, 0:4], AF.Exp, [pk], [gk])
                    self.act(g[:, 40:44], ps[:, 4:8], AF.Exp, [pk], [gk])
                    self.act(g[:, 52:60], ps[:, 0:8], AF.Copy, [pk], [gk])
                    self.dve(lambda e, g=g: e.tensor_tensor(out=g[:, 32:36], in0=g[:, 56:60], in1=g[:, 52:56], op=ALU.subtract), [gk], [gk])
                    self.act(g[:, 32:36], g[:, 32:36], AF.Exp, [gk], [gk])
                    otile, ok_ = O.next()
                    gens = [self.d2_unit(l, dr, h, q, qkk, kv, kvk, g, gk, S[dr][h], ("S", dr, h), otile, ok_, RU[h], (LI, LS, UI, US, ident)) for h in range(4)]
                    while gens:
                        alive = []
                        for gen in gens:
                            try:
                                next(gen)
                                alive.append(gen)
                            except StopIteration:
                                pass
                        gens = alive
                    self.store(d["OF" if dr == 0 else "OB"][g0:g0 + 128, :], otile[:], reads=[ok_], writes=[("OFB", dr)])
            if not sample:
                for dr in range(2):
                    for h in range(4):
                        self.store(d["o_sf" if dr == 0 else "o_sb"][sidx, l, h], S[dr][h], reads=[("S", dr, h)], is_output=True)

    def d2_unit(self, l, dr, h, q, qkk, kv, kvk, g, gk, S, Sk, otile, ok_, R, consts):
        LI, LS, UI, US, ident = consts
        qT, kT = q[:, h, 0, :], q[:, h, 1, :]
        kn, v = kv[:, h * 64:(h + 1) * 64], kv[:, 256 + h * 64:256 + (h + 1) * 64]
        col = lambda o: g[:, o + h:o + h + 1]
        graw, beta, nbeta, eG, edG, etot = col(0), col(8), col(16), col(24), col(32), col(40)
        m_strict = LS if dr == 0 else US
        m_incl = LI if dr == 0 else UI
        gl, glk = R["gL"].next()
        self.dve(lambda e: e.tensor_scalar(out=gl, in0=(LI if dr == 0 else UI), scalar1=graw, scalar2=None, op0=ALU.mult), [gk, "cst"], [glk])
        ps, pk = self.sc.next()
        self.mm(ps[:, 0:128], US if dr == 0 else LS, gl, True, True, [glk, "cst"], [pk])
        et, etk = R["ET"].next()
        self.act(et, ps[:, 0:128], AF.Exp, [pk], [etk])
        yield
        psk, pkk = self.sc.next()
        self.mm(psk[:, 0:128], kT, kT, True, True, [qkk], [pkk])
        self.mm(psk[:, 128:256], kT, qT, True, True, [qkk], [pkk])
        ta, tak = R["tA"].next()
        self.dve(lambda e: e.tensor_tensor(out=ta, in0=psk[:, 0:128], in1=et, op=ALU.mult), [pkk, etk], [tak])
        p0t, p0tk = R["Yr"].tiles[5], (R["Yr"].name, 5)
        self.dve(lambda e: e.scalar_tensor_tensor(out=p0t, in0=ta, scalar=nbeta, in1=m_strict, op0=ALU.mult, op1=ALU.mult), [tak, gk, "cst"], [p0tk])
        aq, aqk = R["AqT"].next()
        self.dve(lambda e: e.tensor_tensor(out=aq, in0=psk[:, 128:256], in1=et, op=ALU.mult), [pkk, etk], [aqk])
        self.dve(lambda e: e.tensor_tensor(out=aq, in0=aq, in1=m_incl, op=ALU.mult), [aqk, "cst"], [aqk])
        yield
        blk = self.cst[:, C_BLK:C_BLK + 128]
        ps, pk = self.sc.next()
        self.tr(ps[:, 0:128], p0t, ident, [p0tk, "cst"], [pk])
        p0f, p0fk = R["Pm"].next()
        self.act(p0f, ps[:, 0:128], AF.Copy, [pk], [p0fk])
        yield
        p0tb, p0tbk = R["Pm"].next()
        self.dve(lambda e: e.tensor_tensor(out=p0tb, in0=p0t, in1=blk, op=ALU.mult), [p0tk, "cst"], [p0tbk])
        p0b, p0bk = R["Pm"].next()
        self.dve(lambda e: e.tensor_tensor(out=p0b, in0=p0f, in1=blk, op=ALU.mult), [p0fk, "cst"], [p0bk])
        yt = R["Yr"].tiles
        nm = R["Yr"].name
        noff, noffk = yt[0], (nm, 0)
        rp, rpk = yt[1], (nm, 1)
        self.dve(lambda e: e.tensor_tensor(out=noff, in0=p0f, in1=p0b, op=ALU.subtract), [p0fk, p0bk], [noffk])
        self.act(rp[:, 0:64], v, AF.Copy, [kvk], [rpk])
        self.dve(lambda e: e.tensor_scalar(out=rp[:, 64:128], in0=kn, scalar1=eG, scalar2=None, op0=ALU.mult), [kvk, gk], [rpk])
        tcur, tck = yt[2], (nm, 2)
        self.dve(lambda e: e.tensor_tensor(out=yt[2], in0=p0tb, in1=ident, op=ALU.add), [p0tbk, "cst"], [(nm, 2)])
        yield
        pt_, ptk_, p_, pk_ = p0tb, p0tbk, p0b, p0bk
        for lev in range(5):
            ps2, pk2 = self.sc.next()
            self.mm(ps2[:, 0:128], pt_, p_, True, True, [pk_, ptk_], [pk2])
            np_, npk = R["Pm"].next()
            self.act(np_, ps2[:, 0:128], AF.Copy, [pk2], [npk])
            if lev < 4:
                ps1, pk1 = self.sc.next()
                self.mm(ps1[:, 0:128], p_, pt_, True, True, [pk_, ptk_], [pk1])
                npt, nptk = R["Pm"].next()
                self.dve(lambda e, npt=npt, ps1=ps1: e.tensor_copy(out=npt, in_=ps1[:, 0:128]), [pk1], [nptk])
                pt_, ptk_ = npt, nptk
            p_, pk_ = np_, npk
            yield
            psa, pka = self.sc.next()
            self.mm(psa[:, 0:128], p_, tcur, True, True, [pk_, tck], [pka])
            ni = 3 if tck[1] == 2 else 2
            tnew, tnk = yt[ni], (nm, ni)
            self.dve(lambda e, tcur=tcur, tnew=tnew, psa=psa: e.tensor_tensor(out=tnew, in0=tcur, in1=psa[:, 0:128], op=ALU.add), [tck, pka], [tnk])
            tcur, tck = tnew, tnk
            yield
        psy, pyk = self.sc.next()
        self.mm(psy[:, 0:128], tcur, rp, True, True, [tck, rpk], [pyk])
        ysb, ysk = yt[4], (nm, 4)
        self.act(ysb, psy[:, 0:128], AF.Copy, [pyk], [ysk])
        psq, pqk = self.sc.next()
        self.mm(psq[:, 0:128], noff, tcur, True, True, [noffk, tck], [pqk])
        mqT, mqk = R["Pm"].next()
        self.dve(lambda e: e.tensor_copy(out=mqT, in_=psq[:, 0:128]), [pqk], [mqk])
        yield
        psx_, pxk = self.sc.next()
        self.mm(psx_[:, 0:128], mqT, ysb, True, True, [mqk, ysk], [pxk])
        xs, xsk = R["tA"].next()
        self.dve(lambda e: e.tensor_tensor(out=xs, in0=ysb, in1=psx_[:, 0:128], op=ALU.add), [ysk, pxk], [xsk])
        yield
        psr, prk = self.sc.next()
        self.mm(psr[:, 0:128], p0t, xs, True, True, [p0tk, xsk], [prk])
        yield
        rr, rrk = yt[2], (nm, 2)
        if tck[1] == 2:
            rr, rrk = yt[3], (nm, 3)
        self.dve(lambda e: e.tensor_tensor(out=rr, in0=rp, in1=xs, op=ALU.subtract), [rpk, xsk], [rrk])
        self.dve(lambda e: e.tensor_tensor(out=rr, in0=rr, in1=psr[:, 0:128], op=ALU.add), [rrk, prk], [rrk])
        psy2, py2k = self.sc.next()
        self.mm(psy2[:, 0:128], tcur, rr, True, True, [tck, rrk], [py2k])
        self.act(ysb, psy2[:, 0:128], AF.Copy, [py2k], [ysk])
        yield
        psx2, px2k = self.sc.next()
        self.mm(psx2[:, 0:128], mqT, ysb, True, True, [mqk, ysk], [px2k])
        self.dve(lambda e: e.tensor_tensor(out=xs, in0=xs, in1=ysb, op=ALU.add), [xsk, ysk], [xsk])
        self.dve(lambda e: e.tensor_tensor(out=xs, in0=xs, in1=psx2[:, 0:128], op=ALU.add), [xsk, px2k], [xsk])
        yield
        y, yk = xs, xsk
        xu, xuk = R["XU"].next()
        self.dve(lambda e: e.tensor_scalar(out=xu, in0=y, scalar1=beta, scalar2=None, op0=ALU.mult), [yk, gk], [xuk])
        ps, pk = self.sc.next()
        self.tr(ps[0:64, 0:128], xu[:, 64:128], ident, [xuk, "cst"], [pk])
        wt, wtk = R["wT"].next()
        self.act(wt, ps[0:64, 0:128], AF.Copy, [pk], [wtk])
        yield
        kdt, kdk = R["kd"].next()
        self.dve(lambda e: e.tensor_scalar(out=kdt, in0=kn, scalar1=edG, scalar2=None, op0=ALU.mult), [kvk, gk], [kdk])
        ps, pk = self.sc.next()
        self.mm(ps[:, 0:64], wt, S, True, True, [wtk, Sk], [pk])
        self.mm(ps[:, 64:128], qT, S, True, True, [qkk, Sk], [pk])
        vnt, vnk = R["vn"].next()
        self.dve(lambda e: e.tensor_tensor(out=vnt, in0=xu[:, 0:64], in1=ps[:, 0:64], op=ALU.subtract), [xuk, pk], [vnk])
        qst, qsk = R["qs"].next()
        self.dve(lambda e: e.tensor_scalar(out=qst, in0=ps[:, 64:128], scalar1=eG, scalar2=None, op0=ALU.mult), [pk, gk], [qsk])
        yield
        ps2, pk2 = self.sc.next()
        self.mm(ps2[:, 0:64], aq, vnt, True, True, [aqk, vnk], [pk2])
        self.mm(ps2[0:64, 64:128], kdt, vnt, True, True, [kdk, vnk], [pk2])
        self.dve(lambda e: e.tensor_tensor(out=otile[:, h * 64:(h + 1) * 64], in0=qst, in1=ps2[:, 0:64], op=ALU.add), [qsk, pk2], [ok_])
        self.dve(lambda e: e.scalar_tensor_tensor(out=S, in0=S, scalar=g[0:64, 40 + h:41 + h], in1=ps2[0:64, 64:128], op0=ALU.mult, op1=ALU.add), [Sk, gk, pk2], [Sk])

    def mixer_d3(self, l):
        c, d = self.cfg, self.dram
        aF = self.arenaF
        rb = self.rowsb[l]
        ident = self.cst[:, C_ID:C_ID + 128]
        ofr = ARing("of", [aF[:, i * 256:(i + 1) * 256] for i in range(2)])
        obr = ARing("ob", [aF[:, 512 + i * 256:512 + (i + 1) * 256] for i in range(2)])
        zr = ARing("z", [aF[:, 1024 + i * 256:1024 + (i + 1) * 256] for i in range(2)])
        sqr = ARing("sqd", [aF[:, 1536 + i * 256:1536 + (i + 1) * 256] for i in range(2)])
        onr = ARing("on", [aF[:, 2048 + i * 256:2048 + (i + 1) * 256] for i in range(2)])
        for tt in range(c.NT // 128):
            g0 = tt * 128
            of, ofk = ofr.next(); ob, obk = obr.next(); z, zk = zr.next(); sq, sqk = sqr.next(); on, onk = onr.next()
            self.load(of, d["OF"][g0:g0 + 128, :], reads=[("OFB", 0)], writes=[ofk])
            self.load(ob, d["OB"][g0:g0 + 128, :], reads=[("OFB", 1)], writes=[obk])
            self.load(z, d["DZ"][g0:g0 + 128, 0:256], reads=[("DZ", 0)], writes=[zk])
            self.dve(lambda e, of=of, ob=ob: e.tensor_tensor(out=of, in0=of, in1=ob, op=ALU.add), [ofk, obk], [ofk])
            self.dve(lambda e, of=of, sq=sq: e.tensor_tensor(out=sq, in0=of, in1=of, op=ALU.mult), [ofk], [sqk])
            sm, smk = self.small.next()
            self.dve(lambda e, sm=sm, sq=sq: e.tensor_reduce(out=sm[:, 0:4], in_=sq.rearrange("p (h e) -> p h e", h=4), axis=AX.X, op=ALU.add), [sqk], [smk])
            self.act(sm[:, 4:8], sm[:, 0:4], AF.Sqrt, [smk], [smk], bias=self.eps_ap, scale=1.0 / 64)
            self.dve(lambda e, sm=sm: e.reciprocal(out=sm[:, 4:8], in_=sm[:, 4:8]), [smk], [smk])
            self.act(z, z, AF.Silu, [zk], [zk])
            for h in range(4):
                hs = slice(h * 64, (h + 1) * 64)
                self.dve(lambda e, of=of, on=on, sm=sm, hs=hs, h=h: e.scalar_tensor_tensor(out=on[:, hs], in0=of[:, hs], scalar=sm[:, 4 + h:5 + h], in1=rb[:, R_GNORM:R_GNORM + 64], op0=ALU.mult, op1=ALU.mult),
                         [ofk, smk, f"rowsb{l}"], [onk])
            self.dve(lambda e, on=on, z=z: e.tensor_tensor(out=on, in0=on, in1=z, op=ALU.mult), [onk, zk], [onk])
            for cc in range(2):
                ps, pk = self.sc.next()
                self.tr(ps[:, 0:128], on[:, cc * 128:(cc + 1) * 128], ident, [onk, "cst"], [pk])
                o, ok = self.bf16r.next()
                self.act(o[:, 0:128], ps[:, 0:128], AF.Copy, [pk], [ok])
                self.store(d["BRT"][3, cc * 128:(cc + 1) * 128, g0:g0 + 128], o[:, 0:128], reads=[ok], writes=[("BRT", 3)])


def drive(*gens):
    gens = list(gens)
    while gens:
        alive = []
        for g in gens:
            try:
                next(g)
                alive.append(g)
            except StopIteration:
                pass
        gens = alive


def build_program(cfg, debug=()):
    B = BuilderD(cfg, debug=debug)
    B.setup_eps()
    B.phase_init()
    B.P.barrier()
    B.phase_w(0)
    B.mixer_rings()
    for l in range(2):
        B.layer_setup(l)
        B.phase_ctx(l)
        for bi in range(cfg.NB):
            B.phase_a_block(l, bi)
        if l == 0:
            B.phase_w(1)
        B.P.barrier()
        B.mixer_setup(l)
        B.P.barrier()
        drive(B.mixer_c(l), B.mixer_a(l))
        B.P.barrier()
        drive(B.mixer_b(l), B.mixer_d1(l))
        B.P.barrier()
        B.mixer_d2(l)
        B.P.barrier()
        B.mixer_d3(l)
        B.P.barrier()
        for bi in range(cfg.NB):
            B.phase_c_block(l, bi, last=(l == 1))
        B.P.barrier()
    B.P.finish()
    return B


def run_cfg(cfg, inputs, n_cores=8):
    inp = {k: np.ascontiguousarray(np.asarray(v)) for k, v in inputs.items()}
    B = build_program(cfg)
    shared = host_prep_shared(cfg, inp)
    nb = inp["x_sample"].shape[0]
    in_maps = []
    for core in range(n_cores):
        m = dict(shared)
        b = (core * nb) // n_cores
        seqs = list(range(core * cfg.NSEQ, (core + 1) * cfg.NSEQ))
        m.update(host_prep(cfg, inp, b, seqs))
        in_maps.append(m)
    res = run_bass_kernel_spmd(B.nc, in_maps, core_ids=list(range(n_cores)))
    R = res.results
    f32 = np.float32
    y_prompt = np.concatenate([np.asarray(R[c]["yp"], f32).reshape(cfg.NSEQ, 256, D) for c in range(n_cores)], 0)
    per_b = n_cores // nb
    y_sample = np.stack([np.asarray(R[b * per_b]["ys"], f32) for b in range(nb)], 0)
    cat = lambda k, shp: np.concatenate([np.asarray(R[c][k], f32).reshape((cfg.NSEQ,) + shp) for c in range(n_cores)], 0)
    return (y_prompt, y_sample, cat("o_ak", (2, 256, 2, 64)), cat("o_av", (2, 256, 2, 64)), cat("o_ckv", (2, 256, 128)),
            cat("o_kr", (2, 256, 32)), cat("o_sf", (2, 4, 64, 64)), cat("o_sb", (2, 4, 64, 64)))


def kernel(**inputs):
    cfg = Cfg(T_S=4096, NSEQ=4)
    return run_cfg(cfg, inputs)
```

```python
import os
import numpy as np
import concourse.bass as bass
import concourse.mybir as mybir
from concourse.bass_utils import run_bass_kernel_spmd
from contextlib import ExitStack


F32 = mybir.dt.float32
BF16 = mybir.dt.bfloat16
I32 = mybir.dt.int32
AF = mybir.ActivationFunctionType
ALU = mybir.AluOpType
AX = mybir.AxisListType

ENGS = ("pe", "act", "dve", "pool", "sp")
SEM_ROT = 4000


class Prog:
    def __init__(self, nc, n_dma_sems=12):
        self.nc = nc
        self.es = ExitStack()
        self.ops = {e: [] for e in ENGS}
        self.cnt = {e: 0 for e in ENGS}
        self.cur_sem = {}
        self.seen = {e: {} for e in ENGS}
        self.res_w = {}
        self.res_r = {}
        self.sem_count = 0
        for e in ENGS:
            self.cur_sem[e] = self._new_sem()
        self.dma_pool = {}
        for e in ("sp", "pool", "act"):
            self.dma_pool[e] = [[self._new_sem(), 0] for _ in range(n_dma_sems)]
        self.dma_rr = {e: 0 for e in ("sp", "pool", "act")}
        self.out_events = []
        self._uid = 0

    def _new_sem(self):
        self.sem_count += 1
        return self.es.enter_context(self.nc.semaphore(f"s{self.sem_count}"))

    def sbuf(self, name, shape, dt):
        return self.es.enter_context(self.nc.sbuf_tensor(name, list(shape), dt))

    def psum(self, name, shape, dt=F32):
        return self.es.enter_context(self.nc.psum_tensor(name, list(shape), dt))

    def _need(self, eng, ev):
        if ev is None:
            return
        sem, val, _ = ev
        k = id(sem)
        if self.seen[eng].get(k, 0) >= val:
            return
        self.seen[eng][k] = val
        self.ops[eng].append(("wait", sem, val))

    def _deps(self, eng, reads, writes, pe_accum=False):
        for r in reads:
            self._need(eng, self.res_w.get(r))
        for w in writes:
            lw = self.res_w.get(w)
            if not (pe_accum and lw is not None and lw[2] == "pe" and eng == "pe"):
                self._need(eng, lw)
            for ev in self.res_r.get(w, ()):
                self._need(eng, ev)

    def _commit(self, ev, reads, writes):
        for r in reads:
            self.res_r.setdefault(r, []).append(ev)
            if len(self.res_r[r]) > 24:
                best = {}
                for s, v, e in self.res_r[r]:
                    if id(s) not in best or best[id(s)][1] < v:
                        best[id(s)] = (s, v, e)
                self.res_r[r] = list(best.values())
        for w in writes:
            self.res_w[w] = ev
            self.res_r[w] = []

    def op(self, eng, fn, reads=(), writes=(), pe_accum=False):
        self._deps(eng, reads, writes, pe_accum)
        if self.cnt[eng] >= SEM_ROT:
            self.cur_sem[eng] = self._new_sem()
            self.cnt[eng] = 0
        self.cnt[eng] += 1
        ev = (self.cur_sem[eng], self.cnt[eng], eng)
        self.ops[eng].append(("op", fn, ev[0], 1))
        self._commit(ev, reads, writes)
        return ev

    def dma(self, q, out, in_, reads=(), writes=(), is_output=False, **kw):
        pool = self.dma_pool[q]
        i = self.dma_rr[q]
        self.dma_rr[q] = (i + 1) % len(pool)
        slot = pool[i]
        sem = slot[0]
        if slot[1] > 0:
            self._need(q, (sem, slot[1], "dma"))
        self._deps(q, reads, writes)
        slot[1] += 16
        ev = (sem, slot[1], "dma")

        def fn(e, out=out, in_=in_, kw=kw):
            return e.dma_start(out=out, in_=in_, **kw)
        self.ops[q].append(("op", fn, sem, 16))
        self._commit(ev, reads, writes)
        if is_output:
            self.out_events.append(ev)
        return ev

    def barrier(self):
        evs = []
        for e in ENGS:
            if self.cnt[e] > 0:
                evs.append((self.cur_sem[e], self.cnt[e], e))
        for q in self.dma_pool:
            for sem, val in self.dma_pool[q]:
                if val > 0:
                    evs.append((sem, val, "dma"))
        for e in ENGS:
            for ev in evs:
                if ev[2] == e and ev[0] is self.cur_sem[e]:
                    continue
                self._need(e, ev)
        self.res_w.clear()
        self.res_r.clear()

    def finish(self):
        for ev in self.out_events:
            self._need("sp", ev)
        self.barrier()
        nc = self.nc
        emap = {"pe": "tensor", "act": "scalar", "dve": "vector", "pool": "gpsimd", "sp": "sync"}
        with nc.Block() as block:
            for e in ENGS:
                lst = self.ops[e]

                def body(h, lst=lst):
                    for item in lst:
                        if item[0] == "wait":
                            h.wait_ge(item[1], item[2])
                        else:
                            ins = item[1](h)
                            ins.then_inc(item[2], item[3])
                getattr(block, emap[e])(body)
        self.es.close()


D = 1024
NK = 8
TB = 512
EPS = 1e-6
D_FF = 2816
NKF = 22
CTX = 256

O_AQ, O_AK, O_AV, O_CQ, O_CKV, O_KR, O_C, O_DQKV, O_DZ, O_DB, O_DA, O_G = 0, 256, 384, 512, 704, 832, 864, 1120, 1888, 2144, 2152, 2160


def rope_perm(dim):
    q = dim // 4
    idx = np.arange(dim)
    perm = np.where((idx // q) % 2 == 0, idx + q, idx - q)
    sign = np.where((idx // q) % 2 == 0, -1.0, 1.0).astype(np.float32)
    return perm, sign


def rope_tables(T, dim):
    GRID_W = 64
    n_rows = T // GRID_W
    row = np.repeat(np.arange(n_rows), GRID_W).astype(np.float32)
    col = np.tile(np.arange(GRID_W), n_rows).astype(np.float32)
    nfreq = dim // 4
    inv = (np.float32(10000.0) ** (-np.arange(nfreq, dtype=np.float32) / np.float32(nfreq))).astype(np.float32)
    ang_r = row[:, None] * inv
    ang_c = col[:, None] * inv
    ang = np.concatenate([ang_r, ang_r, ang_c, ang_c], axis=-1).astype(np.float32)
    return np.cos(ang).astype(np.float32), np.sin(ang).astype(np.float32)


class Cfg:
    def __init__(self, T_S=4096, NSEQ=4):
        self.T_S, self.NSEQ = T_S, NSEQ
        self.T_P = NSEQ * 256
        self.NT = T_S + self.T_P
        self.NBS = T_S // TB
        self.NBP = self.T_P // TB
        self.NB = self.NBS + self.NBP


V_BMOD, V_GPRE1, V_GPOST1, V_GPRE2, V_GPOST2, V_GCQ, V_CSC, V_CONV = 0, 48, 56, 64, 72, 80, 82, 84
NV = 84 + 30
R_GCKV, R_GNORM, R_SINK, R_ALOG, R_DTB = 0, 128, 192, 196, 204
NR = 212
C_ID, C_LI, C_LS, C_UI, C_US, C_ONE, C_BLK = 0, 128, 256, 384, 512, 640, 768
NCONST = 896


def fm(v, nk):
    return np.ascontiguousarray(v.reshape(nk, 128).T)


def host_prep(cfg, inp, b_idx, seq_ids):
    f32 = np.float32
    m = {}
    m["xs"] = np.ascontiguousarray(inp["x_sample"][b_idx, :cfg.T_S])
    m["xp"] = np.ascontiguousarray(inp["x_prompt"][seq_ids].reshape(cfg.T_P, D))
    m["cak"] = np.ascontiguousarray(inp["cache_a_k"][b_idx].reshape(2, CTX, 128))
    m["cav"] = np.ascontiguousarray(inp["cache_a_v"][b_idx].reshape(2, CTX, 128))
    m["cckv"] = np.ascontiguousarray(inp["cache_b_ckv"][b_idx])
    m["ckr"] = np.ascontiguousarray(inp["cache_b_krope"][b_idx])
    m["sdf"] = np.ascontiguousarray(inp["state_d_fwd"][b_idx])
    m["sdb"] = np.ascontiguousarray(inp["state_d_bwd"][b_idx])
    c2 = np.stack([inp["c"][b_idx], inp["c_ctx"]], axis=1)
    m["c2T"] = np.ascontiguousarray(c2.reshape(NK, 128, 2).transpose(1, 0, 2))
    return m


def host_prep_shared(cfg, inp):
    f32 = np.float32
    m = {}
    w_in = inp["w_in"]
    pA, sA = rope_perm(64)
    pB, sB = rope_perm(32)
    aq = np.arange(O_AQ, O_AK)
    ak = np.arange(O_AK, O_AV)
    cq = np.arange(O_CQ, O_CKV)
    kr = np.arange(O_KR, O_C)
    cc = np.arange(O_C, O_DQKV)
    dq = np.arange(O_DQKV, O_DZ)
    aqp = (O_AQ + (np.arange(256) // 64) * 64 + pA[np.arange(256) % 64])
    akp = (O_AK + (np.arange(128) // 64) * 64 + pA[np.arange(128) % 64])
    krp = O_KR + pB
    cols = np.concatenate([aq, ak, cq[:128],
                           cq[128:], kr, krp, cc, dq[:128],
                           dq[128:640],
                           dq[640:], aqp, akp])
    assert cols.size == 2048
    m["wfm"] = np.ascontiguousarray(w_in[:, :, cols])
    tok = np.zeros((2, D, 1024), f32)
    tok[:, :, 0:128] = w_in[:, :, O_AK:O_AV]
    tok[:, :, 128:256] = w_in[:, :, O_AV:O_CQ]
    tok[:, :, 256:384] = w_in[:, :, O_CKV:O_KR]
    tok[:, :, 384:416] = w_in[:, :, O_KR:O_C]
    tok[:, :, 512:784] = w_in[:, :, O_DZ:O_G]
    m["wtok"] = tok
    m["wgate"] = np.ascontiguousarray(w_in[:, :, O_G:])
    m["wmod"] = inp["w_mod"]
    m["wbr"] = inp["w_br"]
    m["wo"] = inp["w_o"]
    ucols = []
    for n in range(11):
        for j in (2 * n, 2 * n + 1):
            ucols.append(np.arange(j * 128, (j + 1) * 128))
        for j in (2 * n, 2 * n + 1):
            ucols.append(D_FF + np.arange(j * 128, (j + 1) * 128))
    ucols = np.concatenate(ucols)
    m["wup"] = np.ascontiguousarray(inp["w_up"][:, :, ucols])
    m["wdown"] = inp["w_down"]
    wuq = inp["b_w_uq"]
    pcols = np.arange(384)
    h_, r_ = pcols // 96, pcols % 96
    pcols_p = np.where(r_ >= 64, h_ * 96 + 64 + pB[np.clip(r_ - 64, 0, 31)], pcols)
    wuq2 = np.zeros((2, 256, 768), f32)
    wuq2[:, :192, :384] = wuq
    wuq2[:, :192, 384:] = wuq[:, :, pcols_p]
    m["wuq2"] = wuq2
    wukv = inp["b_w_ukv"]
    ncols = np.concatenate([h * 128 + np.arange(64) for h in range(4)])
    vcols = np.concatenate([h * 128 + 64 + np.arange(64) for h in range(4)])
    m["wukv2"] = np.ascontiguousarray(np.concatenate([wukv[:, :, ncols], wukv[:, :, vcols]], axis=2))
    vecs = np.zeros((2, 128, NV), f32)
    rowsb = np.zeros((2, 128, NR), f32)
    for l in range(2):
        vecs[l, :, V_BMOD:V_BMOD + 48] = fm(inp["b_mod"][l], 48)
        vecs[l, :, V_GPRE1:V_GPRE1 + 8] = fm(inp["g_pre1"][l], 8)
        vecs[l, :, V_GPOST1:V_GPOST1 + 8] = fm(inp["g_post1"][l], 8)
        vecs[l, :, V_GPRE2:V_GPRE2 + 8] = fm(inp["g_pre2"][l], 8)
        vecs[l, :, V_GPOST2:V_GPOST2 + 8] = fm(inp["g_post2"][l], 8)
        gcq = np.zeros(256, f32)
        gcq[:192] = inp["b_g_cq"][l]
        vecs[l, :, V_GCQ:V_GCQ + 2] = fm(gcq, 2)
        vecs[l, :, V_CSC:V_CSC + 2] = fm(inp["c_scale"][l], 2)
        for tap in range(5):
            vecs[l, :, V_CONV + tap * 6:V_CONV + tap * 6 + 6] = fm(inp["d_conv"][l, tap], 6)
        rowsb[l, :, R_GCKV:R_GCKV + 128] = inp["b_g_ckv"][l][None, :]
        rowsb[l, :, R_GNORM:R_GNORM + 64] = inp["d_g_norm"][l][None, :]
        rowsb[l, :, R_SINK:R_SINK + 4] = inp["a_sink"][l][None, :]
        rowsb[l, :, R_ALOG:R_ALOG + 8] = inp["d_a_log"][l].reshape(8)[None, :]
        rowsb[l, :, R_DTB:R_DTB + 8] = inp["d_dt_bias"][l].reshape(8)[None, :]
    m["vecs"] = vecs
    m["rowsb"] = rowsb
    m["wpool"] = inp["c_w_pool"]
    cA, sA_ = rope_tables(cfg.T_S, 64)
    cB, sB_ = rope_tables(cfg.T_S, 32)
    ropeA = np.zeros((128, 2, cfg.T_S), f32)
    for p in range(128):
        ropeA[p, 0] = cA[:, p % 64]
        ropeA[p, 1] = sA_[:, p % 64] * sA[p % 64]
    ropeB = np.zeros((96, 2, cfg.T_S), f32)
    for p in range(32):
        for base in (0, 64):
            ropeB[base + p, 0] = cB[:, p]
            ropeB[base + p, 1] = sB_[:, p] * sB[p]
    m["ropeA"] = ropeA
    m["ropeB"] = ropeB

    def invcnt(T):
        t = np.arange(T)
        out = np.zeros((128, 2, T), f32)
        for g, w in enumerate((2, 4, 8, 16)):
            lo = np.clip(t - w // 2, 0, T)
            hi = np.clip(t + w - w // 2, 0, T)
            ic = (1.0 / (hi - lo).astype(f32)).astype(f32)
            out[(g % 2) * 64:(g % 2) * 64 + 64, g // 2, :] = ic[None, :]
        return out
    m["icS"] = invcnt(cfg.T_S)
    m["icP"] = invcnt(256)
    cst = np.zeros((128, NCONST), f32)
    i = np.arange(128)
    cst[:, C_ID:C_ID + 128] = (i[:, None] == i[None, :])
    cst[:, C_LI:C_LI + 128] = (i[:, None] <= i[None, :])
    cst[:, C_LS:C_LS + 128] = (i[:, None] < i[None, :])
    cst[:, C_UI:C_UI + 128] = (i[:, None] >= i[None, :])
    cst[:, C_US:C_US + 128] = (i[:, None] > i[None, :])
    cst[:, C_ONE:C_ONE + 128] = 1.0
    cst[:, C_BLK:C_BLK + 128] = ((i[:, None] // 64) == (i[None, :] // 64))
    m["consts"] = cst
    return m


class Ring:
    def __init__(self, P, name, n, shape, dt, psum=False):
        self.name, self.n, self.i = name, n, 0
        self.tiles = [(P.psum if psum else P.sbuf)(f"{name}{i}", shape, dt) for i in range(n)]

    def next(self):
        t, k = self.tiles[self.i], (self.name, self.i)
        self.i = (self.i + 1) % self.n
        return t, k


class ARing:
    def __init__(self, name, aps):
        self.name, self.tiles, self.n, self.i = name, aps, len(aps), 0

    def next(self):
        t, k = self.tiles[self.i], (self.name, self.i)
        self.i = (self.i + 1) % self.n
        return t, k


class Builder:
    def __init__(self, cfg, debug=(), debug_in=()):
        self.cfg = cfg
        self.debug = set(debug)
        self.debug_in = set(debug_in)
        nc = bass.Bass("TRN2", target_bir_lowering=False)
        self.nc = nc
        self.P = Prog(nc, n_dma_sems=10)
        self.dram = {}
        self.declare_io()
        self.alloc()

    def din(self, name, shape, dt=F32):
        self.dram[name] = self.nc.dram_tensor(name, list(shape), dt, kind="ExternalInput").ap()
        return self.dram[name]

    def dout(self, name, shape, dt=F32):
        self.dram[name] = self.nc.dram_tensor(name, list(shape), dt, kind="ExternalOutput").ap()
        return self.dram[name]

    def dscr(self, name, shape, dt):
        kind = "ExternalOutput" if name in self.debug else ("ExternalInput" if name in self.debug_in else "Internal")
        self.dram[name] = self.nc.dram_tensor(name, list(shape), dt, kind=kind).ap()
        return self.dram[name]

    def declare_io(self):
        c = self.cfg
        self.din("xs", [c.T_S, D]); self.din("xp", [c.T_P, D])
        self.din("cak", [2, CTX, 128]); self.din("cav", [2, CTX, 128])
        self.din("cckv", [2, CTX, 128]); self.din("ckr", [2, CTX, 32])
        self.din("sdf", [2, 4, 64, 64]); self.din("sdb", [2, 4, 64, 64])
        self.din("c2T", [128, NK, 2])
        self.din("wfm", [2, D, 2048]); self.din("wtok", [2, D, 1024]); self.din("wgate", [2, D, 4096])
        self.din("wmod", [2, D, 6144]); self.din("wbr", [2, 4, 256, D]); self.din("wo", [2, D, D])
        self.din("wup", [2, D, 2 * D_FF]); self.din("wdown", [2, D_FF, D])
        self.din("wuq2", [2, 256, 768]); self.din("wukv2", [2, 128, 512])
        self.din("vecs", [2, 128, NV]); self.din("rowsb", [2, 128, NR]); self.din("wpool", [2, 4, 64, 64])
        self.din("ropeA", [128, 2, c.T_S]); self.din("ropeB", [96, 2, c.T_S])
        self.din("icS", [128, 2, c.T_S]); self.din("icP", [128, 2, 256]); self.din("consts", [128, NCONST])
        self.dout("ys", [c.T_S, D]); self.dout("yp", [c.T_P, D])
        self.dout("o_ak", [c.NSEQ, 2, 256, 128]); self.dout("o_av", [c.NSEQ, 2, 256, 128])
        self.dout("o_ckv", [c.NSEQ, 2, 256, 128]); self.dout("o_kr", [c.NSEQ, 2, 256, 32])
        self.dout("o_sf", [c.NSEQ, 2, 4, 64, 64]); self.dout("o_sb", [c.NSEQ, 2, 4, 64, 64])
        for l in range(2):
            self.dscr(f"Wfm{l}", [4, 128, NK, 512], BF16)
            self.dscr(f"Wtok{l}", [2, 128, NK, 512], BF16)
            self.dscr(f"Wgate{l}", [8, 128, NK, 512], BF16)
            self.dscr(f"Wmod{l}", [12, 128, NK, 512], BF16)
            self.dscr(f"Wbr{l}", [4, 128, 2, 1024], BF16)
            self.dscr(f"Wo{l}", [2, 128, NK, 512], BF16)
            self.dscr(f"Wup{l}", [11, 128, NK, 512], BF16)
            self.dscr(f"Wdown{l}", [8, 128, NKF, 128], BF16)
        NT = c.NT
        self.dscr("XT", [D, NT], F32)
        self.dscr("HT", [D, NT], BF16)
        self.dscr("AQT", [256, NT], BF16)
        self.dscr("AQRT", [256, c.T_S], BF16)
        self.dscr("AKT", [128, NT], BF16)
        self.dscr("AKCT", [128, CTX], BF16)
        self.dscr("AV", [NT, 128], BF16)
        self.dscr("QCT", [4, 96, NT], BF16)
        self.dscr("QLT", [4, 96, c.T_S], BF16)
        self.dscr("BKT", [4, 96, NT + CTX], BF16)
        self.dscr("BV", [NT + CTX, 256], BF16)
        self.dscr("CT", [256, NT], F32)
        self.dscr("DTs", [768, NT], F32)
        self.dscr("DZ", [NT, 272], F32)
        self.dscr("BRT", [4, 256, NT], BF16)
        for nm in ("QNT", "KNT"):
            self.dscr(nm, [256, NT], F32)
        for nm in ("KN", "VV", "OF", "OB"):
            self.dscr(nm, [NT, 256], F32)

    def alloc(self):
        P = self.P
        self.cst = P.sbuf("cst", [128, NCONST], F32)
        self.onesb = P.sbuf("onesb", [128, 128], BF16)
        self.identb = P.sbuf("identb", [128, 128], BF16)
        self.vecs = [P.sbuf(f"vecs{l}", [128, NV], F32) for l in range(2)]
        self.rowsb = [P.sbuf(f"rowsb{l}", [128, NR], F32) for l in range(2)]
        self.modv = P.sbuf("modv", [128, 48, 2], F32)
        self.gms = [P.sbuf(f"gm{l}", [128, 6, 8, 2], F32) for l in range(2)]
        self.c2 = P.sbuf("c2", [128, NK, 2], F32)
        self.c2b = P.sbuf("c2b", [128, NK, 2], BF16)
        self.wuq = P.sbuf("wuq", [128, 2, 768], BF16)
        self.wukv = P.sbuf("wukv", [128, 512], BF16)
        self.wring = Ring(P, "wb", 6, [128, 4096], BF16)
        self.ps = Ring(P, "ps", 6, [128, 512], F32, psum=True)
        self.psx = Ring(P, "psx", 2, [128, 512], F32, psum=True)
        self.xT = P.sbuf("xT", [128, NK, TB], F32)
        self.sq = P.sbuf("sq", [128, NK, TB], BF16)
        self.hT = P.sbuf("hT", [128, NK, TB], BF16)
        self.rstd = P.sbuf("rstd", [128, TB], F32)
        self.f32r = Ring(P, "f32r", 5, [128, TB], F32)
        self.bf16r = Ring(P, "bf16r", 6, [128, TB], BF16)
        self.small = Ring(P, "small", 8, [128, 8], F32)
        self.arenaF = P.sbuf("arenaF", [128, 12288], F32)
        self.arenaB = P.sbuf("arenaB", [128, 24576], BF16)
        aF, aB = self.arenaF, self.arenaB
        self.cqs = aF[:, 0:1024].rearrange("p (k t) -> p k t", k=2)
        self.tab = aF[:, 1024:2048].rearrange("p (k t) -> p k t", k=2)
        self.tabB = aF[0:96, 2048:3072].rearrange("p (k t) -> p k t", k=2)
        self.tokf = ARing("tokf", [aF[:, 3072 + i * 512:3072 + (i + 1) * 512] for i in range(3)])
        self.xin = ARing("xin", [aF[:, 4608 + i * 1024:4608 + (i + 1) * 1024] for i in range(2)])
        self.cqn = aB[:, 0:1024].rearrange("p (k t) -> p k t", k=2)
        self.ckvnT = aB[:, 1024:1536]
        self.accT = aF[:, 0:4096].rearrange("p (k t) -> p k t", k=NK)
        self.mixT = aF[:, 4096:8192].rearrange("p (k t) -> p k t", k=NK)
        self.yt = ARing("yt", [aF[:, 8192 + i * 1024:8192 + (i + 1) * 1024] for i in range(2)])
        self.gT = aB[:, 0:4096].rearrange("p (k t) -> p k t", k=NK)
        self.brT = aB[:, 4096:8192].rearrange("p (m c t) -> p m c t", m=4, c=2)
        self.aT = aB[:, 8192:8192 + NKF * TB].rearrange("p (k t) -> p k t", k=NKF)

    def mm(self, out, lhsT, rhs, start, stop, reads, writes):
        return self.P.op("pe", lambda e: e.matmul(out, lhsT=lhsT, rhs=rhs, start=start, stop=stop), reads, writes, pe_accum=True)

    def tr(self, out, in_, ident, reads, writes):
        return self.P.op("pe", lambda e: e.transpose(out, in_, ident), reads, writes, pe_accum=True)

    def act(self, out, in_, func, reads, writes, **kw):
        return self.P.op("act", lambda e: e.activation(out=out, in_=in_, func=func, **kw), reads, writes)

    def dve(self, fn, reads, writes):
        return self.P.op("dve", fn, reads, writes)

    def pool(self, fn, reads, writes):
        return self.P.op("pool", fn, reads, writes)

    def load(self, out, in_, reads=(), writes=()):
        return self.P.dma("sp", out, in_, reads=reads, writes=writes)

    def store(self, out, in_, reads=(), writes=(), is_output=False):
        return self.P.dma("pool", out, in_, reads=reads, writes=writes, is_output=is_output)

    def wload(self, dram_panel, kc, pc, rkey):
        t, k = self.wring.next()
        v = t[:, 0:kc * pc].rearrange("p (k c) -> p k c", k=kc)
        self.load(v, dram_panel, reads=[rkey], writes=[k])
        return v, k

    def phase_init(self):
        d = self.dram
        self.load(self.cst[:], d["consts"], writes=["cst"])
        for l in range(2):
            self.load(self.vecs[l][:], d["vecs"][l], writes=[f"vecs{l}"])
            self.load(self.rowsb[l][:], d["rowsb"][l], writes=[f"rowsb{l}"])
        self.load(self.c2[:], d["c2T"], writes=["c2"])
        self.dve(lambda e: e.tensor_copy(out=self.onesb[:], in_=self.cst[:, C_ONE:C_ONE + 128]), ["cst"], ["onesb"])
        self.dve(lambda e: e.tensor_copy(out=self.identb[:], in_=self.cst[:, C_ID:C_ID + 128]), ["cst"], ["identb"])
        self.act(self.c2b[:], self.c2[:], AF.Silu, ["c2"], ["c2b"])

    def phase_w(self, l):
        d = self.dram

        def cast(dst, src2d, n, pc, key):
            srcv = src2d.rearrange("(k p) c -> p k c", p=128)
            for i in range(n):
                self.store(dst[i], srcv[:, :, i * pc:(i + 1) * pc], writes=[(key, i)])
        cast(d[f"Wmod{l}"], d["wmod"][l], 12, 512, f"Wmod{l}")
        cast(d[f"Wfm{l}"], d["wfm"][l], 4, 512, f"Wfm{l}")
        cast(d[f"Wtok{l}"], d["wtok"][l], 2, 512, f"Wtok{l}")
        cast(d[f"Wgate{l}"], d["wgate"][l], 8, 512, f"Wgate{l}")
        for m_ in range(4):
            self.store(d[f"Wbr{l}"][m_], d["wbr"][l, m_].rearrange("(k p) c -> p k c", p=128), writes=[(f"Wbr{l}", m_)])
        cast(d[f"Wo{l}"], d["wo"][l], 2, 512, f"Wo{l}")
        cast(d[f"Wup{l}"], d["wup"][l], 11, 512, f"Wup{l}")
        cast(d[f"Wdown{l}"], d["wdown"][l], 8, 128, f"Wdown{l}")

    def layer_setup(self, l):
        d, P = self.dram, self.P
        self.store(self.wuq[:], d["wuq2"][l].rearrange("(k p) c -> p k c", p=128), writes=["wuq"])
        self.store(self.wukv[:], d["wukv2"][l], writes=["wukv"])
        ps, pk = self.psx.next()
        for n in range(12):
            w, wk = self.wload(d[f"Wmod{l}"][n], NK, 512, (f"Wmod{l}", n))
            for j in range(4):
                fc = n * 4 + j
                for k in range(NK):
                    self.mm(ps[:, fc * 2:fc * 2 + 2], w[:, k, j * 128:(j + 1) * 128], self.c2b[:, k, :],
                            k == 0, k == NK - 1, [wk, "c2b"], [pk])
        vec = self.vecs[l]
        for c_ in range(2):
            self.dve(lambda e, c_=c_: e.tensor_tensor(out=self.modv[:, :, c_], in0=ps[:, 0:96].rearrange("p (f c) -> p f c", c=2)[:, :, c_],
                                                      in1=vec[:, V_BMOD:V_BMOD + 48], op=ALU.add), [pk, f"vecs{l}"], ["modv"])
        gm = self.gms[l]
        for c_ in range(2):
            mv = lambda i, c_=c_: self.modv[:, i * 8:(i + 1) * 8, c_]
            self.dve(lambda e, c_=c_, mv=mv: e.scalar_tensor_tensor(out=gm[:, 0, :, c_], in0=mv(1), scalar=1.0, in1=vec[:, V_GPRE1:V_GPRE1 + 8], op0=ALU.add, op1=ALU.mult), ["modv", f"vecs{l}"], [f"gm{l}"])
            self.dve(lambda e, c_=c_, mv=mv: e.tensor_copy(out=gm[:, 1, :, c_], in_=mv(0)), ["modv"], [f"gm{l}"])
            self.dve(lambda e, c_=c_, mv=mv: e.tensor_tensor(out=gm[:, 2, :, c_], in0=mv(2), in1=vec[:, V_GPOST1:V_GPOST1 + 8], op=ALU.mult), ["modv", f"vecs{l}"], [f"gm{l}"])
            self.dve(lambda e, c_=c_, mv=mv: e.scalar_tensor_tensor(out=gm[:, 3, :, c_], in0=mv(4), scalar=1.0, in1=vec[:, V_GPRE2:V_GPRE2 + 8], op0=ALU.add, op1=ALU.mult), ["modv", f"vecs{l}"], [f"gm{l}"])
            self.dve(lambda e, c_=c_, mv=mv: e.tensor_copy(out=gm[:, 4, :, c_], in_=mv(3)), ["modv"], [f"gm{l}"])
            self.dve(lambda e, c_=c_, mv=mv: e.tensor_tensor(out=gm[:, 5, :, c_], in0=mv(5), in1=vec[:, V_GPOST2:V_GPOST2 + 8], op=ALU.mult), ["modv", f"vecs{l}"], [f"gm{l}"])

    def fm_rstd(self, src, skey, nk, nfeat, out_rstd, okey, krows=None):
        ps, pk = self.psx.next()
        for k in range(nk):
            rows = 128 if krows is None else krows[k]
            self.act(self.sq[0:rows, k, :], src[0:rows, k, :], AF.Square, [skey], [("sq", k)])
            self.mm(ps[:], self.onesb[0:rows, :], self.sq[0:rows, k, :], k == 0, k == nk - 1, [("sq", k), "onesb"], [pk])
        self.act(out_rstd, ps[:], AF.Sqrt, [pk], [okey], bias=self.eps_ap, scale=1.0 / nfeat)
        self.dve(lambda e: e.reciprocal(out=out_rstd, in_=out_rstd), [okey], [okey])

    def blk_info(self, bi):
        c = self.cfg
        t0 = bi * TB
        sample = bi < c.NBS
        return t0, sample, (0 if sample else 1)

    def phase_a_block(self, l, bi, load_x=True):
        c, d, P = self.cfg, self.dram, self.P
        t0, sample, cond = self.blk_info(bi)
        rope = sample
        ts = slice(t0, t0 + TB)
        xT = self.xT
        if l == 0:
            for tt in range(TB // 128):
                xi, xk = self.xin.next()
                src = d["xs"][t0 + tt * 128:t0 + (tt + 1) * 128, :] if sample else d["xp"][t0 - c.T_S + tt * 128:t0 - c.T_S + (tt + 1) * 128, :]
                self.load(xi[:], src, writes=[xk])
                for k in range(NK):
                    ps, pk = self.ps.next()
                    self.tr(ps[:, 0:128], xi[:, k * 128:(k + 1) * 128], self.cst[:, C_ID:C_ID + 128], [xk, "cst"], [pk])
                    if k % 2 == 0:
                        self.act(xT[:, k, tt * 128:(tt + 1) * 128], ps[:, 0:128], AF.Copy, [pk], [("xT", k)])
                    else:
                        self.dve(lambda e, k=k, ps=ps, tt=tt: e.tensor_copy(out=xT[:, k, tt * 128:(tt + 1) * 128], in_=ps[:, 0:128]), [pk], [("xT", k)])
            self.store(d["XT"].rearrange("(k p) t -> p k t", p=128)[:, :, ts], xT[:], reads=[("xT", k) for k in range(NK)], writes=[("XT", bi)])
        elif load_x:
            self.load(xT[:], d["XT"].rearrange("(k p) t -> p k t", p=128)[:, :, ts], reads=[("XT", bi)], writes=[("xT", k) for k in range(NK)])
        xkeys = [("xT", k) for k in range(NK)]
        ps, pk = self.psx.next()
        for k in range(NK):
            self.act(self.sq[:, k, :], xT[:, k, :], AF.Square, [("xT", k)], [("sq", k)])
            self.mm(ps[:], self.onesb[:], self.sq[:, k, :], k == 0, k == NK - 1, [("sq", k), "onesb"], [pk])
        self.act(self.rstd[:], ps[:], AF.Sqrt, [pk], ["rstd"], bias=self.eps_ap, scale=1.0 / D)
        self.dve(lambda e: e.reciprocal(out=self.rstd[:], in_=self.rstd[:]), ["rstd"], ["rstd"])
        for k in range(NK):
            t, tk = self.f32r.next()
            self.dve(lambda e, k=k, t=t: e.tensor_tensor(out=t[:], in0=xT[:, k, :], in1=self.rstd[:], op=ALU.mult), [("xT", k), "rstd"], [tk])
            if True:
                self.dve(lambda e, k=k, t=t: e.tensor_scalar(out=self.hT[:, k, :], in0=t[:], scalar1=self.gms[l][:, 0, k, cond:cond + 1], scalar2=self.gms[l][:, 1, k, cond:cond + 1], op0=ALU.mult, op1=ALU.add), [tk, f"gm{l}"], [("hT", k)])
            else:
                self.act(self.hT[:, k, :], t[:], AF.Identity, [tk], [("hT", k)], scale=self.gms[l][:, 0, k, cond:cond + 1], bias=self.gms[l][:, 1, k, cond:cond + 1])
        hkeys = [("hT", k) for k in range(NK)]
        self.store(d["HT"].rearrange("(k p) t -> p k t", p=128)[:, :, ts], self.hT[:], reads=hkeys, writes=[("HT", bi)])
        if rope:
            self.load(self.tab[:], d["ropeA"][:, :, ts], writes=["tab"])
            self.load(self.tabB[:], d["ropeB"][:, :, ts], writes=["tabB"])
        hT = self.hT

        def proj_fm(w, wk, c0, m):
            ps, pk = self.ps.next()
            for k in range(NK):
                self.mm(ps[0:m, :], w[:, k, c0:c0 + m], hT[:, k, :], k == 0, k == NK - 1, [wk, ("hT", k)], [pk])
            return ps, pk

        def evac_store(ps, pk, m, dst, dt, eng="act", wkey=None):
            ring = self.bf16r if dt == BF16 else self.f32r
            t, tk = ring.next()
            if eng == "act":
                self.act(t[0:m, :], ps[0:m, :], AF.Copy, [pk], [tk])
            else:
                self.dve(lambda e: e.tensor_copy(out=t[0:m, :], in_=ps[0:m, :]), [pk], [tk])
            self.store(dst, t[0:m, :], reads=[tk], writes=[wkey] if wkey else [])

        def rope_store(ps, pk, psp, ppk, m, tabt, tabk, dsts, wkey, p0=0):
            a1, ak1 = self.f32r.next()
            a2, ak2 = self.f32r.next()
            o, ok = self.bf16r.next()
            sl = slice(p0, p0 + m)
            self.act(a1[sl, :], ps[sl, :], AF.Copy, [pk], [ak1])
            self.act(a2[sl, :], psp[sl, :], AF.Copy, [ppk], [ak2])
            self.dve(lambda e: e.tensor_tensor(out=a1[sl, :], in0=a1[sl, :], in1=tabt[sl, 0, :], op=ALU.mult), [ak1, tabk], [ak1])
            self.dve(lambda e: e.tensor_tensor(out=a2[sl, :], in0=a2[sl, :], in1=tabt[sl, 1, :], op=ALU.mult), [ak2, tabk], [ak2])
            self.dve(lambda e: e.tensor_tensor(out=o[sl, :], in0=a1[sl, :], in1=a2[sl, :], op=ALU.add), [ak1, ak2], [ok])
            for dst in dsts:
                self.store(dst, o[sl, :], reads=[ok], writes=[wkey])
            return o, ok

        Wfm = d[f"Wfm{l}"]
        w0, wk0 = self.wload(Wfm[0], NK, 512, (f"Wfm{l}", 0))
        w1, wk1 = self.wload(Wfm[1], NK, 512, (f"Wfm{l}", 1))
        w3 = wk3 = None
        if rope:
            w3, wk3 = self.wload(Wfm[3], NK, 512, (f"Wfm{l}", 3))
        for j in range(2):
            ps, pk = proj_fm(w0, wk0, j * 128, 128)
            evac_store(ps, pk, 128, d["AQT"][j * 128:(j + 1) * 128, ts], BF16, "act", ("AQT", bi))
            if rope:
                psp, ppk = proj_fm(w3, wk3, 128 + j * 128, 128)
                rope_store(ps, pk, psp, ppk, 128, self.tab, "tab", [d["AQRT"][j * 128:(j + 1) * 128, ts]], ("AQRT", bi))
        ps, pk = proj_fm(w0, wk0, 256, 128)
        if rope:
            psp, ppk = proj_fm(w3, wk3, 384, 128)
            rope_store(ps, pk, psp, ppk, 128, self.tab, "tab", [d["AKT"][:, ts]], ("AKT", bi))
        else:
            evac_store(ps, pk, 128, d["AKT"][:, ts], BF16, "act", ("AKT", bi))
        ps, pk = proj_fm(w0, wk0, 384, 128)
        self.act(self.cqs[:, 0, :], ps[:], AF.Copy, [pk], [("cqs", 0)])
        ps, pk = proj_fm(w1, wk1, 0, 64)
        self.act(self.cqs[0:64, 1, :], ps[0:64, :], AF.Copy, [pk], [("cqs", 1)])
        pss, psk = self.psx.next()
        for k, rows in ((0, 128), (1, 64)):
            self.act(self.sq[0:rows, k, :], self.cqs[0:rows, k, :], AF.Square, [("cqs", k)], [("sq", k)])
            self.mm(pss[:], self.onesb[0:rows, :], self.sq[0:rows, k, :], k == 0, k == 1, [("sq", k), "onesb"], [psk])
        rq, rqk = self.f32r.next()
        self.act(rq[:], pss[:], AF.Sqrt, [psk], [rqk], bias=self.eps_ap, scale=1.0 / 192)
        self.dve(lambda e: e.reciprocal(out=rq[:], in_=rq[:]), [rqk], [rqk])
        vec = self.vecs[l]
        for k, rows in ((0, 128), (1, 64)):
            self.dve(lambda e, k=k, rows=rows: e.scalar_tensor_tensor(out=self.cqn[0:rows, k, :], in0=self.cqs[0:rows, k, :], scalar=vec[0:rows, V_GCQ + k:V_GCQ + k + 1],
                                                                      in1=rq[0:rows, :], op0=ALU.mult, op1=ALU.mult), [("cqs", k), rqk, f"vecs{l}"], [("cqn", k)])
        for h in range(4):
            ps, pk = self.ps.next()
            self.mm(ps[0:96, :], self.wuq[:, 0, h * 96:(h + 1) * 96], self.cqn[:, 0, :], True, False, ["wuq", ("cqn", 0)], [pk])
            self.mm(ps[0:96, :], self.wuq[0:64, 1, h * 96:(h + 1) * 96], self.cqn[0:64, 1, :], False, True, ["wuq", ("cqn", 1)], [pk])
            t, tk = self.bf16r.next()
            self.act(t[0:96, :], ps[0:96, :], AF.Copy, [pk], [tk])
            self.store(d["QCT"][h, :, ts], t[0:96, :], reads=[tk], writes=[("QCT", bi)])
            if rope:
                psp, ppk = self.ps.next()
                self.mm(psp[0:96, :], self.wuq[:, 0, 384 + h * 96:384 + (h + 1) * 96], self.cqn[:, 0, :], True, False, ["wuq", ("cqn", 0)], [ppk])
                self.mm(psp[0:96, :], self.wuq[0:64, 1, 384 + h * 96:384 + (h + 1) * 96], self.cqn[0:64, 1, :], False, True, ["wuq", ("cqn", 1)], [ppk])
                self.store(d["QLT"][h, 0:64, ts], t[0:64, :], reads=[tk], writes=[("QLT", bi)])
                rope_store(ps, pk, psp, ppk, 32, self.tabB, "tabB", [d["QLT"][h, 64:96, ts]], ("QLT", bi), p0=64)
        ps, pk = proj_fm(w1, wk1, 64, 32)
        bkt_dsts = [d["BKT"][h, 64:96, ts] for h in range(4)]
        if rope:
            psp, ppk = proj_fm(w1, wk1, 96, 32)
            rope_store(ps, pk, psp, ppk, 32, self.tabB, "tabB", bkt_dsts, ("BKT", bi))
        else:
            t, tk = self.bf16r.next()
            self.act(t[0:32, :], ps[0:32, :], AF.Copy, [pk], [tk])
            for dst in bkt_dsts:
                self.store(dst, t[0:32, :], reads=[tk], writes=[("BKT", bi)])
        for j in range(2):
            ps, pk = proj_fm(w1, wk1, 128 + j * 128, 128)
            evac_store(ps, pk, 128, d["CT"][j * 128:(j + 1) * 128, ts], F32, "dve" if j else "act", ("CT", bi))
        ps, pk = proj_fm(w1, wk1, 384, 128)
        evac_store(ps, pk, 128, d["DTs"][0:128, ts], F32, "act", ("DTs", bi))
        w2, wk2 = self.wload(Wfm[2], NK, 512, (f"Wfm{l}", 2))
        for j in range(4):
            ps, pk = proj_fm(w2, wk2, j * 128, 128)
            evac_store(ps, pk, 128, d["DTs"][(j + 1) * 128:(j + 2) * 128, ts], F32, "dve" if j % 2 else "act", ("DTs", bi))
        if not rope:
            w3, wk3 = self.wload(Wfm[3], NK, 512, (f"Wfm{l}", 3))
        ps, pk = proj_fm(w3, wk3, 0, 128)
        evac_store(ps, pk, 128, d["DTs"][640:768, ts], F32, "dve", ("DTs", bi))
        wt0, wtk0 = self.wload(d[f"Wtok{l}"][0], NK, 512, (f"Wtok{l}", 0))
        wt1, wtk1 = self.wload(d[f"Wtok{l}"][1], NK, 512, (f"Wtok{l}", 1))
        rb = self.rowsb[l]
        for tt in range(TB // 128):
            tsl = slice(tt * 128, (tt + 1) * 128)
            g0 = t0 + tt * 128
            psA, pkA = self.ps.next()
            for k in range(NK):
                self.mm(psA[:, 0:416], hT[:, k, tsl], wt0[:, k, 0:416], k == 0, k == NK - 1, [wtk0, ("hT", k)], [pkA])
            psB, pkB = self.ps.next()
            for k in range(NK):
                self.mm(psB[:, 0:272], hT[:, k, tsl], wt1[:, k, 0:272], k == 0, k == NK - 1, [wtk1, ("hT", k)], [pkB])
            tz, tzk = self.tokf.next()
            self.act(tz[:, 0:272], psB[:, 0:272], AF.Copy, [pkB], [tzk])
            self.store(d["DZ"][g0:g0 + 128, :], tz[:, 0:272], reads=[tzk], writes=[("DZ", bi)])
            ta, tak = self.tokf.next()
            self.dve(lambda e, ta=ta, psA=psA: e.tensor_copy(out=ta[:, 0:416], in_=psA[:, 0:416]), [pkA], [tak])
            tb, tbk = self.bf16r.next()
            self.dve(lambda e, ta=ta, tb=tb: e.tensor_copy(out=tb[:, 0:128], in_=ta[:, 128:256]), [tak], [tbk])
            self.store(d["AV"][g0:g0 + 128, :], tb[:, 0:128], reads=[tbk], writes=[("AV", bi)])
            junk, jk = self.f32r.next()
            sm, smk = self.small.next()
            self.act(junk[:, 0:128], ta[:, 256:384], AF.Square, [tak], [jk, smk], accum_out=sm[:, 0:1])
            self.act(sm[:, 1:2], sm[:, 0:1], AF.Sqrt, [smk], [smk], bias=self.eps_ap, scale=1.0 / 128)
            self.dve(lambda e, sm=sm: e.reciprocal(out=sm[:, 2:3], in_=sm[:, 1:2]), [smk], [smk])
            cn, cnk = self.f32r.next()
            self.dve(lambda e, cn=cn, ta=ta, sm=sm: e.scalar_tensor_tensor(out=cn[:, 0:128], in0=ta[:, 256:384], scalar=sm[:, 2:3], in1=rb[:, R_GCKV:R_GCKV + 128],
                                                                         op0=ALU.mult, op1=ALU.mult), [tak, smk, f"rowsb{l}"], [cnk])
            if not sample:
                sq_, tq_ = divmod(g0 - c.T_S, 256)
                self.store(d["o_ak"][sq_, l, tq_:tq_ + 128, :], ta[:, 0:128], reads=[tak], is_output=True)
                self.store(d["o_av"][sq_, l, tq_:tq_ + 128, :], ta[:, 128:256], reads=[tak], is_output=True)
                self.store(d["o_kr"][sq_, l, tq_:tq_ + 128, :], ta[:, 384:416], reads=[tak], is_output=True)
                self.store(d["o_ckv"][sq_, l, tq_:tq_ + 128, :], cn[:, 0:128], reads=[cnk], is_output=True)
            psT, pkT = self.ps.next()
            self.tr(psT[:, 0:128], cn[:, 0:128], self.cst[:, C_ID:C_ID + 128], [cnk, "cst"], [pkT])
            self.act(self.ckvnT[:, tsl], psT[:, 0:128], AF.Copy, [pkT], [("ckvnT", tt)])
            self.mla_v(self.ckvnT[:, tsl], [("ckvnT", tt)], g0, ("BV", bi))
        self.mla_knope(self.ckvnT[:], [("ckvnT", i) for i in range(4)], TB, t0, ("BKT", bi))

    def mla_v(self, ckvnT_tile, key, g0, wkey):
        d = self.dram
        ps, pk = self.ps.next()
        self.mm(ps[:, 0:256], ckvnT_tile, self.wukv[:, 256:512], True, True, list(key) + ["wukv"], [pk])
        t, tk = self.bf16r.next()
        self.dve(lambda e: e.tensor_copy(out=t[:, 0:256], in_=ps[:, 0:256]), [pk], [tk])
        self.store(d["BV"][g0:g0 + 128, :], t[:, 0:256], reads=[tk], writes=[wkey])

    def mla_knope(self, ckvnT, key, n, t0, wkey):
        d = self.dram
        for h in range(4):
            ps, pk = self.ps.next()
            self.mm(ps[0:64, 0:n], self.wukv[:, h * 64:(h + 1) * 64], ckvnT, True, True, list(key) + ["wukv"], [pk])
            t, tk = self.bf16r.next()
            self.act(t[0:64, 0:n], ps[0:64, 0:n], AF.Copy, [pk], [tk])
            self.store(d["BKT"][h, 0:64, t0:t0 + n], t[0:64, 0:n], reads=[tk], writes=[wkey])

    def phase_ctx(self, l):
        c, d = self.cfg, self.dram
        NT = c.NT
        ident = self.cst[:, C_ID:C_ID + 128]
        for tt in range(2):
            tsl = slice(tt * 128, (tt + 1) * 128)
            xi, xk = self.xin.next()
            self.load(xi[:, 0:128], d["cckv"][l, tsl, :], writes=[xk])
            self.load(xi[:, 128:160], d["ckr"][l, tsl, :], writes=[xk])
            self.load(xi[:, 256:384], d["cak"][l, tsl, :], writes=[xk])
            ps, pk = self.ps.next()
            self.tr(ps[:, 0:128], xi[:, 0:128], ident, [xk, "cst"], [pk])
            self.act(self.ckvnT[:, tsl], ps[:, 0:128], AF.Copy, [pk], [("ckvnT", tt)])
            self.mla_v(self.ckvnT[:, tsl], [("ckvnT", tt)], NT + tt * 128, ("BV", "ctx"))
            ps, pk = self.ps.next()
            self.tr(ps[0:32, 0:128], xi[:, 128:160], ident, [xk, "cst"], [pk])
            t, tk = self.bf16r.next()
            self.act(t[0:32, 0:128], ps[0:32, 0:128], AF.Copy, [pk], [tk])
            for h in range(4):
                self.store(d["BKT"][h, 64:96, NT + tt * 128:NT + (tt + 1) * 128], t[0:32, 0:128], reads=[tk], writes=[("BKT", "ctx")])
            ps, pk = self.ps.next()
            self.tr(ps[:, 0:128], xi[:, 256:384], ident, [xk, "cst"], [pk])
            t, tk = self.bf16r.next()
            self.act(t[:, 0:128], ps[:, 0:128], AF.Copy, [pk], [tk])
            self.store(d["AKCT"][:, tsl], t[:, 0:128], reads=[tk], writes=[("AKCT", 0)])
        self.mla_knope(self.ckvnT[:, 0:256], [("ckvnT", 0), ("ckvnT", 1)], 256, NT, ("BKT", "ctx"))

    def setup_eps(self):
        self.epst = self.P.sbuf("epst", [128, 1], F32)
        self.pool(lambda e: e.memset(self.epst[:], EPS), [], ["epst"])
        self.eps_ap = self.epst[:, 0:1]


class BuilderC(Builder):
    def norm_resid(self, l, gidx, cond):
        ps, pk = self.psx.next()
        for k in range(NK):
            self.act(self.sq[:, k, :], self.mixT[:, k, :], AF.Square, [("mixT", k)], [("sq", k)])
            self.mm(ps[:], self.onesb[:], self.sq[:, k, :], k == 0, k == NK - 1, [("sq", k), "onesb"], [pk])
        self.act(self.rstd[:], ps[:], AF.Sqrt, [pk], ["rstd"], bias=self.eps_ap, scale=1.0 / D)
        self.dve(lambda e: e.reciprocal(out=self.rstd[:], in_=self.rstd[:]), ["rstd"], ["rstd"])
        gm = self.gms[l]
        for k in range(NK):
            t, tk = self.f32r.next()
            self.dve(lambda e, k=k, t=t: e.tensor_tensor(out=t[:], in0=self.mixT[:, k, :], in1=self.rstd[:], op=ALU.mult), [("mixT", k), "rstd"], [tk])
            self.dve(lambda e, k=k, t=t: e.scalar_tensor_tensor(out=self.xT[:, k, :], in0=t[:], scalar=gm[:, gidx, k, cond:cond + 1], in1=self.xT[:, k, :],
                                                              op0=ALU.mult, op1=ALU.add), [tk, f"gm{l}", ("xT", k)], [("xT", k)])

    def norm_h(self, l, gs, gb, cond):
        ps, pk = self.psx.next()
        for k in range(NK):
            self.act(self.sq[:, k, :], self.xT[:, k, :], AF.Square, [("xT", k)], [("sq", k)])
            self.mm(ps[:], self.onesb[:], self.sq[:, k, :], k == 0, k == NK - 1, [("sq", k), "onesb"], [pk])
        self.act(self.rstd[:], ps[:], AF.Sqrt, [pk], ["rstd"], bias=self.eps_ap, scale=1.0 / D)
        self.dve(lambda e: e.reciprocal(out=self.rstd[:], in_=self.rstd[:]), ["rstd"], ["rstd"])
        gm = self.gms[l]
        for k in range(NK):
            t, tk = self.f32r.next()
            self.dve(lambda e, k=k, t=t: e.tensor_tensor(out=t[:], in0=self.xT[:, k, :], in1=self.rstd[:], op=ALU.mult), [("xT", k), "rstd"], [tk])
            self.dve(lambda e, k=k, t=t: e.tensor_scalar(out=self.hT[:, k, :], in0=t[:], scalar1=gm[:, gs, k, cond:cond + 1], scalar2=gm[:, gb, k, cond:cond + 1],
                                                       op0=ALU.mult, op1=ALU.add), [tk, f"gm{l}"], [("hT", k)])

    def phase_c_block(self, l, bi, last):
        c, d = self.cfg, self.dram
        t0, sample, cond = self.blk_info(bi)
        ts = slice(t0, t0 + TB)
        xT, hT, gT, brT, accT, mixT, aT = self.xT, self.hT, self.gT, self.brT, self.accT, self.mixT, self.aT
        self.load(hT[:], d["HT"].rearrange("(k p) t -> p k t", p=128)[:, :, ts], reads=[("HT", bi)], writes=[("hT", k) for k in range(NK)])
        self.load(xT[:], d["XT"].rearrange("(k p) t -> p k t", p=128)[:, :, ts], reads=[("XT", bi)], writes=[("xT", k) for k in range(NK)])
        for m in range(4):
            self.load(brT[:, m], d["BRT"][m].rearrange("(c p) t -> p c t", p=128)[:, :, ts], reads=[("BRT", m)], writes=[("brT", m)])
        for m in range(4):
            wg = [self.wload(d[f"Wgate{l}"][2 * m + i], NK, 512, (f"Wgate{l}", 2 * m + i)) for i in range(2)]
            wb, wbk = self.wload(d[f"Wbr{l}"][m], 2, 1024, (f"Wbr{l}", m))
            for fc in range(NK):
                w, wk = wg[fc // 4]
                co = (fc % 4) * 128
                psg, pgk = self.ps.next()
                for k in range(NK):
                    self.mm(psg[:], w[:, k, co:co + 128], hT[:, k, :], k == 0, k == NK - 1, [wk, ("hT", k)], [pgk])
                psb, pbk = self.ps.next()
                for cc in range(2):
                    self.mm(psb[:], wb[:, cc, fc * 128:(fc + 1) * 128], brT[:, m, cc, :], cc == 0, cc == 1, [wbk, ("brT", m)], [pbk])
                sg, sgk = self.f32r.next()
                self.act(sg[:], psg[:], AF.Sigmoid, [pgk], [sgk])
                if m == 0:
                    self.dve(lambda e, sg=sg, psb=psb, fc=fc: e.tensor_tensor(out=accT[:, fc, :], in0=sg[:], in1=psb[:], op=ALU.mult), [sgk, pbk], [("accT", fc)])
                else:
                    self.dve(lambda e, sg=sg, psb=psb: e.tensor_tensor(out=sg[:], in0=sg[:], in1=psb[:], op=ALU.mult), [sgk, pbk], [sgk])
                    if m < 3:
                        self.dve(lambda e, sg=sg, fc=fc: e.tensor_tensor(out=accT[:, fc, :], in0=accT[:, fc, :], in1=sg[:], op=ALU.add), [sgk, ("accT", fc)], [("accT", fc)])
                    else:
                        self.dve(lambda e, sg=sg, fc=fc: e.tensor_tensor(out=gT[:, fc, :], in0=accT[:, fc, :], in1=sg[:], op=ALU.add), [sgk, ("accT", fc)], [("gT", fc)])
        wo = [self.wload(d[f"Wo{l}"][i], NK, 512, (f"Wo{l}", i)) for i in range(2)]
        for fc in range(NK):
            w, wk = wo[fc // 4]
            co = (fc % 4) * 128
            ps, pk = self.ps.next()
            for k in range(NK):
                self.mm(ps[:], w[:, k, co:co + 128], gT[:, k, :], k == 0, k == NK - 1, [wk, ("gT", k)], [pk])
            self.act(mixT[:, fc, :], ps[:], AF.Copy, [pk], [("mixT", fc)])
        self.norm_resid(l, 2, cond)
        self.norm_h(l, 3, 4, cond)
        for n in range(11):
            w, wk = self.wload(d[f"Wup{l}"][n], NK, 512, (f"Wup{l}", n))
            for j in range(2):
                psg, pgk = self.ps.next()
                for k in range(NK):
                    self.mm(psg[:], w[:, k, j * 128:(j + 1) * 128], hT[:, k, :], k == 0, k == NK - 1, [wk, ("hT", k)], [pgk])
                psu, puk = self.ps.next()
                for k in range(NK):
                    self.mm(psu[:], w[:, k, 256 + j * 128:256 + (j + 1) * 128], hT[:, k, :], k == 0, k == NK - 1, [wk, ("hT", k)], [puk])
                sg, sgk = self.f32r.next()
                self.act(sg[:], psg[:], AF.Silu, [pgk], [sgk])
                kf = 2 * n + j
                self.dve(lambda e, sg=sg, psu=psu, kf=kf: e.tensor_tensor(out=aT[:, kf, :], in0=sg[:], in1=psu[:], op=ALU.mult), [sgk, puk], [("aT", kf)])
        for fc in range(NK):
            w, wk = self.wload(d[f"Wdown{l}"][fc], NKF, 128, (f"Wdown{l}", fc))
            ps, pk = self.ps.next()
            for kf in range(NKF):
                self.mm(ps[:], w[:, kf, :], aT[:, kf, :], kf == 0, kf == NKF - 1, [wk, ("aT", kf)], [pk])
            self.act(mixT[:, fc, :], ps[:], AF.Copy, [pk], [("mixT", fc)])
        self.norm_resid(l, 5, cond)
        xkeys = [("xT", k) for k in range(NK)]
        if not last:
            self.store(d["XT"].rearrange("(k p) t -> p k t", p=128)[:, :, ts], xT[:], reads=xkeys, writes=[("XT", bi)])
        else:
            ident = self.cst[:, C_ID:C_ID + 128]
            for tt in range(TB // 128):
                yt, ytk = self.yt.next()
                for k in range(NK):
                    ps, pk = self.ps.next()
                    self.tr(ps[:, 0:128], xT[:, k, tt * 128:(tt + 1) * 128], ident, [("xT", k), "cst"], [pk])
                    if k % 2 == 0:
                        self.act(yt[:, k * 128:(k + 1) * 128], ps[:, 0:128], AF.Copy, [pk], [ytk])
                    else:
                        self.dve(lambda e, yt=yt, ps=ps, k=k: e.tensor_copy(out=yt[:, k * 128:(k + 1) * 128], in_=ps[:, 0:128]), [pk], [ytk])
                g0 = t0 + tt * 128
                dst = d["ys"][g0:g0 + 128, :] if sample else d["yp"][g0 - c.T_S:g0 - c.T_S + 128, :]
                self.store(dst, yt[:], reads=[ytk], is_output=True)


A_SCALE = 0.125
B_SCALE = 96.0 ** -0.5
C_WINDOWS = (2, 4, 8, 16)


class BuilderM(BuilderC):
    def mixer_rings(self):
        pst = self.ps.tiles + self.psx.tiles
        self.sc = ARing("sc", [pst[i][:] for i in range(4)])
        self.accp = ARing("accp", [pst[i][:] for i in range(4, 8)])

    def mixer_setup(self, l):
        d = self.dram
        aF, aB = self.arenaF, self.arenaB
        rb = self.rowsb[l]
        self.sinkexp = aF[0:64, 11776:12288]
        self.maskP = aB[:, 23552:24064]
        self.maskN = aB[:, 24064:24576]
        self.wpbd = aB[:, 23296:23552].rearrange("p (c d) -> p c d", c=2)
        wpf = aF[:, 11520:11776].rearrange("p (c d) -> p c d", c=2)
        se, sek = self.small.next()
        self.act(se[:, 0:4], rb[:, R_SINK:R_SINK + 4], AF.Exp, [f"rowsb{l}"], [sek])
        for h in range(4):
            self.dve(lambda e, h=h: e.tensor_scalar(out=self.sinkexp[:, h * 128:(h + 1) * 128], in0=self.cst[0:64, C_ONE:C_ONE + 128], scalar1=se[0:64, h:h + 1], scalar2=None, op0=ALU.mult),
                     [sek, "cst"], ["sinkexp"])
            self.dve(lambda e, h=h: e.tensor_copy(out=self.maskP[:, h * 128:(h + 1) * 128], in_=self.cst[:, C_UI:C_UI + 128]), ["cst"], ["maskP"])
            self.dve(lambda e, h=h: e.tensor_copy(out=self.maskN[:, h * 128:(h + 1) * 128], in_=self.cst[:, C_LI:C_LI + 128]), ["cst"], ["maskN"])
        self.pool(lambda e: e.memset(wpf[:], 0.0), [], ["wpf"])
        for g in range(4):
            p0 = (g % 2) * 64
            self.load(wpf[p0:p0 + 64, g // 2, p0:p0 + 64], d["wpool"][l, g], writes=["wpf"])
        self.dve(lambda e: e.tensor_copy(out=self.wpbd[:], in_=wpf[:]), ["wpf"], ["wpbd"])

    def mixer_c(self, l):
        c, d = self.cfg, self.dram
        aF, aB = self.arenaF, self.arenaB
        L = 528
        X = aF[:, 0:2 * L].rearrange("p (c t) -> p c t", c=2)
        Pa = aF[:, 2 * L:3 * L]
        Pb = aF[:, 3 * L:4 * L]
        Y = aF[:, 4 * L:4 * L + 512]
        IC = aF[:, 5 * L:5 * L + 1024].rearrange("p (c t) -> p c t", c=2)
        Yb = aB[:, 18432:18944]
        vec = self.vecs[l]
        segs = []
        for s0 in range(0, c.T_S, 512):
            segs.append((0, c.T_S, s0, 512, "icS", s0))
        for s in range(c.NSEQ):
            segs.append((c.T_S + s * 256, 256, 0, 256, "icP", 0))
        for (base, T, s0, n, ictab, ic0) in segs:
            lo, hi = max(s0 - 8, 0), min(s0 + n + 8, T)
            self.dve(lambda e: e.memset(X[:], 0.0), [], ["X"])
            self.load(X[:, :, lo - (s0 - 8):hi - (s0 - 8)], d["CT"].rearrange("(c p) t -> p c t", p=128)[:, :, base + lo:base + hi], writes=["X"])
            self.load(IC[:, :, 0:n], d[ictab][:, :, ic0:ic0 + n], writes=["IC"])
            for ch in range(2):
                x = X[:, ch, :]
                self.dve(lambda e, x=x: e.tensor_tensor(out=Pa[:, 0:L - 1], in0=x[:, 0:L - 1], in1=x[:, 1:L], op=ALU.add), ["X"], ["Pa"])
                self.dve(lambda e: e.tensor_tensor(out=Pb[:, 0:L - 3], in0=Pa[:, 0:L - 3], in1=Pa[:, 2:L - 1], op=ALU.add), ["Pa"], ["Pb"])
                if ch == 0:
                    srcs = [(Pa, 2, "Pa"), (Pb, 4, "Pb")]
                else:
                    self.dve(lambda e: e.tensor_tensor(out=Pa[:, 0:L - 7], in0=Pb[:, 0:L - 7], in1=Pb[:, 4:L - 3], op=ALU.add), ["Pb"], ["Pa"])
                    self.dve(lambda e: e.tensor_tensor(out=Pb[:, 0:L - 15], in0=Pa[:, 0:L - 15], in1=Pa[:, 8:L - 7], op=ALU.add), ["Pa"], ["Pb"])
                    srcs = [(Pa, 8, "Pa"), (Pb, 16, "Pb")]
                for half, (Pw, w, pk_) in enumerate(srcs):
                    sl = slice(half * 64, half * 64 + 64)
                    o0 = 8 - w // 2
                    self.dve(lambda e, Pw=Pw, sl=sl, o0=o0, ch=ch, n=n: e.tensor_tensor(out=Y[sl, 0:n], in0=Pw[sl, o0:o0 + n], in1=IC[sl, ch, 0:n], op=ALU.mult), [pk_, "IC"], ["Y"])
                    self.dve(lambda e, sl=sl, x=x, n=n: e.tensor_tensor(out=Yb[sl, 0:n], in0=Y[sl, 0:n], in1=x[sl, 8:8 + n], op=ALU.subtract), ["Y", "X"], ["Yb"])
                ps, pk = self.sc.next()
                self.mm(ps[:, 0:n], self.wpbd[:, ch, :], Yb[:, 0:n], True, True, ["wpbd", "Yb"], [pk])
                o, ok = self.bf16r.next()
                self.dve(lambda e, o=o, ps=ps, ch=ch, n=n: e.tensor_scalar(out=o[:, 0:n], in0=ps[:, 0:n], scalar1=vec[:, V_CSC + ch:V_CSC + ch + 1], scalar2=None, op0=ALU.mult), [pk, f"vecs{l}"], [ok])
                self.store(d["BRT"][2, ch * 128:(ch + 1) * 128, base + s0:base + s0 + n], o[:, 0:n], reads=[ok], writes=[("BRT", 2)])
                yield

    def attn_a_qblock(self, keytiles, qsl_dst):
        d = self.dram
        pts = []
        for (kfn, vt, qfn, mask, rkeys) in keytiles:
            ps, pk = self.sc.next()
            for h in range(4):
                self.mm(ps[:, h * 128:(h + 1) * 128], kfn(h // 2), qfn(h), True, True, rkeys, [pk])
            pt, ptk = self.ptr.next()
            self.act(pt[:], ps[:], AF.Exp, [pk], [ptk], scale=A_SCALE)
            if mask is not None:
                self.dve(lambda e, pt=pt, mask=mask: e.tensor_tensor(out=pt[:], in0=pt[:], in1=mask, op=ALU.mult), [ptk, "maskP", "maskN"], [ptk])
            pts.append((pt, ptk, vt, rkeys))
        pso, pok = self.accp.next()
        for h in range(4):
            for j, (pt, ptk, vt, rkeys) in enumerate(pts):
                self.mm(pso[0:64, h * 128:(h + 1) * 128], vt[:, (h // 2) * 64:(h // 2) * 64 + 64], pt[:, h * 128:(h + 1) * 128], j == 0, j == len(pts) - 1, [ptk] + list(rkeys), [pok])
        psd, pdk = self.accp.next()
        for j, (pt, ptk, vt, rkeys) in enumerate(pts):
            self.mm(psd[0:64, :], self.onesb[:, 0:64], pt[:], j == 0, j == len(pts) - 1, [ptk, "onesb"], [pdk])
        den, dk = self.f32r.next()
        self.dve(lambda e, den=den, psd=psd: e.tensor_tensor(out=den[0:64, :], in0=psd[0:64, :], in1=self.sinkexp, op=ALU.add), [pdk, "sinkexp"], [dk])
        self.dve(lambda e, den=den: e.reciprocal(out=den[0:64, :], in_=den[0:64, :]), [dk], [dk])
        o, ok = self.bf16r.next()
        self.dve(lambda e, o=o, den=den, pso=pso: e.tensor_tensor(out=o[0:64, :], in0=pso[0:64, :], in1=den[0:64, :], op=ALU.mult), [pok, dk], [ok])
        self.store(d["BRT"][0].rearrange("(h d) t -> d h t", d=64)[:, :, qsl_dst], o[0:64, :].rearrange("p (h t) -> p h t", h=4), reads=[ok], writes=[("BRT", 0)])

    def mixer_a(self, l):
        c, d = self.cfg, self.dram
        aB = self.arenaB
        SEG = 1024
        qT = aB[0:64, 0:4096].rearrange("p (h t) -> p h t", h=4)
        qrT = aB[0:64, 4096:8192].rearrange("p (h t) -> p h t", h=4)
        kT = aB[0:64, 8192:8192 + 2 * 1280].rearrange("p (h t) -> p h t", h=2)
        vT = aB[:, 10752:10752 + 10 * 128].rearrange("p (j e) -> p j e", j=10)
        kcT = aB[0:64, 12032:12544].rearrange("p (h t) -> p h t", h=2)
        vc = aB[:, 12544:12800].rearrange("p (j e) -> p j e", j=2)
        self.ptr = ARing("ptr", [aB[:, 12800 + i * 512:12800 + (i + 1) * 512] for i in range(6)])
        AQT3 = d["AQT"].rearrange("(h d) t -> d h t", d=64)
        AQR3 = d["AQRT"].rearrange("(h d) t -> d h t", d=64)
        AKT3 = d["AKT"].rearrange("(h d) t -> d h t", d=64)
        AKC3 = d["AKCT"].rearrange("(h d) t -> d h t", d=64)
        self.load(kcT[:], AKC3, reads=[("AKCT", 0)], writes=["kcT"])
        self.store(vc[:], d["cav"][l].rearrange("(j p) e -> p j e", p=128), writes=["vc"])
        for s0 in range(0, c.T_S, SEG):
            n = min(SEG, c.T_S - s0)
            klo, khi = max(s0 - 128, 0), min(s0 + n + 128, c.T_S)
            self.load(qT[:, :, 0:n], AQT3[:, :, s0:s0 + n], reads=[("AQT", "all")], writes=["qT"])
            self.load(qrT[:, :, 0:n], AQR3[:, :, s0:s0 + n], reads=[("AQRT", "all")], writes=["qrT"])
            self.load(kT[:, :, 0:khi - klo], AKT3[:, :, klo:khi], reads=[("AKT", "all")], writes=["kT"])
            self.load(vT[:, 0:(khi - klo) // 128, :], d["AV"][klo:khi, :].rearrange("(j p) e -> p j e", p=128), reads=[("AV", "all")], writes=["vT"])
            for qb in range(n // 128):
                q0 = s0 + qb * 128
                tiles = []
                for dlt, mask in ((-1, self.maskP), (0, None), (1, self.maskN)):
                    k0 = q0 + dlt * 128
                    if k0 < 0 or k0 >= c.T_S:
                        continue
                    ko = k0 - klo
                    tiles.append((lambda kvh, ko=ko: kT[:, kvh, ko:ko + 128], vT[:, ko // 128, :],
                                  lambda h, qb=qb: qrT[:, h, qb * 128:(qb + 1) * 128], mask, ["kT", "vT", "qrT"]))
                for j in range(2):
                    tiles.append((lambda kvh, j=j: kcT[:, kvh, j * 128:(j + 1) * 128], vc[:, j, :],
                                  lambda h, qb=qb: qT[:, h, qb * 128:(qb + 1) * 128], None, ["kcT", "vc", "qT"]))
                self.attn_a_qblock(tiles, slice(q0, q0 + 128))
                yield
        for s in range(c.NSEQ):
            b0 = c.T_S + s * 256
            self.load(qT[:, :, 0:256], AQT3[:, :, b0:b0 + 256], reads=[("AQT", "all")], writes=["qT"])
            self.load(kT[:, :, 0:256], AKT3[:, :, b0:b0 + 256], reads=[("AKT", "all")], writes=["kT"])
            self.load(vT[:, 0:2, :], d["AV"][b0:b0 + 256, :].rearrange("(j p) e -> p j e", p=128), reads=[("AV", "all")], writes=["vT"])
            for qb in range(2):
                tiles = []
                for j in range(2):
                    tiles.append((lambda kvh, j=j: kT[:, kvh, j * 128:(j + 1) * 128], vT[:, j, :],
                                  lambda h, qb=qb: qT[:, h, qb * 128:(qb + 1) * 128], None, ["kT", "vT", "qT"]))
                self.attn_a_qblock(tiles, slice(b0 + qb * 128, b0 + (qb + 1) * 128))
                yield

    def mla_head(self, h, qsegs, ktiles, kT, vT, rk):
        d = self.dram
        for (qfn, n, dsl) in qsegs:
            pso, pok = self.accp.next()
            psd, pdk = self.accp.next()
            nt = len(ktiles)
            pend = None
            for j, (ko, vj, src) in enumerate(ktiles):
                ps, pk = self.sc.next()
                self.mm(ps[:, 0:n], kT[:, ko:ko + 128], qfn(src), True, True, rk, [pk])
                pt, ptk = self.ptr.next()
                self.act(pt[:, 0:n], ps[:, 0:n], AF.Exp, [pk], [ptk], scale=B_SCALE)
                if pend is not None:
                    jj, ppt, pptk, pvj = pend
                    self.mm(pso[0:64, 0:n], vT[:, pvj, :], ppt[:, 0:n], jj == 0, jj == nt - 1, [pptk] + rk, [pok])
                    self.mm(psd[0:64, 0:n], self.onesb[:, 0:64], ppt[:, 0:n], jj == 0, jj == nt - 1, [pptk, "onesb"], [pdk])
                pend = (j, pt, ptk, vj)
                yield
            jj, ppt, pptk, pvj = pend
            self.mm(pso[0:64, 0:n], vT[:, pvj, :], ppt[:, 0:n], jj == 0, jj == nt - 1, [pptk] + rk, [pok])
            self.mm(psd[0:64, 0:n], self.onesb[:, 0:64], ppt[:, 0:n], jj == 0, jj == nt - 1, [pptk, "onesb"], [pdk])
            den, dk = self.f32r.next()
            self.dve(lambda e, den=den, psd=psd, n=n: e.reciprocal(out=den[0:64, 0:n], in_=psd[0:64, 0:n]), [pdk], [dk])
            o, ok = self.bf16r.next()
            self.dve(lambda e, o=o, den=den, pso=pso, n=n: e.tensor_tensor(out=o[0:64, 0:n], in0=pso[0:64, 0:n], in1=den[0:64, 0:n], op=ALU.mult), [pok, dk], [ok])
            self.store(d["BRT"][1, h * 64:(h + 1) * 64, dsl], o[0:64, 0:n], reads=[ok], writes=[("BRT", 1)])

    def mixer_b(self, l):
        c, d = self.cfg, self.dram
        aB = self.arenaB
        T_S, NT = c.T_S, c.NT
        NKT = T_S // 128 + 2
        qL = aB[0:96, 0:T_S]
        qC = aB[0:96, 4096:4096 + T_S]
        kT = aB[0:96, 8192:8192 + T_S + 256]
        vT = aB[:, 12544:12544 + NKT * 64].rearrange("p (j e) -> p j e", e=64)
        base = 12544 + 34 * 64
        self.ptr = ARing("ptr", [aB[:, base + i * 512:base + (i + 1) * 512] for i in range(6)])
        rk = ["qL", "qC", "kT", "vT"]
        for h in range(4):
            self.load(qL, d["QLT"][h], reads=[("QLT", "all")], writes=["qL"])
            self.load(qC, d["QCT"][h, :, 0:T_S], reads=[("QCT", "all")], writes=["qC"])
            self.load(kT[:, 0:T_S], d["BKT"][h, :, 0:T_S], reads=[("BKT", "all")], writes=["kT"])
            self.load(kT[:, T_S:T_S + 256], d["BKT"][h, :, NT:NT + 256], reads=[("BKT", "ctx")], writes=["kT"])
            self.load(vT[:, 0:T_S // 128, :], d["BV"][0:T_S, h * 64:(h + 1) * 64].rearrange("(j p) e -> p j e", p=128), reads=[("BV", "all")], writes=["vT"])
            self.load(vT[:, T_S // 128:NKT, :], d["BV"][NT:NT + 256, h * 64:(h + 1) * 64].rearrange("(j p) e -> p j e", p=128), reads=[("BV", "ctx")], writes=["vT"])
            ktiles = [(j * 128, j, "lat") for j in range(T_S // 128)] + [(T_S + j * 128, T_S // 128 + j, "ctx") for j in range(2)]
            qsegs = []
            for q0 in range(0, T_S, 512):
                qsegs.append((lambda src, q0=q0: (qL if src == "lat" else qC)[:, q0:q0 + 512], 512, slice(q0, q0 + 512)))
            yield from self.mla_head(h, qsegs, ktiles, kT, vT, rk)
            self.load(qC[:, 0:c.T_P], d["QCT"][h, :, T_S:NT], reads=[("QCT", "all")], writes=["qC"])
            self.load(kT[:, 0:c.T_P], d["BKT"][h, :, T_S:NT], reads=[("BKT", "all")], writes=["kT"])
            self.load(vT[:, 0:c.T_P // 128, :], d["BV"][T_S:NT, h * 64:(h + 1) * 64].rearrange("(j p) e -> p j e", p=128), reads=[("BV", "all")], writes=["vT"])
            for s in range(c.NSEQ):
                ktiles = [(s * 256 + j * 128, s * 2 + j, "ctx") for j in range(2)]
                qsegs = [(lambda src, s=s: qC[:, s * 256:(s + 1) * 256], 256, slice(T_S + s * 256, T_S + (s + 1) * 256))]
                yield from self.mla_head(h, qsegs, ktiles, kT, vT, rk)


class BuilderD(BuilderM):
    def mixer_d1(self, l):
        c, d = self.cfg, self.dram
        aF = self.arenaF
        LX = 516
        xin = aF[:, 0:6 * LX].rearrange("p (c t) -> p c t", c=6)
        u = aF[:, 3096:3096 + 3072].rearrange("p (c t) -> p c t", c=6)
        sqf = aF[:, 6168:6168 + 512]
        rs = aF[:, 6680:6680 + 512]
        tm = ARing("tm", [aF[:, 7192 + i * 512:7192 + (i + 1) * 512] for i in range(2)])
        vec = self.vecs[l]
        blk = self.cst[:, C_BLK:C_BLK + 128]
        ident = self.cst[:, C_ID:C_ID + 128]
        DT3 = d["DTs"].rearrange("(c p) t -> p c t", p=128)
        segs = [(0, c.T_S, s0, 512) for s0 in range(0, c.T_S, 512)] + [(c.T_S + s * 256, 256, 0, 256) for s in range(c.NSEQ)]
        for (base, T, s0, n) in segs:
            lo, hi = max(s0 - 2, 0), min(s0 + n + 2, T)
            self.dve(lambda e: e.memset(xin[:], 0.0), [], ["xin"])
            self.load(xin[:, :, lo - (s0 - 2):hi - (s0 - 2)], DT3[:, :, base + lo:base + hi], writes=["xin"])
            for ch in range(6):
                w = lambda tap, ch=ch: vec[:, V_CONV + tap * 6 + ch:V_CONV + tap * 6 + ch + 1]
                self.dve(lambda e, ch=ch, n=n, w=w: e.tensor_scalar(out=u[:, ch, 0:n], in0=xin[:, ch, 0:n], scalar1=w(0), scalar2=None, op0=ALU.mult), ["xin", f"vecs{l}"], [("u", ch)])
                for tap in range(1, 5):
                    self.dve(lambda e, ch=ch, n=n, w=w, tap=tap: e.scalar_tensor_tensor(out=u[:, ch, 0:n], in0=xin[:, ch, tap:tap + n], scalar=w(tap), in1=u[:, ch, 0:n], op0=ALU.mult, op1=ALU.add),
                             ["xin", f"vecs{l}", ("u", ch)], [("u", ch)])
                self.act(u[:, ch, 0:n], u[:, ch, 0:n], AF.Silu, [("u", ch)], [("u", ch)])
                if ch < 4:
                    self.dve(lambda e, ch=ch, n=n: e.tensor_tensor(out=sqf[:, 0:n], in0=u[:, ch, 0:n], in1=u[:, ch, 0:n], op=ALU.mult), [("u", ch)], ["sqf"])
                    ps, pk = self.sc.next()
                    self.mm(ps[:, 0:n], blk, sqf[:, 0:n], True, True, ["sqf", "cst"], [pk])
                    self.act(rs[:, 0:n], ps[:, 0:n], AF.Sqrt, [pk], ["rs"], bias=self.eps_ap, scale=1.0)
                    self.dve(lambda e, n=n: e.reciprocal(out=rs[:, 0:n], in_=rs[:, 0:n]), ["rs"], ["rs"])
                    self.dve(lambda e, ch=ch, n=n: e.scalar_tensor_tensor(out=u[:, ch, 0:n], in0=u[:, ch, 0:n], scalar=(0.125 if ch < 2 else 1.0), in1=rs[:, 0:n], op0=ALU.mult, op1=ALU.mult),
                             [("u", ch), "rs"], [("u", ch)])
                yield
            tsl = slice(base + s0, base + s0 + n)
            self.store(d["QNT"].rearrange("(c p) t -> p c t", p=128)[:, :, tsl], u[:, 0:2, 0:n], reads=[("u", 0), ("u", 1)], writes=[("QNT", 0)])
            self.store(d["KNT"].rearrange("(c p) t -> p c t", p=128)[:, :, tsl], u[:, 2:4, 0:n], reads=[("u", 2), ("u", 3)], writes=[("KNT", 0)])
            for tt in range(n // 128):
                t, tk = tm.next()
                for j, ch in enumerate((2, 3, 4, 5)):
                    ps, pk = self.sc.next()
                    self.tr(ps[:, 0:128], u[:, ch, tt * 128:(tt + 1) * 128], ident, [("u", ch), "cst"], [pk])
                    self.act(t[:, j * 128:(j + 1) * 128], ps[:, 0:128], AF.Copy, [pk], [tk])
                g0 = base + s0 + tt * 128
                self.store(d["KN"][g0:g0 + 128, :], t[:, 0:256], reads=[tk], writes=[("KN", 0)])
                self.store(d["VV"][g0:g0 + 128, :], t[:, 256:512], reads=[tk], writes=[("VV", 0)])
                yield

    def mixer_d2(self, l):
        c, d = self.cfg, self.dram
        aF = self.arenaF
        cst = self.cst
        LI, LS, UI, US, ONE, ident = (cst[:, o:o + 128] for o in (C_LI, C_LS, C_UI, C_US, C_ONE, C_ID))
        rb = self.rowsb[l]
        off = [0]

        def carve(ncols, parts=128):
            a = aF[0:parts, off[0]:off[0] + ncols]
            off[0] += ncols
            return a
        S = [[carve(64, 64) for h in range(4)] for dr in range(2)]
        xflat = self.xT[:].rearrange("p k t -> p (k t)")
        xo = [0]

        def carve_x(ncols, parts=128):
            a = xflat[0:parts, xo[0]:xo[0] + ncols]
            xo[0] += ncols
            return a
        qk = ARing("qk", [carve_x(1024, 64).rearrange("p (h s t) -> p h s t", h=4, s=2) for _ in range(2)])
        knv = ARing("knv", [carve_x(512) for _ in range(2)])
        dz = ARing("dz", [carve_x(16) for _ in range(2)])
        gs = ARing("gs", [carve_x(64) for _ in range(2)])
        ealog = carve_x(8)
        O = ARing("O", [carve_x(256) for _ in range(2)])
        assert xo[0] <= 4096, xo[0]
        RU = []
        aBf = self.arenaB[:, 0:23296].bitcast(F32)
        offb = [0]

        def carve_b(ncols, parts=128):
            a_ = aBf[0:parts, offb[0]:offb[0] + ncols]
            offb[0] += ncols
            return a_
        for u_ in range(8):
            cv = carve if u_ < 4 else carve_b
            RU.append(dict(
                gL=ARing(f"gL{u_}", [cv(128)]), ET=ARing(f"ET{u_}", [cv(128)]), tA=ARing(f"tA{u_}", [cv(128) for _ in range(2)]),
                Pm=ARing(f"Pm{u_}", [cv(128) for _ in range(6)]), Yr=ARing(f"Yr{u_}", [cv(128) for _ in range(6)]),
                AqT=ARing(f"AqT{u_}", [cv(128)]), kd=ARing(f"kd{u_}", [cv(64)]), XU=ARing(f"XU{u_}", [cv(128)]),
                wT=ARing(f"wT{u_}", [cv(128, 64)]), vn=ARing(f"vn{u_}", [cv(64)]), qs=ARing(f"qs{u_}", [cv(64)])))
        assert offb[0] <= 11648, offb[0]
        assert off[0] <= 11520, off[0]
        self.act(ealog[:], rb[:, R_ALOG:R_ALOG + 8], AF.Exp, [f"rowsb{l}"], ["ealog"])

        seqs = [(0, c.T_S, True, 0)] + [(c.T_S + s * 256, 256, False, s) for s in range(c.NSEQ)]
        QN4 = d["QNT"].rearrange("(h dd) t -> dd h t", dd=64)
        KN4 = d["KNT"].rearrange("(h dd) t -> dd h t", dd=64)
        for (base, T, sample, sidx) in seqs:
            N = T // 128
            for dr in range(2):
                for h in range(4):
                    if sample:
                        self.load(S[dr][h], d["sdf" if dr == 0 else "sdb"][l, h], writes=[("S", dr, h)])
                    else:
                        self.dve(lambda e, dr=dr, h=h: e.memset(S[dr][h], 0.0), [], [("S", dr, h)])
            for i in range(N):
                gens = []
                fin = []
                for dr in range(2):
                    ci = i if dr == 0 else N - 1 - i
                    g0 = base + ci * 128
                    q, qkk = qk.next()
                    self.load(q[:, :, 0, :], QN4[:, :, g0:g0 + 128], reads=[("QNT", 0)], writes=[qkk])
                    self.load(q[:, :, 1, :], KN4[:, :, g0:g0 + 128], reads=[("KNT", 0)], writes=[qkk])
                    kv, kvk = knv.next()
                    self.load(kv[:, 0:256], d["KN"][g0:g0 + 128, :], reads=[("KN", 0)], writes=[kvk])
                    self.load(kv[:, 256:512], d["VV"][g0:g0 + 128, :], reads=[("VV", 0)], writes=[kvk])
                    z, zk = dz.next()
                    self.load(z[:], d["DZ"][g0:g0 + 128, 256:272], reads=[("DZ", 0)], writes=[zk])
                    g, gk = gs.next()
                    c0, c1 = dr * 4, dr * 4 + 4
                    self.act(g[:, 8:12], z[:, c0:c1], AF.Sigmoid, [zk], [gk])
                    self.dve(lambda e, g=g: e.tensor_scalar(out=g[:, 16:20], in0=g[:, 8:12], scalar1=-1.0, scalar2=None, op0=ALU.mult), [gk], [gk])
                    self.dve(lambda e, g=g, z=z, c0=c0, c1=c1: e.tensor_tensor(out=g[:, 48:52], in0=z[:, 8 + c0:8 + c1], in1=rb[:, R_DTB + c0:R_DTB + c1], op=ALU.add), [zk, f"rowsb{l}"], [gk])
                    self.act(g[:, 48:52], g[:, 48:52], AF.Exp, [gk], [gk])
                    self.act(g[:, 48:52], g[:, 48:52], AF.Ln, [gk], [gk], bias=1.0, scale=1.0)
                    self.dve(lambda e, g=g, c0=c0, c1=c1: e.scalar_tensor_tensor(out=g[:, 0:4], in0=g[:, 48:52], scalar=-1.0, in1=ealog[:, c0:c1], op0=ALU.mult, op1=ALU.mult), [gk, "ealog"], [gk])
                    ps, pk = self.sc.next()
                    self.mm(ps[:, 0:4], LI if dr == 0 else UI, g[:, 0:4], True, True, [gk, "cst"], [pk])
                    self.mm(ps[:, 4:8], ONE, g[:, 0:4], True, True, [gk, "cst"], [pk])
                    self.act(g[:, 24:28], ps[:, 0:4], AF.Exp, [pk], [gk])
                    self.act(g[:, 40:44], ps[:, 4:8], AF.Exp, [pk], [gk])
                    self.act(g[:, 52:60], ps[:, 0:8], AF.Copy, [pk], [gk])
                    self.dve(lambda e, g=g: e.tensor_tensor(out=g[:, 32:36], in0=g[:, 56:60], in1=g[:, 52:56], op=ALU.subtract), [gk], [gk])
                    self.act(g[:, 32:36], g[:, 32:36], AF.Exp, [gk], [gk])
                    otile, ok_ = O.next()
                    gens += [self.d2_unit(l, dr, h, q, qkk, kv, kvk, g, gk, S[dr][h], ("S", dr, h), otile, ok_, RU[dr * 4 + h], (LI, LS, UI, US, ident)) for h in range(4)]
                    fin.append((dr, g0, otile, ok_))
                drive(*gens)
                for (dr, g0, otile, ok_) in fin:
                    self.store(d["OF" if dr == 0 else "OB"][g0:g0 + 128, :], otile[:], reads=[ok_], writes=[("OFB", dr)])
            if not sample:
                for dr in range(2):
                    for h in range(4):
                        self.store(d["o_sf" if dr == 0 else "o_sb"][sidx, l, h], S[dr][h], reads=[("S", dr, h)], is_output=True)

    def d2_unit(self, l, dr, h, q, qkk, kv, kvk, g, gk, S, Sk, otile, ok_, R, consts):
        LI, LS, UI, US, ident = consts
        qT, kT = q[:, h, 0, :], q[:, h, 1, :]
        kn, v = kv[:, h * 64:(h + 1) * 64], kv[:, 256 + h * 64:256 + (h + 1) * 64]
        col = lambda o: g[:, o + h:o + h + 1]
        graw, beta, nbeta, eG, edG, etot = col(0), col(8), col(16), col(24), col(32), col(40)
        m_strict = LS if dr == 0 else US
        m_incl = LI if dr == 0 else UI
        gl, glk = R["gL"].next()
        self.dve(lambda e: e.tensor_scalar(out=gl, in0=(LI if dr == 0 else UI), scalar1=graw, scalar2=None, op0=ALU.mult), [gk, "cst"], [glk])
        ps, pk = self.sc.next()
        self.mm(ps[:, 0:128], US if dr == 0 else LS, gl, True, True, [glk, "cst"], [pk])
        et, etk = R["ET"].next()
        self.act(et, ps[:, 0:128], AF.Exp, [pk], [etk])
        yield
        psk, pkk = self.sc.next()
        self.mm(psk[:, 0:128], kT, kT, True, True, [qkk], [pkk])
        self.mm(psk[:, 128:256], kT, qT, True, True, [qkk], [pkk])
        ta, tak = R["tA"].next()
        self.dve(lambda e: e.tensor_tensor(out=ta, in0=psk[:, 0:128], in1=et, op=ALU.mult), [pkk, etk], [tak])
        p0t, p0tk = R["Pm"].next()
        self.dve(lambda e: e.scalar_tensor_tensor(out=p0t, in0=ta, scalar=nbeta, in1=m_strict, op0=ALU.mult, op1=ALU.mult), [tak, gk, "cst"], [p0tk])
        aq, aqk = R["AqT"].next()
        self.dve(lambda e: e.tensor_tensor(out=aq, in0=psk[:, 128:256], in1=et, op=ALU.mult), [pkk, etk], [aqk])
        self.dve(lambda e: e.tensor_tensor(out=aq, in0=aq, in1=m_incl, op=ALU.mult), [aqk, "cst"], [aqk])
        yield
        blk = self.cst[:, C_BLK:C_BLK + 128]
        ps, pk = self.sc.next()
        self.tr(ps[:, 0:128], p0t, ident, [p0tk, "cst"], [pk])
        p0f, p0fk = R["Pm"].next()
        self.act(p0f, ps[:, 0:128], AF.Copy, [pk], [p0fk])
        yield
        p0tb, p0tbk = R["Pm"].next()
        self.dve(lambda e: e.tensor_tensor(out=p0tb, in0=p0t, in1=blk, op=ALU.mult), [p0tk, "cst"], [p0tbk])
        p0b, p0bk = R["Pm"].next()
        self.dve(lambda e: e.tensor_tensor(out=p0b, in0=p0f, in1=blk, op=ALU.mult), [p0fk, "cst"], [p0bk])
        yt = R["Yr"].tiles
        nm = R["Yr"].name
        noff, noffk = yt[0], (nm, 0)
        rp, rpk = yt[1], (nm, 1)
        self.dve(lambda e: e.tensor_tensor(out=noff, in0=p0f, in1=p0b, op=ALU.subtract), [p0fk, p0bk], [noffk])
        self.act(rp[:, 0:64], v, AF.Copy, [kvk], [rpk])
        self.dve(lambda e: e.tensor_scalar(out=rp[:, 64:128], in0=kn, scalar1=eG, scalar2=None, op0=ALU.mult), [kvk, gk], [rpk])
        tcur, tck = yt[2], (nm, 2)
        self.dve(lambda e: e.tensor_tensor(out=yt[2], in0=p0tb, in1=ident, op=ALU.add), [p0tbk, "cst"], [(nm, 2)])
        yield
        pt_, ptk_, p_, pk_ = p0tb, p0tbk, p0b, p0bk
        for lev in range(5):
            ps2, pk2 = self.sc.next()
            self.mm(ps2[:, 0:128], pt_, p_, True, True, [pk_, ptk_], [pk2])
            np_, npk = R["Pm"].next()
            self.act(np_, ps2[:, 0:128], AF.Copy, [pk2], [npk])
            if lev < 4:
                ps1, pk1 = self.sc.next()
                self.mm(ps1[:, 0:128], p_, pt_, True, True, [pk_, ptk_], [pk1])
                npt, nptk = R["Pm"].next()
                self.dve(lambda e, npt=npt, ps1=ps1: e.tensor_copy(out=npt, in_=ps1[:, 0:128]), [pk1], [nptk])
                pt_, ptk_ = npt, nptk
            p_, pk_ = np_, npk
            yield
            psa, pka = self.sc.next()
            self.mm(psa[:, 0:128], p_, tcur, True, True, [pk_, tck], [pka])
            ni = 3 if tck[1] == 2 else 2
            tnew, tnk = yt[ni], (nm, ni)
            self.dve(lambda e, tcur=tcur, tnew=tnew, psa=psa: e.tensor_tensor(out=tnew, in0=tcur, in1=psa[:, 0:128], op=ALU.add), [tck, pka], [tnk])
            tcur, tck = tnew, tnk
            yield
        psy, pyk = self.sc.next()
        self.mm(psy[:, 0:128], tcur, rp, True, True, [tck, rpk], [pyk])
        ysb, ysk = yt[4], (nm, 4)
        self.act(ysb, psy[:, 0:128], AF.Copy, [pyk], [ysk])
        psq, pqk = self.sc.next()
        self.mm(psq[:, 0:128], noff, tcur, True, True, [noffk, tck], [pqk])
        mqT, mqk = R["Pm"].next()
        self.dve(lambda e: e.tensor_copy(out=mqT, in_=psq[:, 0:128]), [pqk], [mqk])
        yield
        psx_, pxk = self.sc.next()
        self.mm(psx_[:, 0:128], mqT, ysb, True, True, [mqk, ysk], [pxk])
        xs, xsk = R["tA"].next()
        self.dve(lambda e: e.tensor_tensor(out=xs, in0=ysb, in1=psx_[:, 0:128], op=ALU.add), [ysk, pxk], [xsk])
        yield
        y, yk = xs, xsk
        xu, xuk = R["XU"].next()
        self.dve(lambda e: e.tensor_scalar(out=xu, in0=y, scalar1=beta, scalar2=None, op0=ALU.mult), [yk, gk], [xuk])
        ps, pk = self.sc.next()
        self.tr(ps[0:64, 0:128], xu[:, 64:128], ident, [xuk, "cst"], [pk])
        wt, wtk = R["wT"].next()
        self.act(wt, ps[0:64, 0:128], AF.Copy, [pk], [wtk])
        yield
        kdt, kdk = R["kd"].next()
        self.dve(lambda e: e.tensor_scalar(out=kdt, in0=kn, scalar1=edG, scalar2=None, op0=ALU.mult), [kvk, gk], [kdk])
        ps, pk = self.sc.next()
        self.mm(ps[:, 0:64], wt, S, True, True, [wtk, Sk], [pk])
        self.mm(ps[:, 64:128], qT, S, True, True, [qkk, Sk], [pk])
        vnt, vnk = R["vn"].next()
        self.dve(lambda e: e.tensor_tensor(out=vnt, in0=xu[:, 0:64], in1=ps[:, 0:64], op=ALU.subtract), [xuk, pk], [vnk])
        qst, qsk = R["qs"].next()
        self.dve(lambda e: e.tensor_scalar(out=qst, in0=ps[:, 64:128], scalar1=eG, scalar2=None, op0=ALU.mult), [pk, gk], [qsk])
        yield
        ps2, pk2 = self.sc.next()
        self.mm(ps2[:, 0:64], aq, vnt, True, True, [aqk, vnk], [pk2])
        self.mm(ps2[0:64, 64:128], kdt, vnt, True, True, [kdk, vnk], [pk2])
        self.dve(lambda e: e.tensor_tensor(out=otile[:, h * 64:(h + 1) * 64], in0=qst, in1=ps2[:, 0:64], op=ALU.add), [qsk, pk2], [ok_])
        self.dve(lambda e: e.scalar_tensor_tensor(out=S, in0=S, scalar=g[0:64, 40 + h:41 + h], in1=ps2[0:64, 64:128], op0=ALU.mult, op1=ALU.add), [Sk, gk, pk2], [Sk])

    def mixer_d3(self, l):
        c, d = self.cfg, self.dram
        aF = self.arenaF
        rb = self.rowsb[l]
        ident = self.cst[:, C_ID:C_ID + 128]
        ofr = ARing("of", [aF[:, i * 256:(i + 1) * 256] for i in range(2)])
        obr = ARing("ob", [aF[:, 512 + i * 256:512 + (i + 1) * 256] for i in range(2)])
        zr = ARing("z", [aF[:, 1024 + i * 256:1024 + (i + 1) * 256] for i in range(2)])
        sqr = ARing("sqd", [aF[:, 1536 + i * 256:1536 + (i + 1) * 256] for i in range(2)])
        onr = ARing("on", [aF[:, 2048 + i * 256:2048 + (i + 1) * 256] for i in range(2)])
        for tt in range(c.NT // 128):
            g0 = tt * 128
            of, ofk = ofr.next(); ob, obk = obr.next(); z, zk = zr.next(); sq, sqk = sqr.next(); on, onk = onr.next()
            self.load(of, d["OF"][g0:g0 + 128, :], reads=[("OFB", 0)], writes=[ofk])
            self.load(ob, d["OB"][g0:g0 + 128, :], reads=[("OFB", 1)], writes=[obk])
            self.load(z, d["DZ"][g0:g0 + 128, 0:256], reads=[("DZ", 0)], writes=[zk])
            self.dve(lambda e, of=of, ob=ob: e.tensor_tensor(out=of, in0=of, in1=ob, op=ALU.add), [ofk, obk], [ofk])
            self.dve(lambda e, of=of, sq=sq: e.tensor_tensor(out=sq, in0=of, in1=of, op=ALU.mult), [ofk], [sqk])
            sm, smk = self.small.next()
            self.dve(lambda e, sm=sm, sq=sq: e.tensor_reduce(out=sm[:, 0:4], in_=sq.rearrange("p (h e) -> p h e", h=4), axis=AX.X, op=ALU.add), [sqk], [smk])
            self.act(sm[:, 4:8], sm[:, 0:4], AF.Sqrt, [smk], [smk], bias=self.eps_ap, scale=1.0 / 64)
            self.dve(lambda e, sm=sm: e.reciprocal(out=sm[:, 4:8], in_=sm[:, 4:8]), [smk], [smk])
            self.act(z, z, AF.Silu, [zk], [zk])
            for h in range(4):
                hs = slice(h * 64, (h + 1) * 64)
                self.dve(lambda e, of=of, on=on, sm=sm, hs=hs, h=h: e.scalar_tensor_tensor(out=on[:, hs], in0=of[:, hs], scalar=sm[:, 4 + h:5 + h], in1=rb[:, R_GNORM:R_GNORM + 64], op0=ALU.mult, op1=ALU.mult),
                         [ofk, smk, f"rowsb{l}"], [onk])
            self.dve(lambda e, on=on, z=z: e.tensor_tensor(out=on, in0=on, in1=z, op=ALU.mult), [onk, zk], [onk])
            for cc in range(2):
                ps, pk = self.sc.next()
                self.tr(ps[:, 0:128], on[:, cc * 128:(cc + 1) * 128], ident, [onk, "cst"], [pk])
                o, ok = self.bf16r.next()
                self.act(o[:, 0:128], ps[:, 0:128], AF.Copy, [pk], [ok])
                self.store(d["BRT"][3, cc * 128:(cc + 1) * 128, g0:g0 + 128], o[:, 0:128], reads=[ok], writes=[("BRT", 3)])


def drive(*gens):
    gens = list(gens)
    while gens:
        alive = []
        for g in gens:
            try:
                next(g)
                alive.append(g)
            except StopIteration:
                pass
        gens = alive


def build_program(cfg, debug=()):
    B = BuilderD(cfg, debug=debug)
    B.setup_eps()
    B.phase_init()
    B.P.barrier()
    B.phase_w(0)
    B.mixer_rings()
    for l in range(2):
        B.layer_setup(l)
        B.phase_ctx(l)
        for bi in range(cfg.NB):
            B.phase_a_block(l, bi)
        if l == 0:
            B.phase_w(1)
        B.P.barrier()
        B.mixer_setup(l)
        B.P.barrier()
        drive(B.mixer_c(l), B.mixer_a(l))
        B.P.barrier()
        drive(B.mixer_b(l), B.mixer_d1(l))
        B.P.barrier()
        B.mixer_d2(l)
        B.P.barrier()
        B.mixer_d3(l)
        B.P.barrier()
        for bi in range(cfg.NB):
            B.phase_c_block(l, bi, last=(l == 1))
        B.P.barrier()
    B.P.finish()
    return B


def run_cfg(cfg, inputs, n_cores=8):
    inp = {k: np.ascontiguousarray(np.asarray(v)) for k, v in inputs.items()}
    B = build_program(cfg)
    shared = host_prep_shared(cfg, inp)
    nb = inp["x_sample"].shape[0]
    in_maps = []
    for core in range(n_cores):
        m = dict(shared)
        b = (core * nb) // n_cores
        seqs = list(range(core * cfg.NSEQ, (core + 1) * cfg.NSEQ))
        m.update(host_prep(cfg, inp, b, seqs))
        in_maps.append(m)
    res = run_bass_kernel_spmd(B.nc, in_maps, core_ids=list(range(n_cores)))
    R = res.results
    f32 = np.float32
    y_prompt = np.concatenate([np.asarray(R[c]["yp"], f32).reshape(cfg.NSEQ, 256, D) for c in range(n_cores)], 0)
    per_b = n_cores // nb
    y_sample = np.stack([np.asarray(R[b * per_b]["ys"], f32) for b in range(nb)], 0)
    cat = lambda k, shp: np.concatenate([np.asarray(R[c][k], f32).reshape((cfg.NSEQ,) + shp) for c in range(n_cores)], 0)
    return (y_prompt, y_sample, cat("o_ak", (2, 256, 2, 64)), cat("o_av", (2, 256, 2, 64)), cat("o_ckv", (2, 256, 128)),
            cat("o_kr", (2, 256, 32)), cat("o_sf", (2, 4, 64, 64)), cat("o_sb", (2, 4, 64, 64)))


def kernel(**inputs):
    cfg = Cfg(T_S=4096, NSEQ=4)
    return run_cfg(cfg, inputs)
```

```python
import os
import numpy as np
import concourse.bass as bass
import concourse.mybir as mybir
from concourse.bass_utils import run_bass_kernel_spmd
from contextlib import ExitStack


F32 = mybir.dt.float32
BF16 = mybir.dt.bfloat16
I32 = mybir.dt.int32
AF = mybir.ActivationFunctionType
ALU = mybir.AluOpType
AX = mybir.AxisListType

ENGS = ("pe", "act", "dve", "pool", "sp")
SEM_ROT = 4000


class Prog:
    def __init__(self, nc, n_dma_sems=12):
        self.nc = nc
        self.es = ExitStack()
        self.ops = {e: [] for e in ENGS}
        self.cnt = {e: 0 for e in ENGS}
        self.cur_sem = {}
        self.seen = {e: {} for e in ENGS}
        self.res_w = {}
        self.res_r = {}
        self.sem_count = 0
        for e in ENGS:
            self.cur_sem[e] = self._new_sem()
        self.dma_pool = {}
        for e in ("sp", "pool", "act"):
            self.dma_pool[e] = [[self._new_sem(), 0] for _ in range(n_dma_sems)]
        self.dma_rr = {e: 0 for e in ("sp", "pool", "act")}
        self.out_events = []
        self._uid = 0

    def _new_sem(self):
        self.sem_count += 1
        return self.es.enter_context(self.nc.semaphore(f"s{self.sem_count}"))

    def sbuf(self, name, shape, dt):
        return self.es.enter_context(self.nc.sbuf_tensor(name, list(shape), dt))

    def psum(self, name, shape, dt=F32):
        return self.es.enter_context(self.nc.psum_tensor(name, list(shape), dt))

    def _need(self, eng, ev):
        if ev is None:
            return
        sem, val, _ = ev
        k = id(sem)
        if self.seen[eng].get(k, 0) >= val:
            return
        self.seen[eng][k] = val
        self.ops[eng].append(("wait", sem, val))

    def _deps(self, eng, reads, writes, pe_accum=False):
        for r in reads:
            self._need(eng, self.res_w.get(r))
        for w in writes:
            lw = self.res_w.get(w)
            if not (pe_accum and lw is not None and lw[2] == "pe" and eng == "pe"):
                self._need(eng, lw)
            for ev in self.res_r.get(w, ()):
                self._need(eng, ev)

    def _commit(self, ev, reads, writes):
        for r in reads:
            self.res_r.setdefault(r, []).append(ev)
            if len(self.res_r[r]) > 24:
                best = {}
                for s, v, e in self.res_r[r]:
                    if id(s) not in best or best[id(s)][1] < v:
                        best[id(s)] = (s, v, e)
                self.res_r[r] = list(best.values())
        for w in writes:
            self.res_w[w] = ev
            self.res_r[w] = []

    def op(self, eng, fn, reads=(), writes=(), pe_accum=False):
        self._deps(eng, reads, writes, pe_accum)
        if self.cnt[eng] >= SEM_ROT:
            self.cur_sem[eng] = self._new_sem()
            self.cnt[eng] = 0
        self.cnt[eng] += 1
        ev = (self.cur_sem[eng], self.cnt[eng], eng)
        self.ops[eng].append(("op", fn, ev[0], 1))
        self._commit(ev, reads, writes)
        return ev

    def dma(self, q, out, in_, reads=(), writes=(), is_output=False, **kw):
        pool = self.dma_pool[q]
        i = self.dma_rr[q]
        self.dma_rr[q] = (i + 1) % len(pool)
        slot = pool[i]
        sem = slot[0]
        if slot[1] > 0:
            self._need(q, (sem, slot[1], "dma"))
        self._deps(q, reads, writes)
        slot[1] += 16
        ev = (sem, slot[1], "dma")

        def fn(e, out=out, in_=in_, kw=kw):
            return e.dma_start(out=out, in_=in_, **kw)
        self.ops[q].append(("op", fn, sem, 16))
        self._commit(ev, reads, writes)
        if is_output:
            self.out_events.append(ev)
        return ev

    def barrier(self):
        evs = []
        for e in ENGS:
            if self.cnt[e] > 0:
                evs.append((self.cur_sem[e], self.cnt[e], e))
        for q in self.dma_pool:
            for sem, val in self.dma_pool[q]:
                if val > 0:
                    evs.append((sem, val, "dma"))
        for e in ENGS:
            for ev in evs:
                if ev[2] == e and ev[0] is self.cur_sem[e]:
                    continue
                self._need(e, ev)
        self.res_w.clear()
        self.res_r.clear()

    def finish(self):
        for ev in self.out_events:
            self._need("sp", ev)
        self.barrier()
        nc = self.nc
        emap = {"pe": "tensor", "act": "scalar", "dve": "vector", "pool": "gpsimd", "sp": "sync"}
        with nc.Block() as block:
            for e in ENGS:
                lst = self.ops[e]

                def body(h, lst=lst):
                    for item in lst:
                        if item[0] == "wait":
                            h.wait_ge(item[1], item[2])
                        else:
                            ins = item[1](h)
                            ins.then_inc(item[2], item[3])
                getattr(block, emap[e])(body)
        self.es.close()


D = 1024
NK = 8
TB = 512
EPS = 1e-6
D_FF = 2816
NKF = 22
CTX = 256

O_AQ, O_AK, O_AV, O_CQ, O_CKV, O_KR, O_C, O_DQKV, O_DZ, O_DB, O_DA, O_G = 0, 256, 384, 512, 704, 832, 864, 1120, 1888, 2144, 2152, 2160


def rope_perm(dim):
    q = dim // 4
    idx = np.arange(dim)
    perm = np.where((idx // q) % 2 == 0, idx + q, idx - q)
    sign = np.where((idx // q) % 2 == 0, -1.0, 1.0).astype(np.float32)
    return perm, sign


def rope_tables(T, dim):
    GRID_W = 64
    n_rows = T // GRID_W
    row = np.repeat(np.arange(n_rows), GRID_W).astype(np.float32)
    col = np.tile(np.arange(GRID_W), n_rows).astype(np.float32)
    nfreq = dim // 4
    inv = (np.float32(10000.0) ** (-np.arange(nfreq, dtype=np.float32) / np.float32(nfreq))).astype(np.float32)
    ang_r = row[:, None] * inv
    ang_c = col[:, None] * inv
    ang = np.concatenate([ang_r, ang_r, ang_c, ang_c], axis=-1).astype(np.float32)
    return np.cos(ang).astype(np.float32), np.sin(ang).astype(np.float32)


class Cfg:
    def __init__(self, T_S=4096, NSEQ=4):
        self.T_S, self.NSEQ = T_S, NSEQ
        self.T_P = NSEQ * 256
        self.NT = T_S + self.T_P
        self.NBS = T_S // TB
        self.NBP = self.T_P // TB
        self.NB = self.NBS + self.NBP


V_BMOD, V_GPRE1, V_GPOST1, V_GPRE2, V_GPOST2, V_GCQ, V_CSC, V_CONV = 0, 48, 56, 64, 72, 80, 82, 84
NV = 84 + 30
R_GCKV, R_GNORM, R_SINK, R_ALOG, R_DTB = 0, 128, 192, 196, 204
NR = 212
C_ID, C_LI, C_LS, C_UI, C_US, C_ONE, C_BLK = 0, 128, 256, 384, 512, 640, 768
NCONST = 896


def fm(v, nk):
    return np.ascontiguousarray(v.reshape(nk, 128).T)


def host_prep(cfg, inp, b_idx, seq_ids):
    f32 = np.float32
    m = {}
    m["xs"] = np.ascontiguousarray(inp["x_sample"][b_idx, :cfg.T_S])
    m["xp"] = np.ascontiguousarray(inp["x_prompt"][seq_ids].reshape(cfg.T_P, D))
    m["cak"] = np.ascontiguousarray(inp["cache_a_k"][b_idx].reshape(2, CTX, 128))
    m["cav"] = np.ascontiguousarray(inp["cache_a_v"][b_idx].reshape(2, CTX, 128))
    m["cckv"] = np.ascontiguousarray(inp["cache_b_ckv"][b_idx])
    m["ckr"] = np.ascontiguousarray(inp["cache_b_krope"][b_idx])
    m["sdf"] = np.ascontiguousarray(inp["state_d_fwd"][b_idx])
    m["sdb"] = np.ascontiguousarray(inp["state_d_bwd"][b_idx])
    c2 = np.stack([inp["c"][b_idx], inp["c_ctx"]], axis=1)
    m["c2T"] = np.ascontiguousarray(c2.reshape(NK, 128, 2).transpose(1, 0, 2))
    return m


def host_prep_shared(cfg, inp):
    f32 = np.float32
    m = {}
    w_in = inp["w_in"]
    pA, sA = rope_perm(64)
    pB, sB = rope_perm(32)
    aq = np.arange(O_AQ, O_AK)
    ak = np.arange(O_AK, O_AV)
    cq = np.arange(O_CQ, O_CKV)
    kr = np.arange(O_KR, O_C)
    cc = np.arange(O_C, O_DQKV)
    dq = np.arange(O_DQKV, O_DZ)
    aqp = (O_AQ + (np.arange(256) // 64) * 64 + pA[np.arange(256) % 64])
    akp = (O_AK + (np.arange(128) // 64) * 64 + pA[np.arange(128) % 64])
    krp = O_KR + pB
    cols = np.concatenate([aq, ak, cq[:128],
                           cq[128:], kr, krp, cc, dq[:128],
                           dq[128:640],
                           dq[640:], aqp, akp])
    assert cols.size == 2048
    m["wfm"] = np.ascontiguousarray(w_in[:, :, cols])
    tok = np.zeros((2, D, 1024), f32)
    tok[:, :, 0:128] = w_in[:, :, O_AK:O_AV]
    tok[:, :, 128:256] = w_in[:, :, O_AV:O_CQ]
    tok[:, :, 256:384] = w_in[:, :, O_CKV:O_KR]
    tok[:, :, 384:416] = w_in[:, :, O_KR:O_C]
    tok[:, :, 512:784] = w_in[:, :, O_DZ:O_G]
    m["wtok"] = tok
    m["wgate"] = np.ascontiguousarray(w_in[:, :, O_G:])
    m["wmod"] = inp["w_mod"]
    m["wbr"] = inp["w_br"]
    m["wo"] = inp["w_o"]
    ucols = []
    for n in range(11):
        for j in (2 * n, 2 * n + 1):
            ucols.append(np.arange(j * 128, (j + 1) * 128))
        for j in (2 * n, 2 * n + 1):
            ucols.append(D_FF + np.arange(j * 128, (j + 1) * 128))
    ucols = np.concatenate(ucols)
    m["wup"] = np.ascontiguousarray(inp["w_up"][:, :, ucols])
    m["wdown"] = inp["w_down"]
    wuq = inp["b_w_uq"]
    pcols = np.arange(384)
    h_, r_ = pcols // 96, pcols % 96
    pcols_p = np.where(r_ >= 64, h_ * 96 + 64 + pB[np.clip(r_ - 64, 0, 31)], pcols)
    wuq2 = np.zeros((2, 256, 768), f32)
    wuq2[:, :192, :384] = wuq
    wuq2[:, :192, 384:] = wuq[:, :, pcols_p]
    m["wuq2"] = wuq2
    wukv = inp["b_w_ukv"]
    ncols = np.concatenate([h * 128 + np.arange(64) for h in range(4)])
    vcols = np.concatenate([h * 128 + 64 + np.arange(64) for h in range(4)])
    m["wukv2"] = np.ascontiguousarray(np.concatenate([wukv[:, :, ncols], wukv[:, :, vcols]], axis=2))
    vecs = np.zeros((2, 128, NV), f32)
    rowsb = np.zeros((2, 128, NR), f32)
    for l in range(2):
        vecs[l, :, V_BMOD:V_BMOD + 48] = fm(inp["b_mod"][l], 48)
        vecs[l, :, V_GPRE1:V_GPRE1 + 8] = fm(inp["g_pre1"][l], 8)
        vecs[l, :, V_GPOST1:V_GPOST1 + 8] = fm(inp["g_post1"][l], 8)
        vecs[l, :, V_GPRE2:V_GPRE2 + 8] = fm(inp["g_pre2"][l], 8)
        vecs[l, :, V_GPOST2:V_GPOST2 + 8] = fm(inp["g_post2"][l], 8)
        gcq = np.zeros(256, f32)
        gcq[:192] = inp["b_g_cq"][l]
        vecs[l, :, V_GCQ:V_GCQ + 2] = fm(gcq, 2)
        vecs[l, :, V_CSC:V_CSC + 2] = fm(inp["c_scale"][l], 2)
        for tap in range(5):
            vecs[l, :, V_CONV + tap * 6:V_CONV + tap * 6 + 6] = fm(inp["d_conv"][l, tap], 6)
        rowsb[l, :, R_GCKV:R_GCKV + 128] = inp["b_g_ckv"][l][None, :]
        rowsb[l, :, R_GNORM:R_GNORM + 64] = inp["d_g_norm"][l][None, :]
        rowsb[l, :, R_SINK:R_SINK + 4] = inp["a_sink"][l][None, :]
        rowsb[l, :, R_ALOG:R_ALOG + 8] = inp["d_a_log"][l].reshape(8)[None, :]
        rowsb[l, :, R_DTB:R_DTB + 8] = inp["d_dt_bias"][l].reshape(8)[None, :]
    m["vecs"] = vecs
    m["rowsb"] = rowsb
    m["wpool"] = inp["c_w_pool"]
    cA, sA_ = rope_tables(cfg.T_S, 64)
    cB, sB_ = rope_tables(cfg.T_S, 32)
    ropeA = np.zeros((128, 2, cfg.T_S), f32)
    for p in range(128):
        ropeA[p, 0] = cA[:, p % 64]
        ropeA[p, 1] = sA_[:, p % 64] * sA[p % 64]
    ropeB = np.zeros((96, 2, cfg.T_S), f32)
    for p in range(32):
        for base in (0, 64):
            ropeB[base + p, 0] = cB[:, p]
            ropeB[base + p, 1] = sB_[:, p] * sB[p]
    m["ropeA"] = ropeA
    m["ropeB"] = ropeB

    def invcnt(T):
        t = np.arange(T)
        out = np.zeros((128, 2, T), f32)
        for g, w in enumerate((2, 4, 8, 16)):
            lo = np.clip(t - w // 2, 0, T)
            hi = np.clip(t + w - w // 2, 0, T)
            ic = (1.0 / (hi - lo).astype(f32)).astype(f32)
            out[(g % 2) * 64:(g % 2) * 64 + 64, g // 2, :] = ic[None, :]
        return out
    m["icS"] = invcnt(cfg.T_S)
    m["icP"] = invcnt(256)
    cst = np.zeros((128, NCONST), f32)
    i = np.arange(128)
    cst[:, C_ID:C_ID + 128] = (i[:, None] == i[None, :])
    cst[:, C_LI:C_LI + 128] = (i[:, None] <= i[None, :])
    cst[:, C_LS:C_LS + 128] = (i[:, None] < i[None, :])
    cst[:, C_UI:C_UI + 128] = (i[:, None] >= i[None, :])
    cst[:, C_US:C_US + 128] = (i[:, None] > i[None, :])
    cst[:, C_ONE:C_ONE + 128] = 1.0
    cst[:, C_BLK:C_BLK + 128] = ((i[:, None] // 64) == (i[None, :] // 64))
    m["consts"] = cst
    return m


class Ring:
    def __init__(self, P, name, n, shape, dt, psum=False):
        self.name, self.n, self.i = name, n, 0
        self.tiles = [(P.psum if psum else P.sbuf)(f"{name}{i}", shape, dt) for i in range(n)]

    def next(self):
        t, k = self.tiles[self.i], (self.name, self.i)
        self.i = (self.i + 1) % self.n
        return t, k


class ARing:
    def __init__(self, name, aps):
        self.name, self.tiles, self.n, self.i = name, aps, len(aps), 0

    def next(self):
        t, k = self.tiles[self.i], (self.name, self.i)
        self.i = (self.i + 1) % self.n
        return t, k


class Builder:
    def __init__(self, cfg, debug=(), debug_in=()):
        self.cfg = cfg
        self.debug = set(debug)
        self.debug_in = set(debug_in)
        nc = bass.Bass("TRN2", target_bir_lowering=False)
        self.nc = nc
        self.P = Prog(nc, n_dma_sems=10)
        self.dram = {}
        self.declare_io()
        self.alloc()

    def din(self, name, shape, dt=F32):
        self.dram[name] = self.nc.dram_tensor(name, list(shape), dt, kind="ExternalInput").ap()
        return self.dram[name]

    def dout(self, name, shape, dt=F32):
        self.dram[name] = self.nc.dram_tensor(name, list(shape), dt, kind="ExternalOutput").ap()
        return self.dram[name]

    def dscr(self, name, shape, dt):
        kind = "ExternalOutput" if name in self.debug else ("ExternalInput" if name in self.debug_in else "Internal")
        self.dram[name] = self.nc.dram_tensor(name, list(shape), dt, kind=kind).ap()
        return self.dram[name]

    def declare_io(self):
        c = self.cfg
        self.din("xs", [c.T_S, D]); self.din("xp", [c.T_P, D])
        self.din("cak", [2, CTX, 128]); self.din("cav", [2, CTX, 128])
        self.din("cckv", [2, CTX, 128]); self.din("ckr", [2, CTX, 32])
        self.din("sdf", [2, 4, 64, 64]); self.din("sdb", [2, 4, 64, 64])
        self.din("c2T", [128, NK, 2])
        self.din("wfm", [2, D, 2048]); self.din("wtok", [2, D, 1024]); self.din("wgate", [2, D, 4096])
        self.din("wmod", [2, D, 6144]); self.din("wbr", [2, 4, 256, D]); self.din("wo", [2, D, D])
        self.din("wup", [2, D, 2 * D_FF]); self.din("wdown", [2, D_FF, D])
        self.din("wuq2", [2, 256, 768]); self.din("wukv2", [2, 128, 512])
        self.din("vecs", [2, 128, NV]); self.din("rowsb", [2, 128, NR]); self.din("wpool", [2, 4, 64, 64])
        self.din("ropeA", [128, 2, c.T_S]); self.din("ropeB", [96, 2, c.T_S])
        self.din("icS", [128, 2, c.T_S]); self.din("icP", [128, 2, 256]); self.din("consts", [128, NCONST])
        self.dout("ys", [c.T_S, D]); self.dout("yp", [c.T_P, D])
        self.dout("o_ak", [c.NSEQ, 2, 256, 128]); self.dout("o_av", [c.NSEQ, 2, 256, 128])
        self.dout("o_ckv", [c.NSEQ, 2, 256, 128]); self.dout("o_kr", [c.NSEQ, 2, 256, 32])
        self.dout("o_sf", [c.NSEQ, 2, 4, 64, 64]); self.dout("o_sb", [c.NSEQ, 2, 4, 64, 64])
        for l in range(2):
            self.dscr(f"Wfm{l}", [4, 128, NK, 512], BF16)
            self.dscr(f"Wtok{l}", [2, 128, NK, 512], BF16)
            self.dscr(f"Wgate{l}", [8, 128, NK, 512], BF16)
            self.dscr(f"Wmod{l}", [12, 128, NK, 512], BF16)
            self.dscr(f"Wbr{l}", [4, 128, 2, 1024], BF16)
            self.dscr(f"Wo{l}", [2, 128, NK, 512], BF16)
            self.dscr(f"Wup{l}", [11, 128, NK, 512], BF16)
            self.dscr(f"Wdown{l}", [8, 128, NKF, 128], BF16)
        NT = c.NT
        self.dscr("XT", [D, NT], F32)
        self.dscr("HT", [D, NT], BF16)
        self.dscr("AQT", [256, NT], BF16)
        self.dscr("AQRT", [256, c.T_S], BF16)
        self.dscr("AKT", [128, NT], BF16)
        self.dscr("AKCT", [128, CTX], BF16)
        self.dscr("AV", [NT, 128], BF16)
        self.dscr("QCT", [4, 96, NT], BF16)
        self.dscr("QLT", [4, 96, c.T_S], BF16)
        self.dscr("BKT", [4, 96, NT + CTX], BF16)
        self.dscr("BV", [NT + CTX, 256], BF16)
        self.dscr("CT", [256, NT], F32)
        self.dscr("DTs", [768, NT], F32)
        self.dscr("DZ", [NT, 272], F32)
        self.dscr("BRT", [4, 256, NT], BF16)
        for nm in ("QNT", "KNT"):
            self.dscr(nm, [256, NT], F32)
        for nm in ("KN", "VV", "OF", "OB"):
            self.dscr(nm, [NT, 256], F32)

    def alloc(self):
        P = self.P
        self.cst = P.sbuf("cst", [128, NCONST], F32)
        self.onesb = P.sbuf("onesb", [128, 128], BF16)
        self.identb = P.sbuf("identb", [128, 128], BF16)
        self.usb = P.sbuf("usb", [128, 128], BF16)
        self.lsb = P.sbuf("lsb", [128, 128], BF16)
        self.vecs = [P.sbuf(f"vecs{l}", [128, NV], F32) for l in range(2)]
        self.rowsb = [P.sbuf(f"rowsb{l}", [128, NR], F32) for l in range(2)]
        self.modv = P.sbuf("modv", [128, 48, 2], F32)
        self.gms = [P.sbuf(f"gm{l}", [128, 6, 8, 2], F32) for l in range(2)]
        self.c2 = P.sbuf("c2", [128, NK, 2], F32)
        self.c2b = P.sbuf("c2b", [128, NK, 2], BF16)
        self.wuq = P.sbuf("wuq", [128, 2, 768], BF16)
        self.wukv = P.sbuf("wukv", [128, 512], BF16)
        self.wring = Ring(P, "wb", 6, [128, 4096], BF16)
        self.ps = Ring(P, "ps", 6, [128, 512], F32, psum=True)
        self.psx = Ring(P, "psx", 2, [128, 512], F32, psum=True)
        self.xT = P.sbuf("xT", [128, NK, TB], F32)
        self.sq = P.sbuf("sq", [128, NK, TB], BF16)
        self.hT = P.sbuf("hT", [128, NK, TB], BF16)
        self.rstd = P.sbuf("rstd", [128, TB], F32)
        self.f32r = Ring(P, "f32r", 5, [128, TB], F32)
        self.bf16r = Ring(P, "bf16r", 6, [128, TB], BF16)
        self.small = Ring(P, "small", 8, [128, 8], F32)
        self.arenaF = P.sbuf("arenaF", [128, 12288], F32)
        self.arenaB = P.sbuf("arenaB", [128, 24576], BF16)
        aF, aB = self.arenaF, self.arenaB
        self.cqs = aF[:, 0:1024].rearrange("p (k t) -> p k t", k=2)
        self.tab = aF[:, 1024:2048].rearrange("p (k t) -> p k t", k=2)
        self.tabB = aF[0:96, 2048:3072].rearrange("p (k t) -> p k t", k=2)
        self.tokf = ARing("tokf", [aF[:, 3072 + i * 512:3072 + (i + 1) * 512] for i in range(3)])
        self.xin = ARing("xin", [aF[:, 4608 + i * 1024:4608 + (i + 1) * 1024] for i in range(2)])
        self.cqn = aB[:, 0:1024].rearrange("p (k t) -> p k t", k=2)
        self.ckvnT = aB[:, 1024:1536]
        self.accT = aF[:, 0:4096].rearrange("p (k t) -> p k t", k=NK)
        self.mixT = aF[:, 4096:8192].rearrange("p (k t) -> p k t", k=NK)
        self.yt = ARing("yt", [aF[:, 8192 + i * 1024:8192 + (i + 1) * 1024] for i in range(2)])
        self.gT = aB[:, 0:4096].rearrange("p (k t) -> p k t", k=NK)
        self.brT = aB[:, 4096:8192].rearrange("p (m c t) -> p m c t", m=4, c=2)
        self.aT = aB[:, 8192:8192 + NKF * TB].rearrange("p (k t) -> p k t", k=NKF)

    def mm(self, out, lhsT, rhs, start, stop, reads, writes):
        return self.P.op("pe", lambda e: e.matmul(out, lhsT=lhsT, rhs=rhs, start=start, stop=stop), reads, writes, pe_accum=True)

    def tr(self, out, in_, ident, reads, writes):
        return self.P.op("pe", lambda e: e.transpose(out, in_, ident), reads, writes, pe_accum=True)

    def act(self, out, in_, func, reads, writes, **kw):
        return self.P.op("act", lambda e: e.activation(out=out, in_=in_, func=func, **kw), reads, writes)

    def dve(self, fn, reads, writes):
        return self.P.op("dve", fn, reads, writes)

    def pool(self, fn, reads, writes):
        return self.P.op("pool", fn, reads, writes)

    def load(self, out, in_, reads=(), writes=()):
        return self.P.dma("sp", out, in_, reads=reads, writes=writes)

    def store(self, out, in_, reads=(), writes=(), is_output=False):
        return self.P.dma("pool", out, in_, reads=reads, writes=writes, is_output=is_output)

    def wload(self, dram_panel, kc, pc, rkey):
        t, k = self.wring.next()
        v = t[:, 0:kc * pc].rearrange("p (k c) -> p k c", k=kc)
        self.load(v, dram_panel, reads=[rkey], writes=[k])
        return v, k

    def phase_init(self):
        d = self.dram
        self.load(self.cst[:], d["consts"], writes=["cst"])
        for l in range(2):
            self.load(self.vecs[l][:], d["vecs"][l], writes=[f"vecs{l}"])
            self.load(self.rowsb[l][:], d["rowsb"][l], writes=[f"rowsb{l}"])
        self.load(self.c2[:], d["c2T"], writes=["c2"])
        self.dve(lambda e: e.tensor_copy(out=self.onesb[:], in_=self.cst[:, C_ONE:C_ONE + 128]), ["cst"], ["onesb"])
        self.dve(lambda e: e.tensor_copy(out=self.identb[:], in_=self.cst[:, C_ID:C_ID + 128]), ["cst"], ["identb"])
        self.dve(lambda e: e.tensor_copy(out=self.usb[:], in_=self.cst[:, C_US:C_US + 128]), ["cst"], ["usb"])
        self.dve(lambda e: e.tensor_copy(out=self.lsb[:], in_=self.cst[:, C_LS:C_LS + 128]), ["cst"], ["lsb"])
        self.act(self.c2b[:], self.c2[:], AF.Silu, ["c2"], ["c2b"])

    def phase_w(self, l):
        d = self.dram

        def cast(dst, src2d, n, pc, key):
            srcv = src2d.rearrange("(k p) c -> p k c", p=128)
            for i in range(n):
                self.store(dst[i], srcv[:, :, i * pc:(i + 1) * pc], writes=[(key, i)])
        cast(d[f"Wmod{l}"], d["wmod"][l], 12, 512, f"Wmod{l}")
        cast(d[f"Wfm{l}"], d["wfm"][l], 4, 512, f"Wfm{l}")
        cast(d[f"Wtok{l}"], d["wtok"][l], 2, 512, f"Wtok{l}")
        cast(d[f"Wgate{l}"], d["wgate"][l], 8, 512, f"Wgate{l}")
        for m_ in range(4):
            self.store(d[f"Wbr{l}"][m_], d["wbr"][l, m_].rearrange("(k p) c -> p k c", p=128), writes=[(f"Wbr{l}", m_)])
        cast(d[f"Wo{l}"], d["wo"][l], 2, 512, f"Wo{l}")
        cast(d[f"Wup{l}"], d["wup"][l], 11, 512, f"Wup{l}")
        cast(d[f"Wdown{l}"], d["wdown"][l], 8, 128, f"Wdown{l}")

    def layer_setup(self, l):
        d, P = self.dram, self.P
        self.store(self.wuq[:], d["wuq2"][l].rearrange("(k p) c -> p k c", p=128), writes=["wuq"])
        self.store(self.wukv[:], d["wukv2"][l], writes=["wukv"])
        ps, pk = self.psx.next()
        for n in range(12):
            w, wk = self.wload(d[f"Wmod{l}"][n], NK, 512, (f"Wmod{l}", n))
            for j in range(4):
                fc = n * 4 + j
                for k in range(NK):
                    self.mm(ps[:, fc * 2:fc * 2 + 2], w[:, k, j * 128:(j + 1) * 128], self.c2b[:, k, :],
                            k == 0, k == NK - 1, [wk, "c2b"], [pk])
        vec = self.vecs[l]
        for c_ in range(2):
            self.dve(lambda e, c_=c_: e.tensor_tensor(out=self.modv[:, :, c_], in0=ps[:, 0:96].rearrange("p (f c) -> p f c", c=2)[:, :, c_],
                                                      in1=vec[:, V_BMOD:V_BMOD + 48], op=ALU.add), [pk, f"vecs{l}"], ["modv"])
        gm = self.gms[l]
        for c_ in range(2):
            mv = lambda i, c_=c_: self.modv[:, i * 8:(i + 1) * 8, c_]
            self.dve(lambda e, c_=c_, mv=mv: e.scalar_tensor_tensor(out=gm[:, 0, :, c_], in0=mv(1), scalar=1.0, in1=vec[:, V_GPRE1:V_GPRE1 + 8], op0=ALU.add, op1=ALU.mult), ["modv", f"vecs{l}"], [f"gm{l}"])
            self.dve(lambda e, c_=c_, mv=mv: e.tensor_copy(out=gm[:, 1, :, c_], in_=mv(0)), ["modv"], [f"gm{l}"])
            self.dve(lambda e, c_=c_, mv=mv: e.tensor_tensor(out=gm[:, 2, :, c_], in0=mv(2), in1=vec[:, V_GPOST1:V_GPOST1 + 8], op=ALU.mult), ["modv", f"vecs{l}"], [f"gm{l}"])
            self.dve(lambda e, c_=c_, mv=mv: e.scalar_tensor_tensor(out=gm[:, 3, :, c_], in0=mv(4), scalar=1.0, in1=vec[:, V_GPRE2:V_GPRE2 + 8], op0=ALU.add, op1=ALU.mult), ["modv", f"vecs{l}"], [f"gm{l}"])
            self.dve(lambda e, c_=c_, mv=mv: e.tensor_copy(out=gm[:, 4, :, c_], in_=mv(3)), ["modv"], [f"gm{l}"])
            self.dve(lambda e, c_=c_, mv=mv: e.tensor_tensor(out=gm[:, 5, :, c_], in0=mv(5), in1=vec[:, V_GPOST2:V_GPOST2 + 8], op=ALU.mult), ["modv", f"vecs{l}"], [f"gm{l}"])

    def fm_rstd(self, src, skey, nk, nfeat, out_rstd, okey, krows=None):
        ps, pk = self.psx.next()
        for k in range(nk):
            rows = 128 if krows is None else krows[k]
            self.act(self.sq[0:rows, k, :], src[0:rows, k, :], AF.Square, [skey], [("sq", k)])
            self.mm(ps[:], self.onesb[0:rows, :], self.sq[0:rows, k, :], k == 0, k == nk - 1, [("sq", k), "onesb"], [pk])
        self.act(out_rstd, ps[:], AF.Sqrt, [pk], [okey], bias=self.eps_ap, scale=1.0 / nfeat)
        self.dve(lambda e: e.reciprocal(out=out_rstd, in_=out_rstd), [okey], [okey])

    def blk_info(self, bi):
        c = self.cfg
        t0 = bi * TB
        sample = bi < c.NBS
        return t0, sample, (0 if sample else 1)

    def phase_a_block(self, l, bi, load_x=True):
        c, d, P = self.cfg, self.dram, self.P
        t0, sample, cond = self.blk_info(bi)
        rope = sample
        ts = slice(t0, t0 + TB)
        xT = self.xT
        if l == 0:
            for tt in range(TB // 128):
                xi, xk = self.xin.next()
                src = d["xs"][t0 + tt * 128:t0 + (tt + 1) * 128, :] if sample else d["xp"][t0 - c.T_S + tt * 128:t0 - c.T_S + (tt + 1) * 128, :]
                self.load(xi[:], src, writes=[xk])
                for k in range(NK):
                    ps, pk = self.ps.next()
                    self.tr(ps[:, 0:128], xi[:, k * 128:(k + 1) * 128], self.cst[:, C_ID:C_ID + 128], [xk, "cst"], [pk])
                    if k % 2 == 0:
                        self.act(xT[:, k, tt * 128:(tt + 1) * 128], ps[:, 0:128], AF.Copy, [pk], [("xT", k)])
                    else:
                        self.dve(lambda e, k=k, ps=ps, tt=tt: e.tensor_copy(out=xT[:, k, tt * 128:(tt + 1) * 128], in_=ps[:, 0:128]), [pk], [("xT", k)])
            self.store(d["XT"].rearrange("(k p) t -> p k t", p=128)[:, :, ts], xT[:], reads=[("xT", k) for k in range(NK)], writes=[("XT", bi)])
        elif load_x:
            self.load(xT[:], d["XT"].rearrange("(k p) t -> p k t", p=128)[:, :, ts], reads=[("XT", bi)], writes=[("xT", k) for k in range(NK)])
        xkeys = [("xT", k) for k in range(NK)]
        ps, pk = self.psx.next()
        for k in range(NK):
            self.act(self.sq[:, k, :], xT[:, k, :], AF.Square, [("xT", k)], [("sq", k)])
            self.mm(ps[:], self.onesb[:], self.sq[:, k, :], k == 0, k == NK - 1, [("sq", k), "onesb"], [pk])
        self.act(self.rstd[:], ps[:], AF.Sqrt, [pk], ["rstd"], bias=self.eps_ap, scale=1.0 / D)
        self.dve(lambda e: e.reciprocal(out=self.rstd[:], in_=self.rstd[:]), ["rstd"], ["rstd"])
        for k in range(NK):
            t, tk = self.f32r.next()
            self.dve(lambda e, k=k, t=t: e.tensor_tensor(out=t[:], in0=xT[:, k, :], in1=self.rstd[:], op=ALU.mult), [("xT", k), "rstd"], [tk])
            if True:
                self.dve(lambda e, k=k, t=t: e.tensor_scalar(out=self.hT[:, k, :], in0=t[:], scalar1=self.gms[l][:, 0, k, cond:cond + 1], scalar2=self.gms[l][:, 1, k, cond:cond + 1], op0=ALU.mult, op1=ALU.add), [tk, f"gm{l}"], [("hT", k)])
            else:
                self.act(self.hT[:, k, :], t[:], AF.Identity, [tk], [("hT", k)], scale=self.gms[l][:, 0, k, cond:cond + 1], bias=self.gms[l][:, 1, k, cond:cond + 1])
        hkeys = [("hT", k) for k in range(NK)]
        self.store(d["HT"].rearrange("(k p) t -> p k t", p=128)[:, :, ts], self.hT[:], reads=hkeys, writes=[("HT", bi)])
        if rope:
            self.load(self.tab[:], d["ropeA"][:, :, ts], writes=["tab"])
            self.load(self.tabB[:], d["ropeB"][:, :, ts], writes=["tabB"])
        hT = self.hT

        def proj_fm(w, wk, c0, m):
            ps, pk = self.ps.next()
            for k in range(NK):
                self.mm(ps[0:m, :], w[:, k, c0:c0 + m], hT[:, k, :], k == 0, k == NK - 1, [wk, ("hT", k)], [pk])
            return ps, pk

        def evac_store(ps, pk, m, dst, dt, eng="act", wkey=None):
            ring = self.bf16r if dt == BF16 else self.f32r
            t, tk = ring.next()
            if eng == "act":
                self.act(t[0:m, :], ps[0:m, :], AF.Copy, [pk], [tk])
            else:
                self.dve(lambda e: e.tensor_copy(out=t[0:m, :], in_=ps[0:m, :]), [pk], [tk])
            self.store(dst, t[0:m, :], reads=[tk], writes=[wkey] if wkey else [])

        def rope_store(ps, pk, psp, ppk, m, tabt, tabk, dsts, wkey, p0=0):
            a1, ak1 = self.f32r.next()
            a2, ak2 = self.f32r.next()
            o, ok = self.bf16r.next()
            sl = slice(p0, p0 + m)
            self.act(a1[sl, :], ps[sl, :], AF.Copy, [pk], [ak1])
            self.act(a2[sl, :], psp[sl, :], AF.Copy, [ppk], [ak2])
            self.dve(lambda e: e.tensor_tensor(out=a1[sl, :], in0=a1[sl, :], in1=tabt[sl, 0, :], op=ALU.mult), [ak1, tabk], [ak1])
            self.dve(lambda e: e.tensor_tensor(out=a2[sl, :], in0=a2[sl, :], in1=tabt[sl, 1, :], op=ALU.mult), [ak2, tabk], [ak2])
            self.dve(lambda e: e.tensor_tensor(out=o[sl, :], in0=a1[sl, :], in1=a2[sl, :], op=ALU.add), [ak1, ak2], [ok])
            for dst in dsts:
                self.store(dst, o[sl, :], reads=[ok], writes=[wkey])
            return o, ok

        Wfm = d[f"Wfm{l}"]
        w0, wk0 = self.wload(Wfm[0], NK, 512, (f"Wfm{l}", 0))
        w1, wk1 = self.wload(Wfm[1], NK, 512, (f"Wfm{l}", 1))
        w3 = wk3 = None
        if rope:
            w3, wk3 = self.wload(Wfm[3], NK, 512, (f"Wfm{l}", 3))
        for j in range(2):
            ps, pk = proj_fm(w0, wk0, j * 128, 128)
            evac_store(ps, pk, 128, d["AQT"][j * 128:(j + 1) * 128, ts], BF16, "act", ("AQT", bi))
            if rope:
                psp, ppk = proj_fm(w3, wk3, 128 + j * 128, 128)
                rope_store(ps, pk, psp, ppk, 128, self.tab, "tab", [d["AQRT"][j * 128:(j + 1) * 128, ts]], ("AQRT", bi))
        ps, pk = proj_fm(w0, wk0, 256, 128)
        if rope:
            psp, ppk = proj_fm(w3, wk3, 384, 128)
            rope_store(ps, pk, psp, ppk, 128, self.tab, "tab", [d["AKT"][:, ts]], ("AKT", bi))
        else:
            evac_store(ps, pk, 128, d["AKT"][:, ts], BF16, "act", ("AKT", bi))
        ps, pk = proj_fm(w0, wk0, 384, 128)
        self.act(self.cqs[:, 0, :], ps[:], AF.Copy, [pk], [("cqs", 0)])
        ps, pk = proj_fm(w1, wk1, 0, 64)
        self.act(self.cqs[0:64, 1, :], ps[0:64, :], AF.Copy, [pk], [("cqs", 1)])
        pss, psk = self.psx.next()
        for k, rows in ((0, 128), (1, 64)):
            self.act(self.sq[0:rows, k, :], self.cqs[0:rows, k, :], AF.Square, [("cqs", k)], [("sq", k)])
            self.mm(pss[:], self.onesb[0:rows, :], self.sq[0:rows, k, :], k == 0, k == 1, [("sq", k), "onesb"], [psk])
        rq, rqk = self.f32r.next()
        self.act(rq[:], pss[:], AF.Sqrt, [psk], [rqk], bias=self.eps_ap, scale=1.0 / 192)
        self.dve(lambda e: e.reciprocal(out=rq[:], in_=rq[:]), [rqk], [rqk])
        vec = self.vecs[l]
        for k, rows in ((0, 128), (1, 64)):
            self.dve(lambda e, k=k, rows=rows: e.scalar_tensor_tensor(out=self.cqn[0:rows, k, :], in0=self.cqs[0:rows, k, :], scalar=vec[0:rows, V_GCQ + k:V_GCQ + k + 1],
                                                                      in1=rq[0:rows, :], op0=ALU.mult, op1=ALU.mult), [("cqs", k), rqk, f"vecs{l}"], [("cqn", k)])
        for h in range(4):
            ps, pk = self.ps.next()
            self.mm(ps[0:96, :], self.wuq[:, 0, h * 96:(h + 1) * 96], self.cqn[:, 0, :], True, False, ["wuq", ("cqn", 0)], [pk])
            self.mm(ps[0:96, :], self.wuq[0:64, 1, h * 96:(h + 1) * 96], self.cqn[0:64, 1, :], False, True, ["wuq", ("cqn", 1)], [pk])
            t, tk = self.bf16r.next()
            self.act(t[0:96, :], ps[0:96, :], AF.Copy, [pk], [tk])
            self.store(d["QCT"][h, :, ts], t[0:96, :], reads=[tk], writes=[("QCT", bi)])
            if rope:
                psp, ppk = self.ps.next()
                self.mm(psp[0:96, :], self.wuq[:, 0, 384 + h * 96:384 + (h + 1) * 96], self.cqn[:, 0, :], True, False, ["wuq", ("cqn", 0)], [ppk])
                self.mm(psp[0:96, :], self.wuq[0:64, 1, 384 + h * 96:384 + (h + 1) * 96], self.cqn[0:64, 1, :], False, True, ["wuq", ("cqn", 1)], [ppk])
                self.store(d["QLT"][h, 0:64, ts], t[0:64, :], reads=[tk], writes=[("QLT", bi)])
                rope_store(ps, pk, psp, ppk, 32, self.tabB, "tabB", [d["QLT"][h, 64:96, ts]], ("QLT", bi), p0=64)
        ps, pk = proj_fm(w1, wk1, 64, 32)
        bkt_dsts = [d["BKT"][h, 64:96, ts] for h in range(4)]
        if rope:
            psp, ppk = proj_fm(w1, wk1, 96, 32)
            rope_store(ps, pk, psp, ppk, 32, self.tabB, "tabB", bkt_dsts, ("BKT", bi))
        else:
            t, tk = self.bf16r.next()
            self.act(t[0:32, :], ps[0:32, :], AF.Copy, [pk], [tk])
            for dst in bkt_dsts:
                self.store(dst, t[0:32, :], reads=[tk], writes=[("BKT", bi)])
        for j in range(2):
            ps, pk = proj_fm(w1, wk1, 128 + j * 128, 128)
            evac_store(ps, pk, 128, d["CT"][j * 128:(j + 1) * 128, ts], F32, "dve" if j else "act", ("CT", bi))
        ps, pk = proj_fm(w1, wk1, 384, 128)
        evac_store(ps, pk, 128, d["DTs"][0:128, ts], F32, "act", ("DTs", bi))
        w2, wk2 = self.wload(Wfm[2], NK, 512, (f"Wfm{l}", 2))
        for j in range(4):
            ps, pk = proj_fm(w2, wk2, j * 128, 128)
            evac_store(ps, pk, 128, d["DTs"][(j + 1) * 128:(j + 2) * 128, ts], F32, "dve" if j % 2 else "act", ("DTs", bi))
        if not rope:
            w3, wk3 = self.wload(Wfm[3], NK, 512, (f"Wfm{l}", 3))
        ps, pk = proj_fm(w3, wk3, 0, 128)
        evac_store(ps, pk, 128, d["DTs"][640:768, ts], F32, "dve", ("DTs", bi))
        wt0, wtk0 = self.wload(d[f"Wtok{l}"][0], NK, 512, (f"Wtok{l}", 0))
        wt1, wtk1 = self.wload(d[f"Wtok{l}"][1], NK, 512, (f"Wtok{l}", 1))
        rb = self.rowsb[l]
        for tt in range(TB // 128):
            tsl = slice(tt * 128, (tt + 1) * 128)
            g0 = t0 + tt * 128
            psA, pkA = self.ps.next()
            for k in range(NK):
                self.mm(psA[:, 0:416], hT[:, k, tsl], wt0[:, k, 0:416], k == 0, k == NK - 1, [wtk0, ("hT", k)], [pkA])
            psB, pkB = self.ps.next()
            for k in range(NK):
                self.mm(psB[:, 0:272], hT[:, k, tsl], wt1[:, k, 0:272], k == 0, k == NK - 1, [wtk1, ("hT", k)], [pkB])
            tz, tzk = self.tokf.next()
            self.act(tz[:, 0:272], psB[:, 0:272], AF.Copy, [pkB], [tzk])
            self.store(d["DZ"][g0:g0 + 128, :], tz[:, 0:272], reads=[tzk], writes=[("DZ", bi)])
            ta, tak = self.tokf.next()
            self.dve(lambda e, ta=ta, psA=psA: e.tensor_copy(out=ta[:, 0:416], in_=psA[:, 0:416]), [pkA], [tak])
            tb, tbk = self.bf16r.next()
            self.dve(lambda e, ta=ta, tb=tb: e.tensor_copy(out=tb[:, 0:128], in_=ta[:, 128:256]), [tak], [tbk])
            self.store(d["AV"][g0:g0 + 128, :], tb[:, 0:128], reads=[tbk], writes=[("AV", bi)])
            junk, jk = self.f32r.next()
            sm, smk = self.small.next()
            self.act(junk[:, 0:128], ta[:, 256:384], AF.Square, [tak], [jk, smk], accum_out=sm[:, 0:1])
            self.act(sm[:, 1:2], sm[:, 0:1], AF.Sqrt, [smk], [smk], bias=self.eps_ap, scale=1.0 / 128)
            self.dve(lambda e, sm=sm: e.reciprocal(out=sm[:, 2:3], in_=sm[:, 1:2]), [smk], [smk])
            cn, cnk = self.f32r.next()
            self.dve(lambda e, cn=cn, ta=ta, sm=sm: e.scalar_tensor_tensor(out=cn[:, 0:128], in0=ta[:, 256:384], scalar=sm[:, 2:3], in1=rb[:, R_GCKV:R_GCKV + 128],
                                                                         op0=ALU.mult, op1=ALU.mult), [tak, smk, f"rowsb{l}"], [cnk])
            if not sample:
                sq_, tq_ = divmod(g0 - c.T_S, 256)
                self.store(d["o_ak"][sq_, l, tq_:tq_ + 128, :], ta[:, 0:128], reads=[tak], is_output=True)
                self.store(d["o_av"][sq_, l, tq_:tq_ + 128, :], ta[:, 128:256], reads=[tak], is_output=True)
                self.store(d["o_kr"][sq_, l, tq_:tq_ + 128, :], ta[:, 384:416], reads=[tak], is_output=True)
                self.store(d["o_ckv"][sq_, l, tq_:tq_ + 128, :], cn[:, 0:128], reads=[cnk], is_output=True)
            psT, pkT = self.ps.next()
            self.tr(psT[:, 0:128], cn[:, 0:128], self.cst[:, C_ID:C_ID + 128], [cnk, "cst"], [pkT])
            self.act(self.ckvnT[:, tsl], psT[:, 0:128], AF.Copy, [pkT], [("ckvnT", tt)])
            self.mla_v(self.ckvnT[:, tsl], [("ckvnT", tt)], g0, ("BV", bi))
        self.mla_knope(self.ckvnT[:], [("ckvnT", i) for i in range(4)], TB, t0, ("BKT", bi))

    def mla_v(self, ckvnT_tile, key, g0, wkey):
        d = self.dram
        ps, pk = self.ps.next()
        self.mm(ps[:, 0:256], ckvnT_tile, self.wukv[:, 256:512], True, True, list(key) + ["wukv"], [pk])
        t, tk = self.bf16r.next()
        self.dve(lambda e: e.tensor_copy(out=t[:, 0:256], in_=ps[:, 0:256]), [pk], [tk])
        self.store(d["BV"][g0:g0 + 128, :], t[:, 0:256], reads=[tk], writes=[wkey])

    def mla_knope(self, ckvnT, key, n, t0, wkey):
        d = self.dram
        for h in range(4):
            ps, pk = self.ps.next()
            self.mm(ps[0:64, 0:n], self.wukv[:, h * 64:(h + 1) * 64], ckvnT, True, True, list(key) + ["wukv"], [pk])
            t, tk = self.bf16r.next()
            self.act(t[0:64, 0:n], ps[0:64, 0:n], AF.Copy, [pk], [tk])
            self.store(d["BKT"][h, 0:64, t0:t0 + n], t[0:64, 0:n], reads=[tk], writes=[wkey])

    def phase_ctx(self, l):
        c, d = self.cfg, self.dram
        NT = c.NT
        ident = self.cst[:, C_ID:C_ID + 128]
        for tt in range(2):
            tsl = slice(tt * 128, (tt + 1) * 128)
            xi, xk = self.xin.next()
            self.load(xi[:, 0:128], d["cckv"][l, tsl, :], writes=[xk])
            self.load(xi[:, 128:160], d["ckr"][l, tsl, :], writes=[xk])
            self.load(xi[:, 256:384], d["cak"][l, tsl, :], writes=[xk])
            ps, pk = self.ps.next()
            self.tr(ps[:, 0:128], xi[:, 0:128], ident, [xk, "cst"], [pk])
            self.act(self.ckvnT[:, tsl], ps[:, 0:128], AF.Copy, [pk], [("ckvnT", tt)])
            self.mla_v(self.ckvnT[:, tsl], [("ckvnT", tt)], NT + tt * 128, ("BV", "ctx"))
            ps, pk = self.ps.next()
            self.tr(ps[0:32, 0:128], xi[:, 128:160], ident, [xk, "cst"], [pk])
            t, tk = self.bf16r.next()
            self.act(t[0:32, 0:128], ps[0:32, 0:128], AF.Copy, [pk], [tk])
            for h in range(4):
                self.store(d["BKT"][h, 64:96, NT + tt * 128:NT + (tt + 1) * 128], t[0:32, 0:128], reads=[tk], writes=[("BKT", "ctx")])
            ps, pk = self.ps.next()
            self.tr(ps[:, 0:128], xi[:, 256:384], ident, [xk, "cst"], [pk])
            t, tk = self.bf16r.next()
            self.act(t[:, 0:128], ps[:, 0:128], AF.Copy, [pk], [tk])
            self.store(d["AKCT"][:, tsl], t[:, 0:128], reads=[tk], writes=[("AKCT", 0)])
        self.mla_knope(self.ckvnT[:, 0:256], [("ckvnT", 0), ("ckvnT", 1)], 256, NT, ("BKT", "ctx"))

    def setup_eps(self):
        self.epst = self.P.sbuf("epst", [128, 1], F32)
        self.pool(lambda e: e.memset(self.epst[:], EPS), [], ["epst"])
        self.eps_ap = self.epst[:, 0:1]


class BuilderC(Builder):
    def norm_resid(self, l, gidx, cond):
        ps, pk = self.psx.next()
        for k in range(NK):
            self.act(self.sq[:, k, :], self.mixT[:, k, :], AF.Square, [("mixT", k)], [("sq", k)])
            self.mm(ps[:], self.onesb[:], self.sq[:, k, :], k == 0, k == NK - 1, [("sq", k), "onesb"], [pk])
        self.act(self.rstd[:], ps[:], AF.Sqrt, [pk], ["rstd"], bias=self.eps_ap, scale=1.0 / D)
        self.dve(lambda e: e.reciprocal(out=self.rstd[:], in_=self.rstd[:]), ["rstd"], ["rstd"])
        gm = self.gms[l]
        for k in range(NK):
            t, tk = self.f32r.next()
            self.dve(lambda e, k=k, t=t: e.tensor_tensor(out=t[:], in0=self.mixT[:, k, :], in1=self.rstd[:], op=ALU.mult), [("mixT", k), "rstd"], [tk])
            self.dve(lambda e, k=k, t=t: e.scalar_tensor_tensor(out=self.xT[:, k, :], in0=t[:], scalar=gm[:, gidx, k, cond:cond + 1], in1=self.xT[:, k, :],
                                                              op0=ALU.mult, op1=ALU.add), [tk, f"gm{l}", ("xT", k)], [("xT", k)])

    def norm_h(self, l, gs, gb, cond):
        ps, pk = self.psx.next()
        for k in range(NK):
            self.act(self.sq[:, k, :], self.xT[:, k, :], AF.Square, [("xT", k)], [("sq", k)])
            self.mm(ps[:], self.onesb[:], self.sq[:, k, :], k == 0, k == NK - 1, [("sq", k), "onesb"], [pk])
        self.act(self.rstd[:], ps[:], AF.Sqrt, [pk], ["rstd"], bias=self.eps_ap, scale=1.0 / D)
        self.dve(lambda e: e.reciprocal(out=self.rstd[:], in_=self.rstd[:]), ["rstd"], ["rstd"])
        gm = self.gms[l]
        for k in range(NK):
            t, tk = self.f32r.next()
            self.dve(lambda e, k=k, t=t: e.tensor_tensor(out=t[:], in0=self.xT[:, k, :], in1=self.rstd[:], op=ALU.mult), [("xT", k), "rstd"], [tk])
            self.dve(lambda e, k=k, t=t: e.tensor_scalar(out=self.hT[:, k, :], in0=t[:], scalar1=gm[:, gs, k, cond:cond + 1], scalar2=gm[:, gb, k, cond:cond + 1],
                                                       op0=ALU.mult, op1=ALU.add), [tk, f"gm{l}"], [("hT", k)])

    def phase_c_block(self, l, bi, last):
        c, d = self.cfg, self.dram
        t0, sample, cond = self.blk_info(bi)
        ts = slice(t0, t0 + TB)
        xT, hT, gT, brT, accT, mixT, aT = self.xT, self.hT, self.gT, self.brT, self.accT, self.mixT, self.aT
        self.load(hT[:], d["HT"].rearrange("(k p) t -> p k t", p=128)[:, :, ts], reads=[("HT", bi)], writes=[("hT", k) for k in range(NK)])
        self.load(xT[:], d["XT"].rearrange("(k p) t -> p k t", p=128)[:, :, ts], reads=[("XT", bi)], writes=[("xT", k) for k in range(NK)])
        for m in range(4):
            self.load(brT[:, m], d["BRT"][m].rearrange("(c p) t -> p c t", p=128)[:, :, ts], reads=[("BRT", m)], writes=[("brT", m)])
        for m in range(4):
            wg = [self.wload(d[f"Wgate{l}"][2 * m + i], NK, 512, (f"Wgate{l}", 2 * m + i)) for i in range(2)]
            wb, wbk = self.wload(d[f"Wbr{l}"][m], 2, 1024, (f"Wbr{l}", m))
            for fc in range(NK):
                w, wk = wg[fc // 4]
                co = (fc % 4) * 128
                psg, pgk = self.ps.next()
                for k in range(NK):
                    self.mm(psg[:], w[:, k, co:co + 128], hT[:, k, :], k == 0, k == NK - 1, [wk, ("hT", k)], [pgk])
                psb, pbk = self.ps.next()
                for cc in range(2):
                    self.mm(psb[:], wb[:, cc, fc * 128:(fc + 1) * 128], brT[:, m, cc, :], cc == 0, cc == 1, [wbk, ("brT", m)], [pbk])
                sg, sgk = self.f32r.next()
                self.act(sg[:], psg[:], AF.Sigmoid, [pgk], [sgk])
                if m == 0:
                    self.dve(lambda e, sg=sg, psb=psb, fc=fc: e.tensor_tensor(out=accT[:, fc, :], in0=sg[:], in1=psb[:], op=ALU.mult), [sgk, pbk], [("accT", fc)])
                else:
                    self.dve(lambda e, sg=sg, psb=psb: e.tensor_tensor(out=sg[:], in0=sg[:], in1=psb[:], op=ALU.mult), [sgk, pbk], [sgk])
                    if m < 3:
                        self.dve(lambda e, sg=sg, fc=fc: e.tensor_tensor(out=accT[:, fc, :], in0=accT[:, fc, :], in1=sg[:], op=ALU.add), [sgk, ("accT", fc)], [("accT", fc)])
                    else:
                        self.dve(lambda e, sg=sg, fc=fc: e.tensor_tensor(out=gT[:, fc, :], in0=accT[:, fc, :], in1=sg[:], op=ALU.add), [sgk, ("accT", fc)], [("gT", fc)])
        wo = [self.wload(d[f"Wo{l}"][i], NK, 512, (f"Wo{l}", i)) for i in range(2)]
        for fc in range(NK):
            w, wk = wo[fc // 4]
            co = (fc % 4) * 128
            ps, pk = self.ps.next()
            for k in range(NK):
                self.mm(ps[:], w[:, k, co:co + 128], gT[:, k, :], k == 0, k == NK - 1, [wk, ("gT", k)], [pk])
            self.act(mixT[:, fc, :], ps[:], AF.Copy, [pk], [("mixT", fc)])
        self.norm_resid(l, 2, cond)
        self.norm_h(l, 3, 4, cond)
        for n in range(11):
            w, wk = self.wload(d[f"Wup{l}"][n], NK, 512, (f"Wup{l}", n))
            for j in range(2):
                psg, pgk = self.ps.next()
                for k in range(NK):
                    self.mm(psg[:], w[:, k, j * 128:(j + 1) * 128], hT[:, k, :], k == 0, k == NK - 1, [wk, ("hT", k)], [pgk])
                psu, puk = self.ps.next()
                for k in range(NK):
                    self.mm(psu[:], w[:, k, 256 + j * 128:256 + (j + 1) * 128], hT[:, k, :], k == 0, k == NK - 1, [wk, ("hT", k)], [puk])
                sg, sgk = self.f32r.next()
                self.act(sg[:], psg[:], AF.Silu, [pgk], [sgk])
                kf = 2 * n + j
                self.dve(lambda e, sg=sg, psu=psu, kf=kf: e.tensor_tensor(out=aT[:, kf, :], in0=sg[:], in1=psu[:], op=ALU.mult), [sgk, puk], [("aT", kf)])
        for fc in range(NK):
            w, wk = self.wload(d[f"Wdown{l}"][fc], NKF, 128, (f"Wdown{l}", fc))
            ps, pk = self.ps.next()
            for kf in range(NKF):
                self.mm(ps[:], w[:, kf, :], aT[:, kf, :], kf == 0, kf == NKF - 1, [wk, ("aT", kf)], [pk])
            self.act(mixT[:, fc, :], ps[:], AF.Copy, [pk], [("mixT", fc)])
        self.norm_resid(l, 5, cond)
        xkeys = [("xT", k) for k in range(NK)]
        if not last:
            self.store(d["XT"].rearrange("(k p) t -> p k t", p=128)[:, :, ts], xT[:], reads=xkeys, writes=[("XT", bi)])
        else:
            ident = self.cst[:, C_ID:C_ID + 128]
            for tt in range(TB // 128):
                yt, ytk = self.yt.next()
                for k in range(NK):
                    ps, pk = self.ps.next()
                    self.tr(ps[:, 0:128], xT[:, k, tt * 128:(tt + 1) * 128], ident, [("xT", k), "cst"], [pk])
                    if k % 2 == 0:
                        self.act(yt[:, k * 128:(k + 1) * 128], ps[:, 0:128], AF.Copy, [pk], [ytk])
                    else:
                        self.dve(lambda e, yt=yt, ps=ps, k=k: e.tensor_copy(out=yt[:, k * 128:(k + 1) * 128], in_=ps[:, 0:128]), [pk], [ytk])
                g0 = t0 + tt * 128
                dst = d["ys"][g0:g0 + 128, :] if sample else d["yp"][g0 - c.T_S:g0 - c.T_S + 128, :]
                self.store(dst, yt[:], reads=[ytk], is_output=True)


A_SCALE = 0.125
B_SCALE = 96.0 ** -0.5
C_WINDOWS = (2, 4, 8, 16)


class BuilderM(BuilderC):
    def mixer_rings(self):
        pst = self.ps.tiles + self.psx.tiles
        self.sc = ARing("sc", [pst[i][:] for i in range(4)])
        self.accp = ARing("accp", [pst[i][:] for i in range(4, 8)])

    def mixer_setup(self, l):
        d = self.dram
        aF, aB = self.arenaF, self.arenaB
        rb = self.rowsb[l]
        self.sinkexp = aF[0:64, 11776:12288]
        self.maskP = aB[:, 23552:24064]
        self.maskN = aB[:, 24064:24576]
        self.wpbd = aB[:, 23296:23552].rearrange("p (c d) -> p c d", c=2)
        wpf = aF[:, 11520:11776].rearrange("p (c d) -> p c d", c=2)
        se, sek = self.small.next()
        self.act(se[:, 0:4], rb[:, R_SINK:R_SINK + 4], AF.Exp, [f"rowsb{l}"], [sek])
        for h in range(4):
            self.dve(lambda e, h=h: e.tensor_scalar(out=self.sinkexp[:, h * 128:(h + 1) * 128], in0=self.cst[0:64, C_ONE:C_ONE + 128], scalar1=se[0:64, h:h + 1], scalar2=None, op0=ALU.mult),
                     [sek, "cst"], ["sinkexp"])
            self.dve(lambda e, h=h: e.tensor_copy(out=self.maskP[:, h * 128:(h + 1) * 128], in_=self.cst[:, C_UI:C_UI + 128]), ["cst"], ["maskP"])
            self.dve(lambda e, h=h: e.tensor_copy(out=self.maskN[:, h * 128:(h + 1) * 128], in_=self.cst[:, C_LI:C_LI + 128]), ["cst"], ["maskN"])
        self.pool(lambda e: e.memset(wpf[:], 0.0), [], ["wpf"])
        for g in range(4):
            p0 = (g % 2) * 64
            self.load(wpf[p0:p0 + 64, g // 2, p0:p0 + 64], d["wpool"][l, g], writes=["wpf"])
        self.dve(lambda e: e.tensor_copy(out=self.wpbd[:], in_=wpf[:]), ["wpf"], ["wpbd"])

    def mixer_c(self, l):
        c, d = self.cfg, self.dram
        aF, aB = self.arenaF, self.arenaB
        L = 528
        X = aF[:, 0:2 * L].rearrange("p (c t) -> p c t", c=2)
        Pa = aF[:, 2 * L:3 * L]
        Pb = aF[:, 3 * L:4 * L]
        Y = aF[:, 4 * L:4 * L + 512]
        IC = aF[:, 5 * L:5 * L + 1024].rearrange("p (c t) -> p c t", c=2)
        Yb = aB[:, 18432:18944]
        vec = self.vecs[l]
        segs = []
        for s0 in range(0, c.T_S, 512):
            segs.append((0, c.T_S, s0, 512, "icS", s0))
        for s in range(c.NSEQ):
            segs.append((c.T_S + s * 256, 256, 0, 256, "icP", 0))
        for (base, T, s0, n, ictab, ic0) in segs:
            lo, hi = max(s0 - 8, 0), min(s0 + n + 8, T)
            self.dve(lambda e: e.memset(X[:], 0.0), [], ["X"])
            self.load(X[:, :, lo - (s0 - 8):hi - (s0 - 8)], d["CT"].rearrange("(c p) t -> p c t", p=128)[:, :, base + lo:base + hi], writes=["X"])
            self.load(IC[:, :, 0:n], d[ictab][:, :, ic0:ic0 + n], writes=["IC"])
            for ch in range(2):
                x = X[:, ch, :]
                self.dve(lambda e, x=x: e.tensor_tensor(out=Pa[:, 0:L - 1], in0=x[:, 0:L - 1], in1=x[:, 1:L], op=ALU.add), ["X"], ["Pa"])
                self.dve(lambda e: e.tensor_tensor(out=Pb[:, 0:L - 3], in0=Pa[:, 0:L - 3], in1=Pa[:, 2:L - 1], op=ALU.add), ["Pa"], ["Pb"])
                if ch == 0:
                    srcs = [(Pa, 2, "Pa"), (Pb, 4, "Pb")]
                else:
                    self.dve(lambda e: e.tensor_tensor(out=Pa[:, 0:L - 7], in0=Pb[:, 0:L - 7], in1=Pb[:, 4:L - 3], op=ALU.add), ["Pb"], ["Pa"])
                    self.dve(lambda e: e.tensor_tensor(out=Pb[:, 0:L - 15], in0=Pa[:, 0:L - 15], in1=Pa[:, 8:L - 7], op=ALU.add), ["Pa"], ["Pb"])
                    srcs = [(Pa, 8, "Pa"), (Pb, 16, "Pb")]
                for half, (Pw, w, pk_) in enumerate(srcs):
                    sl = slice(half * 64, half * 64 + 64)
                    o0 = 8 - w // 2
                    self.dve(lambda e, Pw=Pw, sl=sl, o0=o0, ch=ch, n=n: e.tensor_tensor(out=Y[sl, 0:n], in0=Pw[sl, o0:o0 + n], in1=IC[sl, ch, 0:n], op=ALU.mult), [pk_, "IC"], ["Y"])
                    self.dve(lambda e, sl=sl, x=x, n=n: e.tensor_tensor(out=Yb[sl, 0:n], in0=Y[sl, 0:n], in1=x[sl, 8:8 + n], op=ALU.subtract), ["Y", "X"], ["Yb"])
                ps, pk = self.sc.next()
                self.mm(ps[:, 0:n], self.wpbd[:, ch, :], Yb[:, 0:n], True, True, ["wpbd", "Yb"], [pk])
                o, ok = self.bf16r.next()
                self.dve(lambda e, o=o, ps=ps, ch=ch, n=n: e.tensor_scalar(out=o[:, 0:n], in0=ps[:, 0:n], scalar1=vec[:, V_CSC + ch:V_CSC + ch + 1], scalar2=None, op0=ALU.mult), [pk, f"vecs{l}"], [ok])
                self.store(d["BRT"][2, ch * 128:(ch + 1) * 128, base + s0:base + s0 + n], o[:, 0:n], reads=[ok], writes=[("BRT", 2)])
                yield

    def attn_a_qblock(self, keytiles, qsl_dst):
        d = self.dram
        pts = []
        for (kfn, vt, qfn, mask, rkeys) in keytiles:
            ps, pk = self.sc.next()
            for h in range(4):
                self.mm(ps[:, h * 128:(h + 1) * 128], kfn(h // 2), qfn(h), True, True, rkeys, [pk])
            pt, ptk = self.ptr.next()
            self.act(pt[:], ps[:], AF.Exp, [pk], [ptk], scale=A_SCALE)
            if mask is not None:
                self.dve(lambda e, pt=pt, mask=mask: e.tensor_tensor(out=pt[:], in0=pt[:], in1=mask, op=ALU.mult), [ptk, "maskP", "maskN"], [ptk])
            pts.append((pt, ptk, vt, rkeys))
        pso, pok = self.accp.next()
        for h in range(4):
            for j, (pt, ptk, vt, rkeys) in enumerate(pts):
                self.mm(pso[0:64, h * 128:(h + 1) * 128], vt[:, (h // 2) * 64:(h // 2) * 64 + 64], pt[:, h * 128:(h + 1) * 128], j == 0, j == len(pts) - 1, [ptk] + list(rkeys), [pok])
        psd, pdk = self.accp.next()
        for j, (pt, ptk, vt, rkeys) in enumerate(pts):
            self.mm(psd[0:64, :], self.onesb[:, 0:64], pt[:], j == 0, j == len(pts) - 1, [ptk, "onesb"], [pdk])
        den, dk = self.f32r.next()
        self.dve(lambda e, den=den, psd=psd: e.tensor_tensor(out=den[0:64, :], in0=psd[0:64, :], in1=self.sinkexp, op=ALU.add), [pdk, "sinkexp"], [dk])
        self.dve(lambda e, den=den: e.reciprocal(out=den[0:64, :], in_=den[0:64, :]), [dk], [dk])
        o, ok = self.bf16r.next()
        self.dve(lambda e, o=o, den=den, pso=pso: e.tensor_tensor(out=o[0:64, :], in0=pso[0:64, :], in1=den[0:64, :], op=ALU.mult), [pok, dk], [ok])
        self.store(d["BRT"][0].rearrange("(h d) t -> d h t", d=64)[:, :, qsl_dst], o[0:64, :].rearrange("p (h t) -> p h t", h=4), reads=[ok], writes=[("BRT", 0)])

    def mixer_a(self, l):
        c, d = self.cfg, self.dram
        aB = self.arenaB
        SEG = 1024
        qT = aB[0:64, 0:4096].rearrange("p (h t) -> p h t", h=4)
        qrT = aB[0:64, 4096:8192].rearrange("p (h t) -> p h t", h=4)
        kT = aB[0:64, 8192:8192 + 2 * 1280].rearrange("p (h t) -> p h t", h=2)
        vT = aB[:, 10752:10752 + 10 * 128].rearrange("p (j e) -> p j e", j=10)
        kcT = aB[0:64, 12032:12544].rearrange("p (h t) -> p h t", h=2)
        vc = aB[:, 12544:12800].rearrange("p (j e) -> p j e", j=2)
        self.ptr = ARing("ptr", [aB[:, 12800 + i * 512:12800 + (i + 1) * 512] for i in range(6)])
        AQT3 = d["AQT"].rearrange("(h d) t -> d h t", d=64)
        AQR3 = d["AQRT"].rearrange("(h d) t -> d h t", d=64)
        AKT3 = d["AKT"].rearrange("(h d) t -> d h t", d=64)
        AKC3 = d["AKCT"].rearrange("(h d) t -> d h t", d=64)
        self.load(kcT[:], AKC3, reads=[("AKCT", 0)], writes=["kcT"])
        self.store(vc[:], d["cav"][l].rearrange("(j p) e -> p j e", p=128), writes=["vc"])
        for s0 in range(0, c.T_S, SEG):
            n = min(SEG, c.T_S - s0)
            klo, khi = max(s0 - 128, 0), min(s0 + n + 128, c.T_S)
            self.load(qT[:, :, 0:n], AQT3[:, :, s0:s0 + n], reads=[("AQT", "all")], writes=["qT"])
            self.load(qrT[:, :, 0:n], AQR3[:, :, s0:s0 + n], reads=[("AQRT", "all")], writes=["qrT"])
            self.load(kT[:, :, 0:khi - klo], AKT3[:, :, klo:khi], reads=[("AKT", "all")], writes=["kT"])
            self.load(vT[:, 0:(khi - klo) // 128, :], d["AV"][klo:khi, :].rearrange("(j p) e -> p j e", p=128), reads=[("AV", "all")], writes=["vT"])
            for qb in range(n // 128):
                q0 = s0 + qb * 128
                tiles = []
                for dlt, mask in ((-1, self.maskP), (0, None), (1, self.maskN)):
                    k0 = q0 + dlt * 128
                    if k0 < 0 or k0 >= c.T_S:
                        continue
                    ko = k0 - klo
                    tiles.append((lambda kvh, ko=ko: kT[:, kvh, ko:ko + 128], vT[:, ko // 128, :],
                                  lambda h, qb=qb: qrT[:, h, qb * 128:(qb + 1) * 128], mask, ["kT", "vT", "qrT"]))
                for j in range(2):
                    tiles.append((lambda kvh, j=j: kcT[:, kvh, j * 128:(j + 1) * 128], vc[:, j, :],
                                  lambda h, qb=qb: qT[:, h, qb * 128:(qb + 1) * 128], None, ["kcT", "vc", "qT"]))
                self.attn_a_qblock(tiles, slice(q0, q0 + 128))
                yield
        for s in range(c.NSEQ):
            b0 = c.T_S + s * 256
            self.load(qT[:, :, 0:256], AQT3[:, :, b0:b0 + 256], reads=[("AQT", "all")], writes=["qT"])
            self.load(kT[:, :, 0:256], AKT3[:, :, b0:b0 + 256], reads=[("AKT", "all")], writes=["kT"])
            self.load(vT[:, 0:2, :], d["AV"][b0:b0 + 256, :].rearrange("(j p) e -> p j e", p=128), reads=[("AV", "all")], writes=["vT"])
            for qb in range(2):
                tiles = []
                for j in range(2):
                    tiles.append((lambda kvh, j=j: kT[:, kvh, j * 128:(j + 1) * 128], vT[:, j, :],
                                  lambda h, qb=qb: qT[:, h, qb * 128:(qb + 1) * 128], None, ["kT", "vT", "qT"]))
                self.attn_a_qblock(tiles, slice(b0 + qb * 128, b0 + (qb + 1) * 128))
                yield

    def mla_head(self, h, qsegs, ktiles, kT, vT, rk):
        d = self.dram
        for (qfn, n, dsl) in qsegs:
            pso, pok = self.accp.next()
            psd, pdk = self.accp.next()
            nt = len(ktiles)
            pend = None
            for j, (ko, vj, src) in enumerate(ktiles):
                ps, pk = self.sc.next()
                self.mm(ps[:, 0:n], kT[:, ko:ko + 128], qfn(src), True, True, rk, [pk])
                pt, ptk = self.ptr.next()
                self.act(pt[:, 0:n], ps[:, 0:n], AF.Exp, [pk], [ptk], scale=B_SCALE)
                if pend is not None:
                    jj, ppt, pptk, pvj = pend
                    self.mm(pso[0:64, 0:n], vT[:, pvj, :], ppt[:, 0:n], jj == 0, jj == nt - 1, [pptk] + rk, [pok])
                    self.mm(psd[0:64, 0:n], self.onesb[:, 0:64], ppt[:, 0:n], jj == 0, jj == nt - 1, [pptk, "onesb"], [pdk])
                pend = (j, pt, ptk, vj)
                yield
            jj, ppt, pptk, pvj = pend
            self.mm(pso[0:64, 0:n], vT[:, pvj, :], ppt[:, 0:n], jj == 0, jj == nt - 1, [pptk] + rk, [pok])
            self.mm(psd[0:64, 0:n], self.onesb[:, 0:64], ppt[:, 0:n], jj == 0, jj == nt - 1, [pptk, "onesb"], [pdk])
            den, dk = self.f32r.next()
            self.dve(lambda e, den=den, psd=psd, n=n: e.reciprocal(out=den[0:64, 0:n], in_=psd[0:64, 0:n]), [pdk], [dk])
            o, ok = self.bf16r.next()
            self.dve(lambda e, o=o, den=den, pso=pso, n=n: e.tensor_tensor(out=o[0:64, 0:n], in0=pso[0:64, 0:n], in1=den[0:64, 0:n], op=ALU.mult), [pok, dk], [ok])
            self.store(d["BRT"][1, h * 64:(h + 1) * 64, dsl], o[0:64, 0:n], reads=[ok], writes=[("BRT", 1)])

    def mixer_b(self, l):
        c, d = self.cfg, self.dram
        aB = self.arenaB
        T_S, NT = c.T_S, c.NT
        NKT = T_S // 128 + 2
        qL = aB[0:96, 0:T_S]
        qC = aB[0:96, 4096:4096 + T_S]
        kT = aB[0:96, 8192:8192 + T_S + 256]
        vT = aB[:, 12544:12544 + NKT * 64].rearrange("p (j e) -> p j e", e=64)
        base = 12544 + 34 * 64
        self.ptr = ARing("ptr", [aB[:, base + i * 512:base + (i + 1) * 512] for i in range(6)])
        rk = ["qL", "qC", "kT", "vT"]
        for h in range(4):
            self.load(qL, d["QLT"][h], reads=[("QLT", "all")], writes=["qL"])
            self.load(qC, d["QCT"][h, :, 0:T_S], reads=[("QCT", "all")], writes=["qC"])
            self.load(kT[:, 0:T_S], d["BKT"][h, :, 0:T_S], reads=[("BKT", "all")], writes=["kT"])
            self.load(kT[:, T_S:T_S + 256], d["BKT"][h, :, NT:NT + 256], reads=[("BKT", "ctx")], writes=["kT"])
            self.load(vT[:, 0:T_S // 128, :], d["BV"][0:T_S, h * 64:(h + 1) * 64].rearrange("(j p) e -> p j e", p=128), reads=[("BV", "all")], writes=["vT"])
            self.load(vT[:, T_S // 128:NKT, :], d["BV"][NT:NT + 256, h * 64:(h + 1) * 64].rearrange("(j p) e -> p j e", p=128), reads=[("BV", "ctx")], writes=["vT"])
            ktiles = [(j * 128, j, "lat") for j in range(T_S // 128)] + [(T_S + j * 128, T_S // 128 + j, "ctx") for j in range(2)]
            qsegs = []
            for q0 in range(0, T_S, 512):
                qsegs.append((lambda src, q0=q0: (qL if src == "lat" else qC)[:, q0:q0 + 512], 512, slice(q0, q0 + 512)))
            yield from self.mla_head(h, qsegs, ktiles, kT, vT, rk)
            self.load(qC[:, 0:c.T_P], d["QCT"][h, :, T_S:NT], reads=[("QCT", "all")], writes=["qC"])
            self.load(kT[:, 0:c.T_P], d["BKT"][h, :, T_S:NT], reads=[("BKT", "all")], writes=["kT"])
            self.load(vT[:, 0:c.T_P // 128, :], d["BV"][T_S:NT, h * 64:(h + 1) * 64].rearrange("(j p) e -> p j e", p=128), reads=[("BV", "all")], writes=["vT"])
            for s in range(c.NSEQ):
                ktiles = [(s * 256 + j * 128, s * 2 + j, "ctx") for j in range(2)]
                qsegs = [(lambda src, s=s: qC[:, s * 256:(s + 1) * 256], 256, slice(T_S + s * 256, T_S + (s + 1) * 256))]
                yield from self.mla_head(h, qsegs, ktiles, kT, vT, rk)


class BuilderD(BuilderM):
    def mixer_d1(self, l):
        c, d = self.cfg, self.dram
        aF = self.arenaF
        LX = 516
        xin = aF[:, 0:6 * LX].rearrange("p (c t) -> p c t", c=6)
        u = aF[:, 3096:3096 + 3072].rearrange("p (c t) -> p c t", c=6)
        sqf = aF[:, 6168:6168 + 512]
        rs = aF[:, 6680:6680 + 512]
        tm = ARing("tm", [aF[:, 7192 + i * 512:7192 + (i + 1) * 512] for i in range(2)])
        vec = self.vecs[l]
        blk = self.cst[:, C_BLK:C_BLK + 128]
        ident = self.cst[:, C_ID:C_ID + 128]
        DT3 = d["DTs"].rearrange("(c p) t -> p c t", p=128)
        segs = [(0, c.T_S, s0, 512) for s0 in range(0, c.T_S, 512)] + [(c.T_S + s * 256, 256, 0, 256) for s in range(c.NSEQ)]
        for (base, T, s0, n) in segs:
            lo, hi = max(s0 - 2, 0), min(s0 + n + 2, T)
            self.dve(lambda e: e.memset(xin[:], 0.0), [], ["xin"])
            self.load(xin[:, :, lo - (s0 - 2):hi - (s0 - 2)], DT3[:, :, base + lo:base + hi], writes=["xin"])
            for ch in range(6):
                w = lambda tap, ch=ch: vec[:, V_CONV + tap * 6 + ch:V_CONV + tap * 6 + ch + 1]
                self.dve(lambda e, ch=ch, n=n, w=w: e.tensor_scalar(out=u[:, ch, 0:n], in0=xin[:, ch, 0:n], scalar1=w(0), scalar2=None, op0=ALU.mult), ["xin", f"vecs{l}"], [("u", ch)])
                for tap in range(1, 5):
                    self.dve(lambda e, ch=ch, n=n, w=w, tap=tap: e.scalar_tensor_tensor(out=u[:, ch, 0:n], in0=xin[:, ch, tap:tap + n], scalar=w(tap), in1=u[:, ch, 0:n], op0=ALU.mult, op1=ALU.add),
                             ["xin", f"vecs{l}", ("u", ch)], [("u", ch)])
                self.act(u[:, ch, 0:n], u[:, ch, 0:n], AF.Silu, [("u", ch)], [("u", ch)])
                if ch < 4:
                    self.dve(lambda e, ch=ch, n=n: e.tensor_tensor(out=sqf[:, 0:n], in0=u[:, ch, 0:n], in1=u[:, ch, 0:n], op=ALU.mult), [("u", ch)], ["sqf"])
                    ps, pk = self.sc.next()
                    self.mm(ps[:, 0:n], blk, sqf[:, 0:n], True, True, ["sqf", "cst"], [pk])
                    self.act(rs[:, 0:n], ps[:, 0:n], AF.Sqrt, [pk], ["rs"], bias=self.eps_ap, scale=1.0)
                    self.dve(lambda e, n=n: e.reciprocal(out=rs[:, 0:n], in_=rs[:, 0:n]), ["rs"], ["rs"])
                    self.dve(lambda e, ch=ch, n=n: e.scalar_tensor_tensor(out=u[:, ch, 0:n], in0=u[:, ch, 0:n], scalar=(0.125 if ch < 2 else 1.0), in1=rs[:, 0:n], op0=ALU.mult, op1=ALU.mult),
                             [("u", ch), "rs"], [("u", ch)])
                yield
            tsl = slice(base + s0, base + s0 + n)
            self.store(d["QNT"].rearrange("(c p) t -> p c t", p=128)[:, :, tsl], u[:, 0:2, 0:n], reads=[("u", 0), ("u", 1)], writes=[("QNT", 0)])
            self.store(d["KNT"].rearrange("(c p) t -> p c t", p=128)[:, :, tsl], u[:, 2:4, 0:n], reads=[("u", 2), ("u", 3)], writes=[("KNT", 0)])
            for tt in range(n // 128):
                t, tk = tm.next()
                for j, ch in enumerate((2, 3, 4, 5)):
                    ps, pk = self.sc.next()
                    self.tr(ps[:, 0:128], u[:, ch, tt * 128:(tt + 1) * 128], ident, [("u", ch), "cst"], [pk])
                    self.act(t[:, j * 128:(j + 1) * 128], ps[:, 0:128], AF.Copy, [pk], [tk])
                g0 = base + s0 + tt * 128
                self.store(d["KN"][g0:g0 + 128, :], t[:, 0:256], reads=[tk], writes=[("KN", 0)])
                self.store(d["VV"][g0:g0 + 128, :], t[:, 256:512], reads=[tk], writes=[("VV", 0)])
                yield

    def mixer_d2(self, l):
        c, d = self.cfg, self.dram
        aF = self.arenaF
        cst = self.cst
        LI, LS, UI, US, ONE, ident = (cst[:, o:o + 128] for o in (C_LI, C_LS, C_UI, C_US, C_ONE, C_ID))
        rb = self.rowsb[l]
        off = [0]

        def carve(ncols, parts=128):
            a = aF[0:parts, off[0]:off[0] + ncols]
            off[0] += ncols
            return a
        S = [[carve(64, 64) for h in range(4)] for dr in range(2)]
        xflat = self.xT[:].rearrange("p k t -> p (k t)")
        xo = [0]

        def carve_x(ncols, parts=128):
            a = xflat[0:parts, xo[0]:xo[0] + ncols]
            xo[0] += ncols
            return a
        qk = ARing("qk", [carve_x(1024, 64).rearrange("p (h s t) -> p h s t", h=4, s=2) for _ in range(2)])
        knv = ARing("knv", [carve_x(512) for _ in range(2)])
        dz = ARing("dz", [carve_x(16) for _ in range(2)])
        gs = ARing("gs", [carve_x(64) for _ in range(2)])
        ealog = carve_x(8)
        O = ARing("O", [carve_x(256) for _ in range(2)])
        assert xo[0] <= 4096, xo[0]
        RU = []
        for u_ in range(4):
            RU.append(dict(
                gL=ARing(f"gL{u_}", [carve(128)]), ET=ARing(f"ET{u_}", [carve(128)]), tA=ARing(f"tA{u_}", [carve(128) for _ in range(2)]),
                Pm=ARing(f"Pm{u_}", [carve(128) for _ in range(6)]), Yr=ARing(f"Yr{u_}", [carve(128) for _ in range(6)]),
                AqT=ARing(f"AqT{u_}", [carve(128)]), kd=ARing(f"kd{u_}", [carve(64)]), XU=ARing(f"XU{u_}", [carve(128)]),
                wT=ARing(f"wT{u_}", [carve(128, 64)]), vn=ARing(f"vn{u_}", [carve(64)]), qs=ARing(f"qs{u_}", [carve(64)])))
        assert off[0] <= 11520, off[0]
        self.act(ealog[:], rb[:, R_ALOG:R_ALOG + 8], AF.Exp, [f"rowsb{l}"], ["ealog"])

        seqs = [(0, c.T_S, True, 0)] + [(c.T_S + s * 256, 256, False, s) for s in range(c.NSEQ)]
        QN4 = d["QNT"].rearrange("(h dd) t -> dd h t", dd=64)
        KN4 = d["KNT"].rearrange("(h dd) t -> dd h t", dd=64)
        for (base, T, sample, sidx) in seqs:
            N = T // 128
            for dr in range(2):
                for h in range(4):
                    if sample:
                        self.load(S[dr][h], d["sdf" if dr == 0 else "sdb"][l, h], writes=[("S", dr, h)])
                    else:
                        self.dve(lambda e, dr=dr, h=h: e.memset(S[dr][h], 0.0), [], [("S", dr, h)])
            for i in range(N):
                for dr in range(2):
                    ci = i if dr == 0 else N - 1 - i
                    g0 = base + ci * 128
                    q, qkk = qk.next()
                    self.load(q[:, :, 0, :], QN4[:, :, g0:g0 + 128], reads=[("QNT", 0)], writes=[qkk])
                    self.load(q[:, :, 1, :], KN4[:, :, g0:g0 + 128], reads=[("KNT", 0)], writes=[qkk])
                    kv, kvk = knv.next()
                    self.load(kv[:, 0:256], d["KN"][g0:g0 + 128, :], reads=[("KN", 0)], writes=[kvk])
                    self.load(kv[:, 256:512], d["VV"][g0:g0 + 128, :], reads=[("VV", 0)], writes=[kvk])
                    z, zk = dz.next()
                    self.load(z[:], d["DZ"][g0:g0 + 128, 256:272], reads=[("DZ", 0)], writes=[zk])
                    g, gk = gs.next()
                    c0, c1 = dr * 4, dr * 4 + 4
                    self.act(g[:, 8:12], z[:, c0:c1], AF.Sigmoid, [zk], [gk])
                    self.dve(lambda e, g=g: e.tensor_scalar(out=g[:, 16:20], in0=g[:, 8:12], scalar1=-1.0, scalar2=None, op0=ALU.mult), [gk], [gk])
                    self.dve(lambda e, g=g, z=z, c0=c0, c1=c1: e.tensor_tensor(out=g[:, 48:52], in0=z[:, 8 + c0:8 + c1], in1=rb[:, R_DTB + c0:R_DTB + c1], op=ALU.add), [zk, f"rowsb{l}"], [gk])
                    self.act(g[:, 48:52], g[:, 48:52], AF.Exp, [gk], [gk])
                    self.act(g[:, 48:52], g[:, 48:52], AF.Ln, [gk], [gk], bias=1.0, scale=1.0)
                    self.dve(lambda e, g=g, c0=c0, c1=c1: e.scalar_tensor_tensor(out=g[:, 0:4], in0=g[:, 48:52], scalar=-1.0, in1=ealog[:, c0:c1], op0=ALU.mult, op1=ALU.mult), [gk, "ealog"], [gk])
                    ps, pk = self.sc.next()
                    self.mm(ps[:, 0:4], LI if dr == 0 else UI, g[:, 0:4], True, True, [gk, "cst"], [pk])
                    self.mm(ps[:, 4:8], ONE, g[:, 0:4], True, True, [gk, "cst"], [pk])
                    self.act(g[:, 24:28], ps[:, 0:4], AF.Exp, [pk], [gk])
                    self.act(g[:, 40:44], ps[:, 4:8], AF.Exp, [pk], [gk])
                    self.act(g[:, 52:60], ps[:, 0:8], AF.Copy, [pk], [gk])
                    self.dve(lambda e, g=g: e.tensor_tensor(out=g[:, 32:36], in0=g[:, 56:60], in1=g[:, 52:56], op=ALU.subtract), [gk], [gk])
                    self.act(g[:, 32:36], g[:, 32:36], AF.Exp, [gk], [gk])
                    otile, ok_ = O.next()
                    gens = [self.d2_unit(l, dr, h, q, qkk, kv, kvk, g, gk, S[dr][h], ("S", dr, h), otile, ok_, RU[h], (LI, LS, UI, US, ident)) for h in range(4)]
                    while gens:
                        alive = []
                        for gen in gens:
                            try:
                                next(gen)
                                alive.append(gen)
                            except StopIteration:
                                pass
                        gens = alive
                    self.store(d["OF" if dr == 0 else "OB"][g0:g0 + 128, :], otile[:], reads=[ok_], writes=[("OFB", dr)])
            if not sample:
                for dr in range(2):
                    for h in range(4):
                        self.store(d["o_sf" if dr == 0 else "o_sb"][sidx, l, h], S[dr][h], reads=[("S", dr, h)], is_output=True)

    def d2_unit(self, l, dr, h, q, qkk, kv, kvk, g, gk, S, Sk, otile, ok_, R, consts):
        LI, LS, UI, US, ident = consts
        qT, kT = q[:, h, 0, :], q[:, h, 1, :]
        kn, v = kv[:, h * 64:(h + 1) * 64], kv[:, 256 + h * 64:256 + (h + 1) * 64]
        col = lambda o: g[:, o + h:o + h + 1]
        graw, beta, nbeta, eG, edG, etot = col(0), col(8), col(16), col(24), col(32), col(40)
        m_strict = LS if dr == 0 else US
        m_incl = LI if dr == 0 else UI
        gl, glk = R["gL"].next()
        gl = gl.bitcast(BF16)[:, 0:128]
        self.dve(lambda e: e.tensor_scalar(out=gl, in0=(LI if dr == 0 else UI), scalar1=graw, scalar2=None, op0=ALU.mult), [gk, "cst"], [glk])
        ps, pk = self.sc.next()
        self.mm(ps[:, 0:128], self.usb[:] if dr == 0 else self.lsb[:], gl, True, True, [glk, "usb", "lsb"], [pk])
        et, etk = R["ET"].next()
        self.act(et, ps[:, 0:128], AF.Exp, [pk], [etk])
        yield
        psk, pkk = self.sc.next()
        self.mm(psk[:, 0:128], kT, kT, True, True, [qkk], [pkk])
        self.mm(psk[:, 128:256], kT, qT, True, True, [qkk], [pkk])
        ta, tak = R["tA"].next()
        self.dve(lambda e: e.tensor_tensor(out=ta, in0=psk[:, 0:128], in1=et, op=ALU.mult), [pkk, etk], [tak])
        p0t, p0tk = R["Pm"].next()
        self.dve(lambda e: e.scalar_tensor_tensor(out=p0t, in0=ta, scalar=nbeta, in1=m_strict, op0=ALU.mult, op1=ALU.mult), [tak, gk, "cst"], [p0tk])
        aq, aqk = R["AqT"].next()
        aqf = aq
        aq = aq.bitcast(BF16)[:, 0:128]
        tq, tqk = R["tA"].tiles[1], (R["tA"].name, 1)
        self.dve(lambda e: e.tensor_tensor(out=tq, in0=psk[:, 128:256], in1=et, op=ALU.mult), [pkk, etk], [tqk])
        self.dve(lambda e: e.tensor_tensor(out=aq, in0=tq, in1=m_incl, op=ALU.mult), [tqk, "cst"], [aqk])
        yield
        blk = self.cst[:, C_BLK:C_BLK + 128]
        ps, pk = self.sc.next()
        self.tr(ps[:, 0:128], p0t, ident, [p0tk, "cst"], [pk])
        p0f, p0fk = R["Pm"].next()
        self.act(p0f, ps[:, 0:128], AF.Copy, [pk], [p0fk])
        yield
        p0tb, p0tbk = R["Pm"].next()
        self.dve(lambda e: e.tensor_tensor(out=p0tb, in0=p0t, in1=blk, op=ALU.mult), [p0tk, "cst"], [p0tbk])
        p0b, p0bk = R["Pm"].next()
        self.dve(lambda e: e.tensor_tensor(out=p0b, in0=p0f, in1=blk, op=ALU.mult), [p0fk, "cst"], [p0bk])
        yt = R["Yr"].tiles
        nm = R["Yr"].name
        noff, noffk = yt[0], (nm, 0)
        rp, rpk = yt[1], (nm, 1)
        self.dve(lambda e: e.tensor_tensor(out=noff, in0=p0f, in1=p0b, op=ALU.subtract), [p0fk, p0bk], [noffk])
        self.act(rp[:, 0:64], v, AF.Copy, [kvk], [rpk])
        self.dve(lambda e: e.tensor_scalar(out=rp[:, 64:128], in0=kn, scalar1=eG, scalar2=None, op0=ALU.mult), [kvk, gk], [rpk])
        tcur, tck = yt[2], (nm, 2)
        self.dve(lambda e: e.tensor_tensor(out=yt[2], in0=p0tb, in1=ident, op=ALU.add), [p0tbk, "cst"], [(nm, 2)])
        yield
        pt_, ptk_, p_, pk_ = p0tb, p0tbk, p0b, p0bk
        for lev in range(5):
            ps2, pk2 = self.sc.next()
            self.mm(ps2[:, 0:128], pt_, p_, True, True, [pk_, ptk_], [pk2])
            np_, npk = R["Pm"].next()
            self.act(np_, ps2[:, 0:128], AF.Copy, [pk2], [npk])
            if lev < 4:
                ps1, pk1 = self.sc.next()
                self.mm(ps1[:, 0:128], p_, pt_, True, True, [pk_, ptk_], [pk1])
                npt, nptk = R["Pm"].next()
                self.dve(lambda e, npt=npt, ps1=ps1: e.tensor_copy(out=npt, in_=ps1[:, 0:128]), [pk1], [nptk])
                pt_, ptk_ = npt, nptk
            p_, pk_ = np_, npk
            yield
            psa, pka = self.sc.next()
            self.mm(psa[:, 0:128], p_, tcur, True, True, [pk_, tck], [pka])
            ni = 3 if tck[1] == 2 else 2
            tnew, tnk = yt[ni], (nm, ni)
            self.dve(lambda e, tcur=tcur, tnew=tnew, psa=psa: e.tensor_tensor(out=tnew, in0=tcur, in1=psa[:, 0:128], op=ALU.add), [tck, pka], [tnk])
            tcur, tck = tnew, tnk
            yield
        psy, pyk = self.sc.next()
        self.mm(psy[:, 0:128], tcur, rp, True, True, [tck, rpk], [pyk])
        ysb, ysk = yt[4], (nm, 4)
        self.act(ysb, psy[:, 0:128], AF.Copy, [pyk], [ysk])
        psq, pqk = self.sc.next()
        self.mm(psq[:, 0:128], noff, tcur, True, True, [noffk, tck], [pqk])
        mqT, mqk = R["Pm"].next()
        self.dve(lambda e: e.tensor_copy(out=mqT, in_=psq[:, 0:128]), [pqk], [mqk])
        yield
        psx_, pxk = self.sc.next()
        self.mm(psx_[:, 0:128], mqT, ysb, True, True, [mqk, ysk], [pxk])
        xs, xsk = R["tA"].next()
        self.dve(lambda e: e.tensor_tensor(out=xs, in0=ysb, in1=psx_[:, 0:128], op=ALU.add), [ysk, pxk], [xsk])
        yield
        y, yk = xs, xsk
        xu, xuk = R["XU"].next()
        self.dve(lambda e: e.tensor_scalar(out=xu, in0=y, scalar1=beta, scalar2=None, op0=ALU.mult), [yk, gk], [xuk])
        ps, pk = self.sc.next()
        self.tr(ps[0:64, 0:128], xu[:, 64:128], ident, [xuk, "cst"], [pk])
        wt, wtk = R["wT"].next()
        self.act(wt, ps[0:64, 0:128], AF.Copy, [pk], [wtk])
        yield
        kdt, kdk = R["kd"].next()
        kdt = kdt.bitcast(BF16)[:, 0:64]
        self.dve(lambda e: e.tensor_scalar(out=kdt, in0=kn, scalar1=edG, scalar2=None, op0=ALU.mult), [kvk, gk], [kdk])
        ps, pk = self.sc.next()
        self.mm(ps[:, 0:64], wt, S, True, True, [wtk, Sk], [pk])
        self.mm(ps[:, 64:128], qT, S, True, True, [qkk, Sk], [pk])
        vnt, vnk = R["vn"].next()
        vnt = vnt.bitcast(BF16)[:, 0:64]
        self.dve(lambda e: e.tensor_tensor(out=vnt, in0=xu[:, 0:64], in1=ps[:, 0:64], op=ALU.subtract), [xuk, pk], [vnk])
        qst, qsk = R["qs"].next()
        self.dve(lambda e: e.tensor_scalar(out=qst, in0=ps[:, 64:128], scalar1=eG, scalar2=None, op0=ALU.mult), [pk, gk], [qsk])
        yield
        ps2, pk2 = self.sc.next()
        self.mm(ps2[:, 0:64], aq, vnt, True, True, [aqk, vnk], [pk2])
        self.mm(ps2[0:64, 64:128], kdt, vnt, True, True, [kdk, vnk], [pk2])
        self.dve(lambda e: e.tensor_tensor(out=otile[:, h * 64:(h + 1) * 64], in0=qst, in1=ps2[:, 0:64], op=ALU.add), [qsk, pk2], [ok_])
        self.dve(lambda e: e.scalar_tensor_tensor(out=S, in0=S, scalar=g[0:64, 40 + h:41 + h], in1=ps2[0:64, 64:128], op0=ALU.mult, op1=ALU.add), [Sk, gk, pk2], [Sk])

    def mixer_d3(self, l):
        c, d = self.cfg, self.dram
        aF = self.arenaF
        rb = self.rowsb[l]
        ident = self.cst[:, C_ID:C_ID + 128]
        ofr = ARing("of", [aF[:, i * 256:(i + 1) * 256] for i in range(2)])
        obr = ARing("ob", [aF[:, 512 + i * 256:512 + (i + 1) * 256] for i in range(2)])
        zr = ARing("z", [aF[:, 1024 + i * 256:1024 + (i + 1) * 256] for i in range(2)])
        sqr = ARing("sqd", [aF[:, 1536 + i * 256:1536 + (i + 1) * 256] for i in range(2)])
        onr = ARing("on", [aF[:, 2048 + i * 256:2048 + (i + 1) * 256] for i in range(2)])
        for tt in range(c.NT // 128):
            g0 = tt * 128
            of, ofk = ofr.next(); ob, obk = obr.next(); z, zk = zr.next(); sq, sqk = sqr.next(); on, onk = onr.next()
            self.load(of, d["OF"][g0:g0 + 128, :], reads=[("OFB", 0)], writes=[ofk])
            self.load(ob, d["OB"][g0:g0 + 128, :], reads=[("OFB", 1)], writes=[obk])
            self.load(z, d["DZ"][g0:g0 + 128, 0:256], reads=[("DZ", 0)], writes=[zk])
            self.dve(lambda e, of=of, ob=ob: e.tensor_tensor(out=of, in0=of, in1=ob, op=ALU.add), [ofk, obk], [ofk])
            self.dve(lambda e, of=of, sq=sq: e.tensor_tensor(out=sq, in0=of, in1=of, op=ALU.mult), [ofk], [sqk])
            sm, smk = self.small.next()
            self.dve(lambda e, sm=sm, sq=sq: e.tensor_reduce(out=sm[:, 0:4], in_=sq.rearrange("p (h e) -> p h e", h=4), axis=AX.X, op=ALU.add), [sqk], [smk])
            self.act(sm[:, 4:8], sm[:, 0:4], AF.Sqrt, [smk], [smk], bias=self.eps_ap, scale=1.0 / 64)
            self.dve(lambda e, sm=sm: e.reciprocal(out=sm[:, 4:8], in_=sm[:, 4:8]), [smk], [smk])
            self.act(z, z, AF.Silu, [zk], [zk])
            for h in range(4):
                hs = slice(h * 64, (h + 1) * 64)
                self.dve(lambda e, of=of, on=on, sm=sm, hs=hs, h=h: e.scalar_tensor_tensor(out=on[:, hs], in0=of[:, hs], scalar=sm[:, 4 + h:5 + h], in1=rb[:, R_GNORM:R_GNORM + 64], op0=ALU.mult, op1=ALU.mult),
                         [ofk, smk, f"rowsb{l}"], [onk])
            self.dve(lambda e, on=on, z=z: e.tensor_tensor(out=on, in0=on, in1=z, op=ALU.mult), [onk, zk], [onk])
            for cc in range(2):
                ps, pk = self.sc.next()
                self.tr(ps[:, 0:128], on[:, cc * 128:(cc + 1) * 128], ident, [onk, "cst"], [pk])
                o, ok = self.bf16r.next()
                self.act(o[:, 0:128], ps[:, 0:128], AF.Copy, [pk], [ok])
                self.store(d["BRT"][3, cc * 128:(cc + 1) * 128, g0:g0 + 128], o[:, 0:128], reads=[ok], writes=[("BRT", 3)])


def drive(*gens):
    gens = list(gens)
    while gens:
        alive = []
        for g in gens:
            try:
                next(g)
                alive.append(g)
            except StopIteration:
                pass
        gens = alive


def build_program(cfg, debug=()):
    B = BuilderD(cfg, debug=debug)
    B.setup_eps()
    B.phase_init()
    B.P.barrier()
    B.phase_w(0)
    B.mixer_rings()
    for l in range(2):
        B.layer_setup(l)
        B.phase_ctx(l)
        for bi in range(cfg.NB):
            B.phase_a_block(l, bi)
        if l == 0:
            B.phase_w(1)
        B.P.barrier()
        B.mixer_setup(l)
        B.P.barrier()
        drive(B.mixer_c(l), B.mixer_a(l))
        B.P.barrier()
        drive(B.mixer_b(l), B.mixer_d1(l))
        B.P.barrier()
        B.mixer_d2(l)
        B.P.barrier()
        B.mixer_d3(l)
        B.P.barrier()
        for bi in range(cfg.NB):
            B.phase_c_block(l, bi, last=(l == 1))
        B.P.barrier()
    B.P.finish()
    return B


def run_cfg(cfg, inputs, n_cores=8):
    inp = {k: np.ascontiguousarray(np.asarray(v)) for k, v in inputs.items()}
    B = build_program(cfg)
    shared = host_prep_shared(cfg, inp)
    nb = inp["x_sample"].shape[0]
    in_maps = []
    for core in range(n_cores):
        m = dict(shared)
        b = (core * nb) // n_cores
        seqs = list(range(core * cfg.NSEQ, (core + 1) * cfg.NSEQ))
        m.update(host_prep(cfg, inp, b, seqs))
        in_maps.append(m)
    res = run_bass_kernel_spmd(B.nc, in_maps, core_ids=list(range(n_cores)))
    R = res.results
    f32 = np.float32
    y_prompt = np.concatenate([np.asarray(R[c]["yp"], f32).reshape(cfg.NSEQ, 256, D) for c in range(n_cores)], 0)
    per_b = n_cores // nb
    y_sample = np.stack([np.asarray(R[b * per_b]["ys"], f32) for b in range(nb)], 0)
    cat = lambda k, shp: np.concatenate([np.asarray(R[c][k], f32).reshape((cfg.NSEQ,) + shp) for c in range(n_cores)], 0)
    return (y_prompt, y_sample, cat("o_ak", (2, 256, 2, 64)), cat("o_av", (2, 256, 2, 64)), cat("o_ckv", (2, 256, 128)),
            cat("o_kr", (2, 256, 32)), cat("o_sf", (2, 4, 64, 64)), cat("o_sb", (2, 4, 64, 64)))


def kernel(**inputs):
    cfg = Cfg(T_S=4096, NSEQ=4)
    return run_cfg(cfg, inputs)
```
